# Optimizing a Trainium2 kernel written in Bass

```python
import math
import jax, jax.numpy as jnp
from jax import lax
import numpy as np

D_MODEL = 1024
BATCH = 8
SEQ = 2048
DEPTH = 4

CHUNK = 64
N_META = 16
HEAD_DIM = 64
RW_HEADS = 8
RW_WIDTH = RW_HEADS * HEAD_DIM
FX_HEADS = 8
FX_WIDTH = FX_HEADS * HEAD_DIM
DECAY_LORA = 64
AAA_LORA = 64
GATE_LORA = 160
D_FF = 2816
CONV_W = 3
Q_BLOCK = 128
NORM_EPS = 1e-6
GN_EPS = HEAD_DIM * 1e-5

RW_SIZES = (RW_WIDTH, RW_WIDTH, RW_WIDTH, DECAY_LORA, AAA_LORA, GATE_LORA)
RW_COLS = 3 * RW_WIDTH + DECAY_LORA + AAA_LORA + GATE_LORA
FX_SIZES = (FX_WIDTH, FX_WIDTH, FX_WIDTH, FX_WIDTH, FX_HEADS)
FX_COLS = 4 * FX_WIDTH + FX_HEADS
N_IN = RW_COLS + FX_COLS
MIX_WIDTH = RW_WIDTH + FX_WIDTH

kernel_name = "hybrid_rwkv7_fox_convffn_trunk"


def _rmsnorm(x, g, eps=NORM_EPS):
    xf = x.astype(jnp.float32)
    y = xf * lax.rsqrt(jnp.mean(xf * xf, axis=-1, keepdims=True) + eps)
    return (y * g.astype(jnp.float32)).astype(x.dtype)


def _split(h, sizes):
    out = []
    o = 0
    for s in sizes:
        out.append(h[..., o:o + s])
        o += s
    return out


def _heads(t, n):
    return t.reshape(t.shape[:-1] + (n, HEAD_DIM))


def _rwkv7_scan(r, w, k, v, a, b):
    B, L, H, N = r.shape

    def step(S, inp):
        r_t, w_t, k_t, v_t, a_t, b_t = inp
        sa = jnp.einsum('bhvk,bhk->bhv', S, a_t)
        S = S * w_t[:, :, None, :] + sa[..., None] * b_t[:, :, None, :] + v_t[..., None] * k_t[:, :, None, :]
        y = jnp.einsum('bhvk,bhk->bhv', S, r_t)
        return S, y

    xs = tuple(jnp.swapaxes(t, 0, 1) for t in (r, w, k, v, a, b))
    S0 = jnp.zeros((B, H, N, N), jnp.float32)
    _, y = lax.scan(step, S0, xs)
    return jnp.swapaxes(y, 0, 1)


def _rwkv7_mixer(h, mu, w0, w_up, a0, a_up, g_up, k_k, k_a, r_k, gn_w, gn_b):
    f32 = jnp.float32
    B, L, _ = h.shape
    dtype = h.dtype
    prev = jnp.pad(h, ((0, 0), (1, 0), (0, 0)))[:, :-1]
    h = h + mu.astype(dtype) * (prev - h)
    r, k, v, wd, ad, gd = _split(h, RW_SIZES)
    w_log = -jax.nn.softplus(-(w0 + jnp.tanh(wd) @ w_up).astype(f32)) - 0.5
    decay = jnp.exp(-jnp.exp(w_log))
    a = jax.nn.sigmoid((a0 + ad @ a_up).astype(f32))
    g = jax.nn.sigmoid(gd) @ g_up
    r = _heads(r.astype(f32), RW_HEADS)
    k = _heads(k.astype(f32), RW_HEADS)
    v = _heads(v.astype(f32), RW_HEADS)
    a = _heads(a, RW_HEADS)
    decay = _heads(decay, RW_HEADS)
    kk = k * _heads(k_k.astype(f32), RW_HEADS)
    kk = kk / jnp.maximum(jnp.sqrt(jnp.sum(kk * kk, axis=-1, keepdims=True)), 1e-12)
    k = k * (1.0 + (a - 1.0) * _heads(k_a.astype(f32), RW_HEADS))
    y = _rwkv7_scan(r, decay, k, v, -kk, kk * a)
    mean = jnp.mean(y, axis=-1, keepdims=True)
    var = jnp.mean(jnp.square(y - mean), axis=-1, keepdims=True)
    y = ((y - mean) * lax.rsqrt(var + GN_EPS)).reshape(B, L, RW_WIDTH)
    y = y * gn_w.astype(f32) + gn_b.astype(f32)
    bonus = jnp.sum(r * k * r_k.astype(f32), axis=-1, keepdims=True) * v
    y = y + bonus.reshape(B, L, RW_WIDTH)
    return (y * g.astype(f32)).astype(dtype)


def _fox_mixer(h, b_f, q_g, k_g):
    f32 = jnp.float32
    B, L, _ = h.shape
    dtype = h.dtype
    q, k, v, og, fl = _split(h, FX_SIZES)
    q = _rmsnorm(_heads(q, FX_HEADS), q_g).astype(f32) * (HEAD_DIM ** -0.5)
    k = _rmsnorm(_heads(k, FX_HEADS), k_g).astype(f32)
    v = _heads(v, FX_HEADS).astype(f32)
    logf = jax.nn.log_sigmoid((fl + b_f).astype(f32))
    c = jnp.cumsum(logf, axis=1)
    q = q.transpose(0, 2, 1, 3)
    k = k.transpose(0, 2, 1, 3)
    v = v.transpose(0, 2, 1, 3)
    c = c.transpose(0, 2, 1)
    n_blk = -(-L // Q_BLOCK)
    pad = n_blk * Q_BLOCK - L
    qb = jnp.pad(q, ((0, 0), (0, 0), (0, pad), (0, 0))).reshape(B, FX_HEADS, n_blk, Q_BLOCK, HEAD_DIM).transpose(2, 0, 1, 3, 4)
    cb = jnp.pad(c, ((0, 0), (0, 0), (0, pad))).reshape(B, FX_HEADS, n_blk, Q_BLOCK).transpose(2, 0, 1, 3)
    qpos = jnp.arange(n_blk * Q_BLOCK).reshape(n_blk, Q_BLOCK)
    kpos = jnp.arange(L)

    def block(args):
        q_i, c_i, pos_i = args
        s = jnp.einsum('bhqd,bhkd->bhqk', q_i, k) + c_i[..., None] - c[:, :, None, :]
        s = jnp.where(kpos[None, :] <= pos_i[:, None], s, -jnp.inf)
        p = jax.nn.softmax(s, axis=-1)
        return jnp.einsum('bhqk,bhkd->bhqd', p, v)

    o = lax.map(block, (qb, cb, qpos))
    o = o.transpose(1, 0, 3, 2, 4).reshape(B, n_blk * Q_BLOCK, FX_WIDTH)[:, :L]
    o = o * jax.nn.sigmoid(og.astype(f32))
    return o.astype(dtype)


def _conv_ffn(x, w_in, conv_w, conv_b, w_out):
    h = x @ w_in
    L = h.shape[1]
    hp = jnp.pad(h, ((0, 0), (CONV_W - 1, 0), (0, 0)))
    y = conv_b
    for i in range(CONV_W):
        y = y + conv_w[i] * hp[:, i:i + L]
    u, gt = jnp.split(y, 2, axis=-1)
    return (jax.nn.silu(gt) * u) @ w_out


def setup_inputs(seed: int = 0) -> dict:
    key = jax.random.key(seed)
    ks = jax.random.split(key, 26)
    f32 = jnp.float32
    nrm = lambda k, s: jax.random.normal(k, s, f32)
    uni = lambda k, s: jax.random.uniform(k, s, f32)
    D = D_MODEL
    conv_center = jnp.zeros((CONV_W, 1), f32).at[CONV_W - 1].set(1.0)
    return {
        "x": nrm(ks[0], (BATCH, SEQ, D)),
        "meta": nrm(ks[1], (N_META, D)),
        "norm1_g": 1.0 + 0.02 * nrm(ks[2], (DEPTH, D)),
        "w_in": nrm(ks[3], (DEPTH, D, N_IN)) * D ** -0.5,
        "rw_mu": uni(ks[4], (DEPTH, RW_COLS)),
        "rw_w0": -6.0 + 5.0 * uni(ks[5], (DEPTH, RW_WIDTH)),
        "rw_w_up": 0.5 * nrm(ks[6], (DEPTH, DECAY_LORA, RW_WIDTH)) * DECAY_LORA ** -0.5,
        "rw_a0": 0.02 * nrm(ks[7], (DEPTH, RW_WIDTH)),
        "rw_a_up": nrm(ks[8], (DEPTH, AAA_LORA, RW_WIDTH)) * AAA_LORA ** -0.5,
        "rw_g_up": nrm(ks[9], (DEPTH, GATE_LORA, RW_WIDTH)) * GATE_LORA ** -0.5,
        "rw_k_k": 0.85 + 0.02 * nrm(ks[10], (DEPTH, RW_WIDTH)),
        "rw_k_a": 1.0 + 0.02 * nrm(ks[11], (DEPTH, RW_WIDTH)),
        "rw_r_k": 0.1 * nrm(ks[12], (DEPTH, RW_HEADS, HEAD_DIM)),
        "rw_gn_w": 1.0 + 0.02 * nrm(ks[13], (DEPTH, RW_WIDTH)),
        "rw_gn_b": 0.02 * nrm(ks[14], (DEPTH, RW_WIDTH)),
        "fx_b_f": 1.0 + 3.0 * uni(ks[15], (DEPTH, FX_HEADS)),
        "fx_q_g": 1.0 + 0.02 * nrm(ks[16], (DEPTH, HEAD_DIM)),
        "fx_k_g": 1.0 + 0.02 * nrm(ks[17], (DEPTH, HEAD_DIM)),
        "w_o": nrm(ks[18], (DEPTH, MIX_WIDTH, D)) * MIX_WIDTH ** -0.5,
        "norm2_g": 1.0 + 0.02 * nrm(ks[19], (DEPTH, D)),
        "ffn_w_in": nrm(ks[20], (DEPTH, D, 2 * D_FF)) * D ** -0.5,
        "ffn_conv_w": 0.2 * nrm(ks[21], (DEPTH, CONV_W, 2 * D_FF)) + conv_center[None],
        "ffn_conv_b": 0.02 * nrm(ks[22], (DEPTH, 2 * D_FF)),
        "ffn_w_out": nrm(ks[23], (DEPTH, D_FF, D)) * D_FF ** -0.5,
    }


def reference(x, meta, norm1_g, w_in, rw_mu, rw_w0, rw_w_up, rw_a0, rw_a_up, rw_g_up,
              rw_k_k, rw_k_a, rw_r_k, rw_gn_w, rw_gn_b, fx_b_f, fx_q_g, fx_k_g, w_o,
              norm2_g, ffn_w_in, ffn_conv_w, ffn_conv_b, ffn_w_out):
    B = x.shape[0]
    meta_b = jnp.broadcast_to(meta[None].astype(x.dtype), (B, N_META, D_MODEL))
    h = jnp.concatenate([meta_b, x], axis=1)
    for l in range(DEPTH):
        p = _rmsnorm(h, norm1_g[l]) @ w_in[l]
        y_rw = _rwkv7_mixer(p[..., :RW_COLS], rw_mu[l], rw_w0[l], rw_w_up[l], rw_a0[l], rw_a_up[l],
                            rw_g_up[l], rw_k_k[l], rw_k_a[l], rw_r_k[l], rw_gn_w[l], rw_gn_b[l])
        y_fx = _fox_mixer(p[..., RW_COLS:], fx_b_f[l], fx_q_g[l], fx_k_g[l])
        h = h + jnp.concatenate([y_rw, y_fx], axis=-1) @ w_o[l]
        h = h + _conv_ffn(_rmsnorm(h, norm2_g[l]), ffn_w_in[l], ffn_conv_w[l], ffn_conv_b[l], ffn_w_out[l])
    return h[:, N_META:]
```

```python
import numpy as np
import concourse.bass as bass
import concourse.mybir as mybir

F32 = mybir.dt.float32
BF16 = mybir.dt.bfloat16
AF = mybir.ActivationFunctionType
ALU = mybir.AluOpType
AX = mybir.AxisListType

_DT_SIZE = {F32: 4, BF16: 2, mybir.dt.float32r: 4}


def _dsize(dt):
    try:
        return _DT_SIZE[dt]
    except KeyError:
        return mybir.dt.size(dt)


def ap_region(ap):
    name = ap.name
    pat = ap.ap
    off = int(ap.offset)
    es = _dsize(ap.dtype)
    sp = str(ap.space)
    if sp == "DRAM":
        lo = off
        hi = off
        for st, cnt in pat:
            if cnt > 1:
                if st >= 0:
                    hi += st * (cnt - 1)
                else:
                    lo += st * (cnt - 1)
        return (name, 0, 1, lo * es, (hi + 1) * es)
    if sp == "PSUM":
        return (name, 0, 128, 0, 2048)
    pstep, pcnt = pat[0]
    if pstep > 0:
        rowlen = pstep
    else:
        rowlen = int(np.prod(ap.tensor.shape[1:])) * _dsize(ap.tensor.dtype) // es
    p0 = off // rowlen
    f0 = off % rowlen
    lo = f0
    hi = f0
    for st, cnt in pat[1:]:
        if cnt > 1:
            if st >= 0:
                hi += st * (cnt - 1)
            else:
                lo += st * (cnt - 1)
    np_ = pcnt if pstep != 0 else 1
    return (name, p0, p0 + np_, lo * es, (hi + 1) * es)


class Op:
    __slots__ = ("eng", "fn", "idx", "cnt_eng", "cnt", "waits", "inc", "clock", "dma_key")


class Prog:
    ENGS = ("tensor", "vector", "scalar", "gpsimd", "sync")
    CAP = 6000

    def __init__(self, nc):
        self.nc = nc
        self.ops = []
        self.streams = {e: [] for e in self.ENGS}
        self.acc = {}
        self.counts = {}
        self.seen = {e: {} for e in self.ENGS}
        self.op_by = {}

    def add(self, eng, fn, reads=(), writes=(), dma_key=None):
        op = Op()
        op.eng = eng
        op.fn = fn
        op.idx = len(self.ops)
        op.dma_key = dma_key
        op.cnt_eng = ("dma", dma_key) if dma_key is not None else eng
        op.inc = False
        need = {}
        rregs = [ap_region(a) for a in reads]
        wregs = [ap_region(a) for a in writes]
        for regs, is_w in ((rregs, False), (wregs, True)):
            for (name, p0, p1, b0, b1) in regs:
                for rec in self.acc.get(name, ()):
                    if rec[1] <= p0 or rec[0] >= p1 or rec[3] <= b0 or rec[2] >= b1:
                        continue
                    if not (is_w or rec[4]):
                        if not (name.startswith("ps") and rec[5] != op.cnt_eng):
                            continue
                    ce, c = rec[5], rec[6]
                    if isinstance(ce, tuple):
                        c = self.counts[ce]
                    if ce == eng and dma_key is None:
                        if eng == "tensor":
                            continue
                    if need.get(ce, 0) < c:
                        need[ce] = c
        seen = self.seen[eng]
        waits = []
        for ce, c in need.items():
            if seen.get(ce, 0) >= c:
                continue
            waits.append((ce, c))
            src = self.op_by[(ce, c)]
            src.inc = True
            for k, v in src.clock.items():
                if seen.get(k, 0) < v:
                    seen[k] = v
            seen[ce] = c
        op.waits = waits
        cnt = self.counts.get(op.cnt_eng, 0) + 1
        self.counts[op.cnt_eng] = cnt
        op.cnt = cnt
        op.clock = dict(seen)
        self.op_by[(op.cnt_eng, cnt)] = op
        for regs, is_w in ((rregs, False), (wregs, True)):
            for (name, p0, p1, b0, b1) in regs:
                lst = self.acc.setdefault(name, [])
                new = []
                for rec in lst:
                    covered = rec[0] >= p0 and rec[1] <= p1 and rec[2] >= b0 and rec[3] <= b1
                    if covered and (is_w or (not rec[4] and rec[5] == op.cnt_eng)):
                        continue
                    new.append(rec)
                new.append((p0, p1, b0, b1, is_w, op.cnt_eng, cnt, op.idx))
                self.acc[name] = new
        self.ops.append(op)
        self.streams[eng].append(op)
        return op

    def pe(self, fn, reads, writes):
        return self.add("tensor", fn, reads, writes)

    def dve(self, fn, reads, writes):
        return self.add("vector", fn, reads, writes)

    def act(self, fn, reads, writes):
        return self.add("scalar", fn, reads, writes)

    def pool(self, fn, reads, writes):
        return self.add("gpsimd", fn, reads, writes)

    def dma(self, queue, out, in_, key, **kw):
        return self.add(queue, lambda e: e.dma_start(out=out, in_=in_, **kw), [in_], [out], dma_key=key)

    def emit(self, final_waits=()):
        nc = self.nc
        incs = {}
        for op in self.ops:
            if op.dma_key is not None or op.inc:
                incs.setdefault(op.cnt_eng, []).append(op.cnt)
        import contextlib
        with contextlib.ExitStack() as st:
            semtab = {}
            valmap = {}
            nsem = 0
            for ce, lst in incs.items():
                is_dma = isinstance(ce, tuple)
                cap = 10 ** 9 if is_dma else self.CAP
                step = 16 if is_dma else 1
                sems = []
                for i, c in enumerate(lst):
                    si = i // cap
                    if si >= len(sems):
                        sems.append(st.enter_context(nc.semaphore("s%d" % nsem)))
                        nsem += 1
                    valmap[(ce, c)] = (sems[si], (i % cap + 1) * step)
                semtab[ce] = sems
            self.nsem = nsem
            block = st.enter_context(nc.Block())

            def make(engname):
                ops = self.streams[engname]

                def body(e):
                    for op in ops:
                        for w in op.waits:
                            s, v = valmap[w]
                            e.wait_ge(s, v)
                        ins = op.fn(e)
                        if op.dma_key is not None:
                            s, v = valmap[(op.cnt_eng, op.cnt)]
                            ins.then_inc(s, 16)
                        elif op.inc:
                            s, v = valmap[(op.cnt_eng, op.cnt)]
                            ins.then_inc(s, 1)
                    if engname == "sync":
                        for key in final_waits:
                            ce = ("dma", key)
                            c = self.counts[ce]
                            s, v = valmap[(ce, c)]
                            e.wait_ge(s, v)
                return body

            for en in self.ENGS:
                getattr(block, en)(make(en))

import contextlib
from concourse.bass_utils import run_bass_kernel_spmd

F32R = F32
D = 1024
SEQ = 2048
NMETA = 16
LP = 2176
NT = 17
DEPTH = 4
RWC = 1824
FXC = 2056
DFF = 2816
EPS = 1e-6
GN_EPS = 64e-5
NB = 8
RWKV_ENABLED = True
import os
RW_STAGE = int(os.environ.get('RW_STAGE', '9'))


class K:
    def __init__(self, P):
        self.P = P

    def mm(self, out, lhsT, rhs, start=True, stop=True):
        self.P.pe(lambda e: e.matmul(out, lhsT, rhs, start=start, stop=stop), [lhsT, rhs], [out])

    def tr(self, out, in_, ident):
        self.P.pe(lambda e: e.transpose(out, in_, ident), [in_, ident], [out])

    def act(self, out, in_, func, bias=None, scale=None, accum=None):
        kw = {}
        reads = [in_]
        writes = [out]
        if bias is not None:
            kw["bias"] = bias
            if not isinstance(bias, (int, float)):
                reads.append(bias)
        if scale is not None:
            kw["scale"] = scale
            if not isinstance(scale, (int, float)):
                reads.append(scale)
        if accum is not None:
            kw["accum_out"] = accum
            writes.append(accum)
        self.P.act(lambda e: e.activation(out, in_, func, **kw), reads, writes)

    def tt(self, eng, out, in0, in1, op):
        self.P.add(eng, lambda e: e.tensor_tensor(out, in0, in1, op), [in0, in1], [out])

    def ts(self, eng, out, in0, s1, s2, op0, op1=None):
        reads = [in0]
        for s in (s1, s2):
            if s is not None and not isinstance(s, (int, float)):
                reads.append(s)
        if op1 is None:
            self.P.add(eng, lambda e: e.tensor_scalar(out, in0, s1, None, op0), reads, [out])
        else:
            self.P.add(eng, lambda e: e.tensor_scalar(out, in0, s1, s2, op0, op1), reads, [out])

    def stt(self, out, in0, scalar, in1, op0, op1):
        reads = [in0, in1]
        if not isinstance(scalar, (int, float)):
            reads.append(scalar)
        self.P.dve(lambda e: e.scalar_tensor_tensor(out, in0, scalar, in1, op0, op1), reads, [out])

    def copy(self, eng, out, in_):
        if eng == "scalar":
            self.P.act(lambda e: e.copy(out, in_), [in_], [out])
        else:
            self.P.add(eng, lambda e: e.tensor_copy(out, in_), [in_], [out])

    def memset(self, eng, ap, val):
        self.P.add(eng, lambda e: e.memset(ap, val), [], [ap])

    def recip(self, out, in_):
        self.P.dve(lambda e: e.reciprocal(out, in_), [in_], [out])

    def reduce(self, out, in_, op=ALU.add):
        self.P.dve(lambda e: e.tensor_reduce(out, in_, AX.X, op), [in_], [out])

    def dma(self, q, out, in_, key, **kw):
        self.P.dma(q, out, in_, key, **kw)


def mixer(env, l):
    k = env["k"]; P = env["P"]; Carver = env["Carver"]; pss = env["pss"]; XT = env["XT"]
    cm = env["cm"]; ident_bf = env["ident_bf"]; negm_bf = env["negm_bf"]; small = env["small"]
    w_in = env["w_in"]; fx_bf = env["fx_bf"]; fx_qg = env["fx_qg"]; fx_kg = env["fx_kg"]
    qa_s = env["qa_s"]; ka_s = env["ka_s"]; sgo_s = env["sgo_s"]; norm1_g = env["norm1_g"]
    env["norm_T"](norm1_g[l:l + 1, :])
    c = Carver()
    c.off = 2600
    vaug = c.take(NT * 8 * 65 // 2 + 4, BF16)[:, 0:NT * 8 * 65].rearrange("p (t h d) -> p t h d", t=NT, h=8, d=65)
    negc = c.take(NT * 8, F32, (NT, 8))
    nlf = c.take(NT * 8, F32, (NT, 8))
    gq = c.take(64)
    gk = c.take(64)
    bfb = c.take(8)
    tmpf = [c.take(512) for _ in range(2)]
    sgst = [c.take(256, BF16) for _ in range(2)]
    persist_off = c.off
    wfx = [c.take(2048, BF16, (8, 512)) for _ in range(2)]
    wfl = c.take(32, BF16, (8, 8))
    qbf = [c.take(256, BF16, (8, 64)) for _ in range(2)]
    qst = [c.take(512, BF16, (8, 128)) for _ in range(2)]
    ss8 = small[:, 96:104]
    rs8 = small[:, 104:112]
    t8 = small[:, 112:120]
    chl = small[:, 128:136].bitcast(BF16)
    cst = c.take(64, BF16)
    k.dma("sync", gq, fx_qg[l:l + 1, :].partition_broadcast(128), "gq")
    k.dma("sync", gk, fx_kg[l:l + 1, :].partition_broadcast(128), "gk")
    k.dma("sync", bfb, fx_bf[l:l + 1, :].partition_broadcast(128), "bfb")
    k.ts("vector", gq, gq, 0.125, None, ALU.mult)
    k.memset("vector", vaug[:, :, :, 64:65], 1.0)
    wv = w_in[l].rearrange("(kc p) c -> p kc c", p=128)
    for ci in range(4):
        wb = wfx[ci % 2]
        k.dma("gpsimd", wb, wv[:, :, RWC + ci * 512:RWC + (ci + 1) * 512], "wfx%d" % (ci % 2))
        for i in range(NT):
            ps = pss[2 + (i % 2)]
            for kc in range(8):
                k.mm(ps[:], XT[:, kc, i * 128:(i + 1) * 128], wb[:, kc, :], start=kc == 0, stop=kc == 7)
            ps3 = ps[:].rearrange("p (h d) -> p h d", d=64)
            if ci < 2:
                tf = tmpf[i % 2]
                tf3 = tf.rearrange("p (h d) -> p h d", d=64)
                k.act(tf, ps[:], AF.Square)
                k.reduce(ss8, tf3)
                k.ts("vector", ss8, ss8, 1.0 / 64, EPS, ALU.mult, ALU.add)
                k.act(ss8, ss8, AF.Sqrt)
                k.recip(rs8, ss8)
                k.tt("vector", tf3, ps3, rs8.unsqueeze(2).to_broadcast([128, 8, 64]), ALU.mult)
                g = gq if ci == 0 else gk
                qb = qbf[i % 2]
                k.tt("gpsimd", qb, tf3, g.unsqueeze(1).to_broadcast([128, 8, 64]), ALU.mult)
                pt = pss[4 + (i % 2)][:].bitcast(BF16).rearrange("p (a b) -> p a b", b=128)[0:64, 0:8, :]
                for hh in range(8):
                    k.tr(pt[:, hh, :], qb[:, hh, :], ident_bf[:])
                qs = qst[i % 2]
                k.copy("scalar", qs[0:64, :, :], pt)
                dst = (qa_s if ci == 0 else ka_s)[:, 0:64, i * 128:(i + 1) * 128].rearrange("h r t -> r h t")
                k.dma("sync", dst, qs[0:64, :, :], "qst%d" % (i % 2))
            elif ci == 2:
                k.copy("scalar", vaug[:, i, :, 0:64], ps3)
            else:
                sg = sgst[i % 2]
                k.act(sg, ps[:], AF.Sigmoid)
                k.dma("sync", sgo_s[i * 128:(i + 1) * 128, :], sg, "sgst%d" % (i % 2))
    k.dma("gpsimd", wfl, wv[:, :, RWC + 2048:RWC + 2056], "wfl")
    for i in range(NT):
        ps = pss[2 + (i % 2)]
        for kc in range(8):
            k.mm(ps[:, 0:8], XT[:, kc, i * 128:(i + 1) * 128], wfl[:, kc, :], start=kc == 0, stop=kc == 7)
        k.tt("vector", t8, ps[:, 0:8], bfb, ALU.add)
        k.act(t8, t8, AF.Exp, scale=-1.0)
        k.act(nlf[:, i, :], t8, AF.Ln, bias=1.0)
    for i in range(NT):
        pc = pss[4 + (i % 2)]
        for ip in range(i):
            k.mm(pc[:, 0:8], cm[:, 5, :], nlf[:, ip, :], start=ip == 0, stop=False)
        k.mm(pc[:, 0:8], cm[:, 6, :], nlf[:, i, :], start=i == 0, stop=True)
        k.copy("vector", negc[:, i, :], pc[:, 0:8])
        k.ts("vector", chl[:, 0:8], pc[:, 0:8], -1.0, None, ALU.mult)
        k.stt(chl[:, 8:16], pc[:, 0:8], -1.0, chl[:, 0:8], ALU.mult, ALU.subtract)
        pt = pss[2 + (i % 2)][:].bitcast(BF16)[0:16, 0:128]
        k.tr(pt, chl, ident_bf[:])
        k.copy("vector", cst[0:16, :], pt)
        k.dma("sync", qa_s[:, 64, i * 128:(i + 1) * 128], cst[0:8, :], "cst")
        k.dma("sync", qa_s[:, 65, i * 128:(i + 1) * 128], cst[8:16, :], "cst")
    if RWKV_ENABLED:
        env["rwkv_inproj"](env, l, persist_off)
    c2 = Carver()
    c2.off = persist_off
    qT = [c2.take(LP // 2, BF16) for _ in range(2)]
    kT = [c2.take(LP // 2, BF16) for _ in range(2)]
    Eb = [c2.take(256, BF16) for _ in range(3)]
    zb = c2.take(256, BF16)
    yfx = c2.take(NT * 512 // 2, BF16, (NT, 512))
    rd = small[:, 140:141]
    k.memset("vector", zb, 0.0)
    ecnt = 0
    for hh in range(8):
        q_ = qT[hh % 2]
        k_ = kT[hh % 2]
        k.dma("sync", q_[0:66, :], qa_s[hh], "qT%d" % (hh % 2))
        k.dma("sync", k_[0:66, :], ka_s[hh], "kT%d" % (hh % 2))
        for qt in range(5):
            q0 = qt * 512
            Nq = min(512, LP - q0)
            nqb = Nq // 128
            Ob = pss[6 + (qt % 2)]
            Oacc = Ob[:, 0:260].rearrange("p (a d) -> p a d", d=65)
            k.mm(Ob[:, 0:260], zb[:, 0:128], zb[:, 0:260], start=True, stop=True)
            kb_max = (q0 + Nq) // 128 - 1
            for kb in range(kb_max + 1):
                j = kb - 4 * qt
                cs = max(0, j) * 128
                pS = pss[ecnt % 2]
                E = Eb[ecnt % 3]
                ecnt += 1
                k.mm(pS[:, cs:Nq], k_[0:66, kb * 128:(kb + 1) * 128], q_[0:66, q0 + cs:q0 + Nq], start=True, stop=j < 0)
                if j >= 0:
                    k.mm(pS[:, cs:cs + 128], ident_bf[:], negm_bf[:], start=False, stop=True)
                k.act(E[:, cs:Nq], pS[:, cs:Nq], AF.Exp, bias=negc[:, kb, hh:hh + 1])
                for qbl in range(max(0, j), nqb):
                    P.pe(lambda e, o=Oacc[:, qbl, :], a=E[:, qbl * 128:(qbl + 1) * 128], b=vaug[:, kb, hh, :]:
                         e.matmul(o, a, b, start=False, stop=True, skip_group_check=True),
                         [E[:, qbl * 128:(qbl + 1) * 128], vaug[:, kb, hh, :]], [Oacc[:, qbl, :]])
            for qbl in range(nqb):
                i = 4 * qt + qbl
                k.recip(rd, Oacc[:, qbl, 64:65])
                k.ts("vector", yfx[:, i, hh * 64:(hh + 1) * 64], Oacc[:, qbl, 0:64], rd, None, ALU.mult)
    for i in range(NT):
        sg = sgst[i % 2]
        k.dma("sync", sg, sgo_s[i * 128:(i + 1) * 128, :], "sgld%d" % (i % 2))
        yg = tmpf[i % 2].bitcast(BF16)[:, 0:512]
        k.tt("vector", yg, yfx[:, i, :], sg, ALU.mult)
        pt = pss[2 + (i % 2)][:].bitcast(BF16).rearrange("p (a b) -> p a b", b=128)[:, 0:4, :]
        for cc in range(4):
            k.tr(pt[:, cc, :], yg[:, cc * 128:(cc + 1) * 128], ident_bf[:])
        k.copy("scalar", XT[:, 4:8, i * 128:(i + 1) * 128], pt)


def rwkv_inproj(env, l, off):
    k = env["k"]; P = env["P"]; Carver = env["Carver"]; pss = env["pss"]; XT = env["XT"]
    cm = env["cm"]; ident_bf = env["ident_bf"]; ident_f = env["ident_f"]; ident_r = env["ident_r"]
    w_in = env["w_in"]; rw_s = env["rw_s"]; g_s = env["g_s"]
    mu_p = env["mu_p"]; rwvec_p = env["rwvec_p"]; wup_aug = env["wup_aug"]; a_up = env["a_up"]
    g_up = env["g_up"]; gn_w = env["gn_w"]; gn_b = env["gn_b"]
    wv = w_in[l].rearrange("(kc p) c -> p kc c", p=128)
    c = Carver()
    c.off = off
    mu = c.take(16)[:, 0:15]
    k.dma("sync", mu, mu_p[l], "mu")
    wrw = [c.take(512, BF16, (8, 128)) for _ in range(2)]
    Psb = [c.take(LP + 2) for _ in range(2)]
    dtmp = c.take(LP)
    sh = [c.take(LP) for _ in range(2)]
    for b in range(2):
        k.memset("vector", Psb[b][:, 0:1], 0.0)
    for m in range(15):
        nco = 128 if m < 14 else 32
        wb = wrw[m % 2]
        k.dma("gpsimd", wb[:, :, 0:nco], wv[:, :, m * 128:m * 128 + nco], "wrw%d" % (m % 2))
        pb = Psb[m % 2]
        for n in range(5):
            c0 = n * 512
            N = min(512, LP - c0)
            ps = pss[2 + (n % 2)]
            for kc in range(8):
                k.mm(ps[0:nco, 0:N], wb[:, kc, 0:nco], XT[:, kc, c0:c0 + N], start=kc == 0, stop=kc == 7)
            k.copy("scalar", pb[0:nco, 1 + c0:1 + c0 + N], ps[0:nco, 0:N])
        k.tt("gpsimd", dtmp[0:nco, :], pb[0:nco, 0:LP], pb[0:nco, 1:LP + 1], ALU.subtract)
        so = sh[m % 2]
        k.stt(so[0:nco, :], dtmp[0:nco, :], mu[0:nco, m:m + 1], pb[0:nco, 1:LP + 1], ALU.mult, ALU.add)
        k.dma("sync", rw_s[m, 0:nco, :], so[0:nco, :], "sh%d" % (m % 2))


def rwkv(env, l):
    k = env["k"]; P = env["P"]; Carver = env["Carver"]; pss = env["pss"]; XT = env["XT"]
    cm = env["cm"]; ident_bf = env["ident_bf"]; ident_f = env["ident_f"]; ident_r = env["ident_r"]
    rw_s = env["rw_s"]; g_s = env["g_s"]
    rwvec_p = env["rwvec_p"]; wup_aug = env["wup_aug"]; a_up = env["a_up"]
    g_up = env["g_up"]; gn_w = env["gn_w"]; gn_b = env["gn_b"]
    c = Carver()
    TW = c.take(LP, F32R)
    AD = c.take(LP, F32R)
    SG1 = c.take(LP // 2, BF16)
    SG2 = c.take(LP // 2, BF16)
    wup = c.take(512, F32R)
    aup = c.take(512, F32R)
    gup1 = c.take(256, BF16)
    gup2 = c.take(256, BF16)
    vec = c.take(16, F32, (4, 4))
    omka = c.take(4)
    hsel = c.take(2, F32R)
    hself = c.take(2)
    NEGE = -float(np.exp(-0.5))
    tric = c.take(256, F32, (2, 128))
    k.ts("vector", tric[:, 0, :], cm[:, 2, :], NEGE, None, ALU.mult)
    k.ts("vector", tric[:, 1, :], cm[:, 1, :], NEGE, None, ALU.mult)
    lt = c.take(LP)
    if RW_STAGE < 2:
        return
    k.dma("sync", vec, rwvec_p[l], "vec")
    k.ts("vector", omka, vec[:, :, 1], -1.0, 1.0, ALU.mult, ALU.add)
    k.copy("vector", hself[:, 0:1], cm[:, 7, 0:1])
    k.copy("vector", hself[:, 1:2], cm[:, 7, 127:128])
    k.copy("vector", hsel, hself)
    k.dma("sync", lt[0:128, :], rw_s[12], "lt")
    k.act(TW[0:64, :], lt[0:64, :], AF.Tanh)
    k.ts("vector", TW[64:128, :], lt[64:128, :], 0.0, 1.0, ALU.mult, ALU.add)
    k.copy("vector", AD, lt)
    lt2 = c.take(LP)
    k.dma("sync", lt2[0:128, :], rw_s[13], "lt2")
    k.act(SG1, lt2, AF.Sigmoid)
    lt3 = c.take(LP)
    k.dma("sync", lt3[0:32, :], rw_s[14, 0:32, :], "lt3")
    k.act(SG2[0:32, :], lt3[0:32, :], AF.Sigmoid)
    wtmp = c.take(512)
    k.dma("sync", wtmp[0:65, :], wup_aug[l], "wtmp")
    k.memset("vector", wup[64:128, :], 0.0)
    k.copy("vector", wup[0:65, :], wtmp[0:65, :])
    wtmp2 = c.take(512)
    k.dma("sync", wtmp2[64:128, :], a_up[l], "wtmp2")
    k.memset("vector", aup[0:64, :], 0.0)
    k.copy("vector", aup[64:128, :], wtmp2[64:128, :])
    k.dma("gpsimd", gup1, g_up[l, 0:128, :], "gup1")
    k.dma("gpsimd", gup2[0:32, :], g_up[l, 128:160, :], "gup2")
    gst = [wtmp, wtmp2]
    for i in range(NT):
        ps = pss[2 + (i % 2)]
        k.mm(ps[:], SG1[:, i * 128:(i + 1) * 128], gup1, start=True, stop=False)
        k.mm(ps[:], SG2[0:32, i * 128:(i + 1) * 128], gup2[0:32, :], start=False, stop=True)
        k.copy("scalar", gst[i % 2], ps[:])
        k.dma("sync", g_s[i * 128:(i + 1) * 128, :], gst[i % 2], "gst%d" % (i % 2))
    per_off = c.off - 3 * LP - 1024
    if RW_STAGE < 3:
        return
    RD = BF16

    def tk(n):
        return c.take(n // 2, RD)

    def run_interleaved(gens):
        active = list(gens)
        while active:
            for g in list(active):
                try:
                    next(g)
                except StopIteration:
                    active.remove(g)

    for hp in range(4):
        c = Carver()
        c.off = per_off
        y_all = c.take(LP)
        vt_all = c.take(LP)
        coef_all = c.take(NT * 2, F32, (NT, 2))
        post_off = c.off
        RKV = c.take(384, F32, (3, 128))
        AS = c.take(128)
        SIGT = c.take(128)
        E13 = c.take(256)
        E2 = c.take(128)
        kk = c.take(128); kk2 = c.take(128); rin = c.take(128); kkn = c.take(128)
        t1 = c.take(128); kp = c.take(128); bb = c.take(128)
        rk = c.take(128)
        bT = tk(128); kT = tk(128)
        bTh = [tk(128) for _ in range(2)]
        kTh = [tk(128) for _ in range(2)]
        A0T = [tk(128) for _ in range(2)]
        Zb = [[tk(384), tk(384)] for _ in range(2)]
        sets = []
        for _ in range(2):
            sets.append(dict(AR=tk(256), v_tok=tk(128), btok=[tk(128), tk(128)], ktok=[tk(128), tk(128)],
                             GM=c.take(4), AB=[tk(256), tk(256)], AK=[tk(256), tk(256)], T=[tk(128), tk(128)]))
        hd_b = []
        for hd in range(2):
            d = dict(W=tk(64), U=tk(64), S16=tk(64), S32=c.take(64), S32g=c.take(64))
            hd_b.append(d)
            for nm in ("W", "U", "S16", "S32"):
                k.memset("vector", d[nm], 0.0)
        kk_s = vec[:, hp, 0:1]; ka_s_ = vec[:, hp, 1:2]; a0_s = vec[:, hp, 2:3]; rk_s = vec[:, hp, 3:4]
        pA = pss[0][:, 0:256]; pC = pss[0][:, 256:512]
        pT = pss[1]
        pM = pss[2]; pM2 = pss[3]
        pIs = [pss[4], pss[5]]
        pSs = [pss[6], pss[7]]

        def prep(i):
            S_ = sets[i % 2]
            AR = S_["AR"]; v_tok = S_["v_tok"]; btok = S_["btok"]; ktok = S_["ktok"]; GM = S_["GM"]
            t0 = i * 128
            for q in range(3):
                k.dma("sync", RKV[:, q, :], rw_s[4 * q + hp, :, t0:t0 + 128], "rkv%d" % q)
            r_ = RKV[:, 0, :]; k_ = RKV[:, 1, :]; v_ = RKV[:, 2, :]
            k.mm(pA[:, 128:256], aup[:, hp * 128:(hp + 1) * 128], AD[:, t0:t0 + 128])
            k.mm(pA[:, 0:128], TW[:, t0:t0 + 128], wup[:, hp * 128:(hp + 1) * 128])
            k.act(AS, pA[:, 128:256], AF.Sigmoid, bias=a0_s)
            k.act(SIGT, pA[:, 0:128], AF.Sigmoid)
            yield
            k.mm(pC[:, 0:128], SIGT, tric[:, 0, :])
            k.mm(pC[:, 128:256], SIGT, tric[:, 1, :])
            k.act(E13, pC[:, 0:256], AF.Exp)
            k.act(E2, pC[:, 0:128], AF.Exp, scale=-1.0)
            yield
            E1 = E13[:, 0:128]; E3 = E13[:, 128:256]
            k.ts("vector", kk, k_, kk_s, None, ALU.mult)
            k.tt("gpsimd", kk2, kk, kk, ALU.mult)
            k.mm(pT[:, 384:512], cm[:, 7, :], kk2)
            k.act(rin, pT[:, 384:512], AF.Sqrt)
            yield
            k.ts("vector", rin, rin, 1e-12, None, ALU.max)
            k.recip(rin, rin)
            k.tt("vector", kkn, kk, rin, ALU.mult)
            k.ts("vector", t1, AS, ka_s_, omka[:, hp:hp + 1], ALU.mult, ALU.add)
            yield
            k.tt("gpsimd", kp, k_, t1, ALU.mult)
            k.tt("gpsimd", bb, kkn, AS, ALU.mult)
            k.tt("vector", AR[:, 128:256], r_, E1, ALU.mult)
            k.tt("gpsimd", t1, kkn, E3, ALU.mult)
            yield
            k.ts("vector", AR[:, 0:128], t1, -1.0, None, ALU.mult)
            k.tt("vector", bT, bb, E2, ALU.mult)
            k.tt("vector", kT, kp, E2, ALU.mult)
            k.tt("gpsimd", rk, r_, kp, ALU.mult)
            yield
            k.ts("vector", rk, rk, rk_s, None, ALU.mult)
            for hd in range(2):
                for cc in range(2):
                    k.ts("vector", GM[:, hd * 2 + cc:hd * 2 + cc + 1], hself[:, hd:hd + 1],
                         E13[:, 63 + 64 * cc:64 + 64 * cc], None, ALU.mult)
            yield
            k.mm(pT[:, 0:128], v_, ident_f[:])
            k.mm(pT[:, 128:256], bT, ident_bf[:])
            k.mm(pT[:, 256:384], kT, ident_bf[:])
            k.mm(pT[:, 384:386], rk, hself)
            k.copy("scalar", v_tok, pT[:, 0:128])
            k.copy("scalar", vt_all[:, t0:t0 + 128], pT[:, 0:128])
            yield
            k.copy("scalar", coef_all[:, i, :], pT[:, 384:386])
            for cc in range(2):
                k.act(btok[cc], pT[:, 128:256], AF.Copy, scale=hself[:, cc:cc + 1])
                k.act(ktok[cc], pT[:, 256:384], AF.Copy, scale=hself[:, cc:cc + 1])
            yield
            for hd in range(2):
                k.ts("vector", bTh[hd], bT, hself[:, hd:hd + 1], None, ALU.mult)
                k.ts("vector", kTh[hd], kT, hself[:, hd:hd + 1], None, ALU.mult)
                k.mm(pM[:, 0:256], bTh[hd], AR)
                k.mm(pM[:, 256:384], AR[:, 0:128], bTh[hd])
                k.mm(pM2[:, 0:256], kTh[hd], AR)
                yield
                k.tt("vector", S_["AB"][hd], pM[:, 0:256], cm[:, 1:3, :].rearrange("p a b -> p (a b)"), ALU.mult)
                k.tt("vector", A0T[hd], pM[:, 256:384], cm[:, 3, :], ALU.mult)
                k.tt("vector", S_["AK"][hd], pM2[:, 0:256], cm[:, 1:3, :].rearrange("p a b -> p (a b)"), ALU.mult)
                yield
            Zc = []
            for hd in range(2):
                Y = S_["AB"][hd][:, 0:128]; YT = A0T[hd]
                Z = Zb[hd][0]
                pI = pIs[hd]
                k.mm(pI[:, 128:256], YT, Y)
                k.mm(pI[:, 256:384], Y, YT)
                k.tt("gpsimd", Z[:, 0:128], Y, ident_bf[:], ALU.add)
                k.copy("scalar", Z[:, 128:384], pI[:, 128:384])
                Zc.append(Z)
            yield
            for lev in range(1, 5):
                for hd in range(2):
                    Z = Zc[hd]
                    Zn = Zb[hd][lev % 2]
                    pI = pIs[hd]
                    k.mm(pI[:, 0:256], Z[:, 256:384], Z[:, 0:256])
                    k.mm(pI[:, 256:384], Z[:, 128:256], Z[:, 256:384])
                    k.tt("vector", Zn[:, 0:128], pI[:, 0:128], Z[:, 0:128], ALU.add)
                    k.copy("scalar", Zn[:, 128:384], pI[:, 128:384])
                    Zc[hd] = Zn
                    yield
            for hd in range(2):
                Z = Zc[hd]
                pI = pIs[hd]
                k.mm(pI[:, 0:128], Z[:, 256:384], Z[:, 0:128])
                k.tt("vector", S_["T"][hd], pI[:, 0:128], Z[:, 0:128], ALU.add)
            yield

        def seq(i):
            S_ = sets[i % 2]
            AR = S_["AR"]; v_tok = S_["v_tok"]; btok = S_["btok"]; ktok = S_["ktok"]; GM = S_["GM"]
            t0 = i * 128
            for cc in range(2):
                rs = slice(cc * 64, cc * 64 + 64)
                for hd in range(2):
                    d = hd_b[hd]
                    vh = v_tok[:, hd * 64:(hd + 1) * 64]
                    gm = GM[:, hd * 2 + cc:hd * 2 + cc + 1]
                    pS = pSs[hd]
                    k.mm(pS[:, 0:64], AR[:, 0:128], d["S16"], start=True, stop=False)
                    k.mm(pS[:, 0:64], S_["AK"][hd][:, 0:128], vh, start=False, stop=True)
                    k.act(d["S32g"], d["S32"], AF.Copy, scale=gm)
                    k.copy("scalar", d["W"][rs, :], pS[rs, 0:64])
                    yield
                for hd in range(2):
                    d = hd_b[hd]
                    pS = pSs[hd]
                    k.mm(pS[:, 64:128], S_["T"][hd], d["W"])
                    k.copy("scalar", d["U"][rs, :], pS[rs, 64:128])
                    yield
                for hd in range(2):
                    d = hd_b[hd]
                    vh = v_tok[:, hd * 64:(hd + 1) * 64]
                    gm = GM[:, hd * 2 + cc:hd * 2 + cc + 1]
                    pS = pSs[hd]
                    k.mm(pS[:, 192:256], btok[cc], d["U"], start=True, stop=False)
                    k.mm(pS[:, 192:256], ktok[cc], vh, start=False, stop=True)
                    k.mm(pS[:, 128:192], AR[:, 128:256], d["S16"], start=True, stop=False)
                    k.mm(pS[:, 128:192], S_["AB"][hd][:, 128:256], d["U"], start=False, stop=False)
                    k.mm(pS[:, 128:192], S_["AK"][hd][:, 128:256], vh, start=False, stop=True)
                    k.stt(d["S32"], pS[:, 192:256], gm, d["S32g"], ALU.mult, ALU.add)
                    k.copy("vector", d["S16"], d["S32"])
                    k.copy("scalar", y_all[rs, t0 + hd * 64:t0 + hd * 64 + 64], pS[rs, 128:192])
                    yield

        run_interleaved([prep(0)])
        for i in range(NT):
            gens = [seq(i)]
            if i + 1 < NT:
                gens.append(prep(i + 1))
            run_interleaved(gens)
        if RW_STAGE < 7:
            continue
        c3 = Carver()
        c3.off = post_off
        sq = c3.take(LP)
        g_all = c3.take(LP)
        gnw = c3.take(128); gnb = c3.take(128)
        s1 = c3.take(34); s2 = c3.take(34); mn = c3.take(34); m2 = c3.take(34)
        yb = c3.take(LP // 2, BF16)
        k.dma("sync", gnw, gn_w[l:l + 1, hp * 128:(hp + 1) * 128].partition_broadcast(128), "gnw")
        k.dma("sync", gnb, gn_b[l:l + 1, hp * 128:(hp + 1) * 128].partition_broadcast(128), "gnb")
        k.dma("sync", g_all.rearrange("p (t c) -> p t c", c=128),
              g_s[:, hp * 128:(hp + 1) * 128].rearrange("(t p) c -> p t c", p=128), "gall")
        y3 = y_all.rearrange("p (a d) -> p a d", d=64)
        sq3 = sq.rearrange("p (a d) -> p a d", d=64)
        k.reduce(s1, y3)
        k.tt("gpsimd", sq, y_all, y_all, ALU.mult)
        k.reduce(s2, sq3)
        k.ts("vector", mn, s1, 1.0 / 64, None, ALU.mult)
        k.tt("vector", m2, mn, mn, ALU.mult)
        k.stt(s2, s2, 1.0 / 64, m2, ALU.mult, ALU.subtract)
        k.ts("vector", s2, s2, GN_EPS, None, ALU.add)
        k.act(s2, s2, AF.Sqrt)
        k.recip(s2, s2)
        k.tt("vector", sq3, y3, mn.unsqueeze(2).to_broadcast([128, 34, 64]), ALU.subtract)
        k.tt("vector", sq3, sq3, s2.unsqueeze(2).to_broadcast([128, 34, 64]), ALU.mult)
        sq4 = sq.rearrange("p (t c) -> p t c", c=128)
        k.tt("vector", sq4, sq4, gnw.unsqueeze(1).to_broadcast([128, NT, 128]), ALU.mult)
        k.tt("vector", sq4, sq4, gnb.unsqueeze(1).to_broadcast([128, NT, 128]), ALU.add)
        vt3 = vt_all.rearrange("p (a d) -> p a d", d=64)
        cf = coef_all.rearrange("p t h -> p (t h)")
        k.tt("vector", vt3, vt3, cf.unsqueeze(2).to_broadcast([128, 34, 64]), ALU.mult)
        k.tt("vector", sq, sq, vt_all, ALU.add)
        k.tt("vector", yb, sq, g_all, ALU.mult)
        for i in range(NT):
            pt = pss[2 + (i % 2)][:].bitcast(BF16)[:, 0:128]
            k.tr(pt, yb[:, i * 128:(i + 1) * 128], ident_bf[:])
            k.copy("scalar" if i % 2 else "vector", XT[:, hp, i * 128:(i + 1) * 128], pt)


def build_program(nlayers=DEPTH, dbg=None, stages=("mix", "ffn")):
    nc = bass.Bass("TRN2", target_bir_lowering=False)
    dram = {}

    def din(name, shape, dt=F32):
        dram[name] = nc.dram_tensor(name, list(shape), dt, kind="ExternalInput").ap()
        return dram[name]

    def dscr(name, shape, dt=F32):
        kind = "ExternalOutput" if (dbg and name in dbg) else "Internal"
        dram[name] = nc.dram_tensor(name, list(shape), dt, kind=kind).ap()
        return dram[name]

    x = din("x", [SEQ, D])
    meta = din("meta", [NMETA, D])
    norm1_g = din("norm1_g", [DEPTH, D])
    norm2_g = din("norm2_g", [DEPTH, D])
    w_in = din("w_in", [DEPTH, D, RWC + FXC])
    w_o = din("w_o", [DEPTH, D, D])
    ffn_w_in = din("ffn_w_in", [DEPTH, D, 2 * DFF])
    ffn_w_out = din("ffn_w_out", [DEPTH, DFF, D])
    conv_wp = din("conv_wp", [DEPTH, 128, 44, 3])
    conv_bp = din("conv_bp", [DEPTH, 128, 44])
    mu_p = din("mu_p", [DEPTH, 128, 15])
    rwvec_p = din("rwvec_p", [DEPTH, 128, 4, 4])
    wup_aug = din("wup_aug", [DEPTH, 65, 512])
    a_up = din("a_up", [DEPTH, 64, 512])
    g_up = din("g_up", [DEPTH, 160, 512])
    gn_w = din("gn_w", [DEPTH, 512])
    gn_b = din("gn_b", [DEPTH, 512])
    fx_bf = din("fx_bf", [DEPTH, 8])
    fx_qg = din("fx_qg", [DEPTH, 64])
    fx_kg = din("fx_kg", [DEPTH, 64])
    cmask = din("cmask", [128, 8, 128])
    out = nc.dram_tensor("out", [SEQ, D], F32, kind="ExternalOutput").ap()

    rw_s = dscr("rw_s", [15, 128, LP])
    g_s = dscr("g_s", [LP, 512])
    qa_s = dscr("qa_s", [NB, 66, LP], BF16)
    ka_s = dscr("ka_s", [NB, 66, LP], BF16)
    sgo_s = dscr("sgo_s", [LP, 512], BF16)
    hdbg = dscr("hdbg", [LP, D]) if dbg else None
    mixdbg = dscr("mixdbg", [128, 8, LP], BF16) if dbg else None

    with contextlib.ExitStack() as st:
        def sb(name, shape, dt=F32):
            return st.enter_context(nc.sbuf_tensor(name, list(shape), dt))

        h = sb("h", [128, NT, D])
        XT = sb("XT", [128, 8, LP], BF16)
        ident_bf = sb("ident_bf", [128, 128], BF16)
        ident_f = sb("ident_f", [128, 128])
        ident_r = ident_f
        cm = sb("cm", [128, 8, 128])
        small = sb("small", [128, 256])
        ARENA_W = 24200
        arena = sb("arena", [128, ARENA_W])
        pss = [st.enter_context(nc.psum_tensor("ps%d" % i, [128, 512], F32)) for i in range(8)]

        P = Prog(nc)
        k = K(P)

        class Carver:
            def __init__(self):
                self.off = 0

            def take(self, nwords, dt=F32, shape=None):
                a = arena[:, self.off:self.off + nwords]
                self.off += nwords
                assert self.off <= ARENA_W, self.off
                if dt != F32:
                    a = a.bitcast(dt)
                if shape is not None:
                    names = " ".join("d%d" % i for i in range(len(shape)))
                    kw = {"d%d" % i: s for i, s in enumerate(shape)}
                    a = a.rearrange("p (%s) -> p %s" % (names, names), **kw)
                return a

        k.dma("sync", cm[:], cmask, "cm")
        k.copy("vector", ident_f[:], cm[:, 0, :])
        k.copy("vector", ident_bf[:], cm[:, 0, :])
        negm_bf = sb("negm_bf", [128, 128], BF16)
        k.copy("vector", negm_bf[:], cm[:, 4, :])

        ones_bf = sb("ones_bf", [8, LP], BF16)
        k.memset("vector", ones_bf[:], 1.0)
        k.dma("sync", ka_s[:, 64, :], ones_bf[:], "kones")
        k.dma("sync", ka_s[:, 65, :], ones_bf[:], "kones")
        k.memset("vector", h[:, NT - 1, :], 0.0)
        k.dma("sync", h[0:16, 0, :], meta, "hload")
        k.dma("sync", h[16:128, 0, :], x[0:112, :], "hload")
        k.dma("sync", h[:, 1:16, :], x[112:112 + 1920, :].rearrange("(t p) d -> p t d", p=128), "hload")
        k.dma("sync", h[0:16, 16, :], x[2032:2048, :], "hload")

        ssq = small[:, 0:17]
        rstd = small[:, 32:49]
        tmp17 = small[:, 64:81]

        def norm_T(g_row):
            c = Carver()
            gbc = c.take(D)
            k.dma("sync", gbc, g_row.partition_broadcast(128), "gbc")
            junk = c.take(512, BF16)
            xn = [c.take(512, BF16), c.take(512, BF16)]
            for i in range(NT):
                k.act(junk, h[:, i, :], AF.Square, accum=ssq[:, i:i + 1])
            k.ts("vector", tmp17, ssq, 1.0 / D, EPS, ALU.mult, ALU.add)
            k.act(tmp17, tmp17, AF.Sqrt)
            k.recip(rstd, tmp17)
            for i in range(NT):
                xb = xn[i % 2]
                k.stt(xb, h[:, i, :], rstd[:, i:i + 1], gbc, ALU.mult, ALU.mult)
                pt = pss[i % 2][:].bitcast(BF16).rearrange("p (a b) -> p a b", b=128)[:, 0:8, :]
                for kc in range(8):
                    k.tr(pt[:, kc, :], xb[:, kc * 128:(kc + 1) * 128], ident_bf[:])
                k.copy("scalar" if i % 2 else "vector", XT[:, :, i * 128:(i + 1) * 128], pt)

        def add_to_h(i, n2, ps):
            k.tt("vector", h[:, i, n2 * 512:(n2 + 1) * 512], ps, h[:, i, n2 * 512:(n2 + 1) * 512], ALU.add)

        def ffn(l):
            norm_T(norm2_g[l:l + 1, :])
            c = Carver()
            cw = c.take(44 * 3, F32, (44, 3))
            cb = c.take(44)
            k.dma("sync", cw, conv_wp[l], "cw")
            k.dma("sync", cb, conv_bp[l], "cb")
            groups = [list(range(0, 5)), list(range(5, 10)), list(range(10, 14)), list(range(14, 18)), list(range(18, 22))]
            hid = c.take(5 * LP // 2, BF16, (5, LP))
            wout = c.take(5 * D // 2, BF16, (5, D))
            wu = [c.take(512, BF16, (8, 128)) for _ in range(2)]
            wg = [c.take(512, BF16, (8, 128)) for _ in range(2)]
            HU = c.take(LP + 2)
            HG = c.take(LP + 2)
            t0 = c.take(LP)
            t1 = c.take(LP)
            t2 = c.take(LP)
            tx = c.take(LP)
            k.memset("vector", HU[:, 0:2], 0.0)
            k.memset("vector", HG[:, 0:2], 0.0)
            wi = ffn_w_in[l].rearrange("(kc p) c -> p kc c", p=128)
            wo = ffn_w_out[l].rearrange("(kc p) n -> p kc n", p=128)
            cnt = 0
            for gi, js in enumerate(groups):
                k.dma("gpsimd", wout[:, 0:len(js), :], wo[:, js[0]:js[0] + len(js), :], "wout")
                for jj, j in enumerate(js):
                    wub = wu[cnt % 2]
                    wgb = wg[cnt % 2]
                    k.dma("gpsimd", wub, wi[:, :, j * 128:(j + 1) * 128], "wu%d" % (cnt % 2))
                    k.dma("gpsimd", wgb, wi[:, :, DFF + j * 128:DFF + (j + 1) * 128], "wg%d" % (cnt % 2))
                    cnt += 1
                    for n in range(5):
                        c0 = n * 512
                        N = min(512, LP - c0)
                        pu = pss[2 + (n % 2)]
                        pg = pss[4 + (n % 2)]
                        for kc in range(8):
                            k.mm(pu[:, 0:N], wub[:, kc, :], XT[:, kc, c0:c0 + N], start=kc == 0, stop=kc == 7)
                        for kc in range(8):
                            k.mm(pg[:, 0:N], wgb[:, kc, :], XT[:, kc, c0:c0 + N], start=kc == 0, stop=kc == 7)
                        k.copy("scalar", HU[:, 2 + c0:2 + c0 + N], pu[:, 0:N])
                        k.copy("scalar", HG[:, 2 + c0:2 + c0 + N], pg[:, 0:N])
                    ju = j
                    jg = 22 + j
                    k.ts("vector", t0, HU[:, 2:LP + 2], cw[:, ju, 2:3], cb[:, ju:ju + 1], ALU.mult, ALU.add)
                    k.ts("vector", t1, HG[:, 2:LP + 2], cw[:, jg, 2:3], cb[:, jg:jg + 1], ALU.mult, ALU.add)
                    k.ts("vector", tx, HU[:, 1:LP + 1], cw[:, ju, 1:2], None, ALU.mult)
                    k.ts("vector", t2, HG[:, 1:LP + 1], cw[:, jg, 1:2], None, ALU.mult)
                    k.tt("vector", t0, t0, tx, ALU.add)
                    k.tt("gpsimd", t1, t1, t2, ALU.add)
                    k.ts("vector", tx, HU[:, 0:LP], cw[:, ju, 0:1], None, ALU.mult)
                    k.ts("vector", t2, HG[:, 0:LP], cw[:, jg, 0:1], None, ALU.mult)
                    k.tt("vector", t0, t0, tx, ALU.add)
                    k.tt("gpsimd", t1, t1, t2, ALU.add)
                    k.act(t2, t1, AF.Silu)
                    k.tt("vector", hid[:, jj, :], t2, t0, ALU.mult)
                for i in range(NT):
                    for n2 in range(2):
                        ps = pss[6 + ((i * 2 + n2) % 2)]
                        for jj in range(len(js)):
                            k.mm(ps[:], hid[:, jj, i * 128:(i + 1) * 128], wout[:, jj, n2 * 512:(n2 + 1) * 512],
                                 start=jj == 0, stop=jj == len(js) - 1)
                        add_to_h(i, n2, ps[:])

        def out_proj(l):
            c = Carver()
            wob = c.take(8 * D // 2, BF16, (8, D))
            k.dma("gpsimd", wob, w_o[l].rearrange("(kc p) n -> p kc n", p=128), "wob")
            for i in range(NT):
                for n2 in range(2):
                    ps = pss[6 + ((i * 2 + n2) % 2)]
                    for kc in range(8):
                        k.mm(ps[:], XT[:, kc, i * 128:(i + 1) * 128], wob[:, kc, n2 * 512:(n2 + 1) * 512],
                             start=kc == 0, stop=kc == 7)
                    add_to_h(i, n2, ps[:])

        env = dict(locals())
        env["rwkv_inproj"] = rwkv_inproj
        for l in range(nlayers):
            if "mix" in stages:
                mixer(env, l)
                if RWKV_ENABLED:
                    rwkv(env, l)
                else:
                    k.memset("gpsimd", XT[:, 0:4, :], 0.0)
                if dbg and "mixdbg" in dbg and l == 0:
                    k.dma("sync", mixdbg, XT[:], "mixdbg")
                out_proj(l)
            if "ffn" in stages:
                ffn(l)

        if dbg and "hdbg" in dbg:
            k.dma("sync", hdbg.rearrange("(t p) d -> p t d", p=128), h[:], "hdbg")
        k.dma("sync", out[0:112, :], h[16:128, 0, :], "ost")
        k.dma("sync", out[112:112 + 1920, :].rearrange("(t p) d -> p t d", p=128), h[:, 1:16, :], "ost")
        k.dma("sync", out[2032:2048, :], h[0:16, 16, :], "ost")
        fw = ["ost"] + (["hdbg"] if dbg and "hdbg" in dbg else []) + (["mixdbg"] if dbg and "mixdbg" in dbg else [])
        P.emit(final_waits=fw)
    return nc, P


def _masks():
    m = np.zeros((128, 8, 128), np.float32)
    j = np.arange(128)[:, None]
    t = np.arange(128)[None, :]
    same = (j // 64) == (t // 64)
    m[:, 0, :] = (j == t)
    m[:, 1, :] = same & (j < t)
    m[:, 2, :] = same & (j <= t)
    m[:, 3, :] = same & (j > t)
    m[:, 4, :] = np.where(j <= t, 0.0, -30000.0)
    m[:, 5, :] = 1.0
    m[:, 6, :] = (j <= t)
    m[:, 7, :] = same
    return m


def prep_shared(inp):
    f = lambda a: np.ascontiguousarray(np.asarray(a, dtype=np.float32))
    sh = {}
    for kk_ in ("meta", "norm1_g", "norm2_g", "w_in", "w_o", "ffn_w_in", "ffn_w_out"):
        sh[kk_] = f(inp[kk_])
    cw = f(inp["ffn_conv_w"])
    sh["conv_wp"] = f(cw.reshape(DEPTH, 3, 44, 128).transpose(0, 3, 2, 1))
    sh["conv_bp"] = f(f(inp["ffn_conv_b"]).reshape(DEPTH, 44, 128).transpose(0, 2, 1))
    mu = np.zeros((DEPTH, 15 * 128), np.float32)
    mu[:, :RWC] = f(inp["rw_mu"])
    sh["mu_p"] = f(mu.reshape(DEPTH, 15, 128).transpose(0, 2, 1))
    vecs = np.stack([f(inp["rw_k_k"]), f(inp["rw_k_a"]), f(inp["rw_a0"]), f(inp["rw_r_k"]).reshape(DEPTH, 512)], -1)
    sh["rwvec_p"] = f(vecs.reshape(DEPTH, 4, 128, 4).transpose(0, 2, 1, 3))
    sh["wup_aug"] = f(np.concatenate([f(inp["rw_w_up"]), f(inp["rw_w0"])[:, None, :]], 1))
    sh["a_up"] = f(inp["rw_a_up"])
    sh["g_up"] = f(inp["rw_g_up"])
    sh["gn_w"] = f(inp["rw_gn_w"])
    sh["gn_b"] = f(inp["rw_gn_b"])
    sh["fx_bf"] = f(inp["fx_b_f"])
    sh["fx_qg"] = f(inp["fx_q_g"])
    sh["fx_kg"] = f(inp["fx_k_g"])
    sh["cmask"] = _masks()
    return sh


def kernel(**inputs):
    nc, P = build_program()
    sh = prep_shared(inputs)
    x = np.asarray(inputs["x"], dtype=np.float32)
    in_maps = []
    for b in range(NB):
        m = dict(sh)
        m["x"] = np.ascontiguousarray(x[b])
        in_maps.append(m)
    res = run_bass_kernel_spmd(nc, in_maps, core_ids=list(range(NB)))
    return np.stack([np.asarray(r["out"], dtype=np.float32) for r in res.results], 0)
```

```python
import numpy as np
import concourse.bass as bass
import concourse.mybir as mybir

F32 = mybir.dt.float32
BF16 = mybir.dt.bfloat16
AF = mybir.ActivationFunctionType
ALU = mybir.AluOpType
AX = mybir.AxisListType

_DT_SIZE = {F32: 4, BF16: 2, mybir.dt.float32r: 4}


def _dsize(dt):
    try:
        return _DT_SIZE[dt]
    except KeyError:
        return mybir.dt.size(dt)


def ap_region(ap):
    name = ap.name
    pat = ap.ap
    off = int(ap.offset)
    es = _dsize(ap.dtype)
    sp = str(ap.space)
    if sp == "DRAM":
        lo = off
        hi = off
        for st, cnt in pat:
            if cnt > 1:
                if st >= 0:
                    hi += st * (cnt - 1)
                else:
                    lo += st * (cnt - 1)
        return (name, 0, 1, lo * es, (hi + 1) * es)
    if sp == "PSUM":
        return (name, 0, 128, 0, 2048)
    pstep, pcnt = pat[0]
    if pstep > 0:
        rowlen = pstep
    else:
        rowlen = int(np.prod(ap.tensor.shape[1:])) * _dsize(ap.tensor.dtype) // es
    p0 = off // rowlen
    f0 = off % rowlen
    lo = f0
    hi = f0
    for st, cnt in pat[1:]:
        if cnt > 1:
            if st >= 0:
                hi += st * (cnt - 1)
            else:
                lo += st * (cnt - 1)
    np_ = pcnt if pstep != 0 else 1
    return (name, p0, p0 + np_, lo * es, (hi + 1) * es)


class Op:
    __slots__ = ("eng", "fn", "idx", "cnt_eng", "cnt", "waits", "inc", "clock", "dma_key", "phase")


class Prog:
    ENGS = ("tensor", "vector", "scalar", "gpsimd", "sync")
    CAP = 6000

    def __init__(self, nc):
        self.nc = nc
        self.ops = []
        self.streams = {e: [] for e in self.ENGS}
        self.acc = {}
        self.counts = {}
        self.seen = {e: {} for e in self.ENGS}
        self.op_by = {}

    def add(self, eng, fn, reads=(), writes=(), dma_key=None):
        op = Op()
        op.eng = eng
        op.fn = fn
        op.idx = len(self.ops)
        op.dma_key = dma_key
        op.cnt_eng = ("dma", dma_key) if dma_key is not None else eng
        op.inc = False
        op.phase = getattr(self, 'phase', '')
        need = {}
        rregs = [ap_region(a) for a in reads]
        wregs = [ap_region(a) for a in writes]
        for regs, is_w in ((rregs, False), (wregs, True)):
            for (name, p0, p1, b0, b1) in regs:
                for rec in self.acc.get(name, ()):
                    if rec[1] <= p0 or rec[0] >= p1 or rec[3] <= b0 or rec[2] >= b1:
                        continue
                    if not (is_w or rec[4]):
                        if not (name.startswith("ps") and rec[5] != op.cnt_eng):
                            continue
                    ce, c = rec[5], rec[6]
                    if isinstance(ce, tuple):
                        c = self.counts[ce]
                    if ce == eng and dma_key is None:
                        if eng == "tensor":
                            continue
                    if need.get(ce, 0) < c:
                        need[ce] = c
        seen = self.seen[eng]
        waits = []
        for ce, c in need.items():
            if seen.get(ce, 0) >= c:
                continue
            waits.append((ce, c))
            src = self.op_by[(ce, c)]
            src.inc = True
            for k, v in src.clock.items():
                if seen.get(k, 0) < v:
                    seen[k] = v
            seen[ce] = c
        op.waits = waits
        cnt = self.counts.get(op.cnt_eng, 0) + 1
        self.counts[op.cnt_eng] = cnt
        op.cnt = cnt
        op.clock = dict(seen)
        self.op_by[(op.cnt_eng, cnt)] = op
        for regs, is_w in ((rregs, False), (wregs, True)):
            for (name, p0, p1, b0, b1) in regs:
                lst = self.acc.setdefault(name, [])
                new = []
                for rec in lst:
                    covered = rec[0] >= p0 and rec[1] <= p1 and rec[2] >= b0 and rec[3] <= b1
                    if covered and (is_w or (not rec[4] and rec[5] == op.cnt_eng)):
                        continue
                    new.append(rec)
                new.append((p0, p1, b0, b1, is_w, op.cnt_eng, cnt, op.idx))
                self.acc[name] = new
        self.ops.append(op)
        self.streams[eng].append(op)
        return op

    def pe(self, fn, reads, writes):
        return self.add("tensor", fn, reads, writes)

    def dve(self, fn, reads, writes):
        return self.add("vector", fn, reads, writes)

    def act(self, fn, reads, writes):
        return self.add("scalar", fn, reads, writes)

    def pool(self, fn, reads, writes):
        return self.add("gpsimd", fn, reads, writes)

    def dma(self, queue, out, in_, key, **kw):
        return self.add(queue, lambda e: e.dma_start(out=out, in_=in_, **kw), [in_], [out], dma_key=key)

    def emit(self, final_waits=()):
        nc = self.nc
        incs = {}
        for op in self.ops:
            if op.dma_key is not None or op.inc:
                incs.setdefault(op.cnt_eng, []).append(op.cnt)
        import contextlib
        with contextlib.ExitStack() as st:
            semtab = {}
            valmap = {}
            nsem = 0
            for ce, lst in incs.items():
                is_dma = isinstance(ce, tuple)
                cap = 10 ** 9 if is_dma else self.CAP
                step = 16 if is_dma else 1
                sems = []
                for i, c in enumerate(lst):
                    si = i // cap
                    if si >= len(sems):
                        sems.append(st.enter_context(nc.semaphore("s%d" % nsem)))
                        nsem += 1
                    valmap[(ce, c)] = (sems[si], (i % cap + 1) * step)
                semtab[ce] = sems
            self.nsem = nsem
            block = st.enter_context(nc.Block())

            def make(engname):
                ops = self.streams[engname]

                def body(e):
                    for op in ops:
                        for w in op.waits:
                            s, v = valmap[w]
                            e.wait_ge(s, v)
                        ins = op.fn(e)
                        if op.dma_key is not None:
                            s, v = valmap[(op.cnt_eng, op.cnt)]
                            ins.then_inc(s, 16)
                        elif op.inc:
                            s, v = valmap[(op.cnt_eng, op.cnt)]
                            ins.then_inc(s, 1)
                    if engname == "sync":
                        for key in final_waits:
                            ce = ("dma", key)
                            c = self.counts[ce]
                            s, v = valmap[(ce, c)]
                            e.wait_ge(s, v)
                return body

            for en in self.ENGS:
                getattr(block, en)(make(en))

import contextlib
from concourse.bass_utils import run_bass_kernel_spmd

F32R = F32
D = 1024
SEQ = 2048
NMETA = 16
LP = 2176
NT = 17
DEPTH = 4
RWC = 1824
FXC = 2056
DFF = 2816
EPS = 1e-6
GN_EPS = 64e-5
NB = 8
RWKV_ENABLED = True
import os
RW_STAGE = int(os.environ.get('RW_STAGE', '9'))


class K:
    def __init__(self, P):
        self.P = P

    def mm(self, out, lhsT, rhs, start=True, stop=True):
        self.P.pe(lambda e: e.matmul(out, lhsT, rhs, start=start, stop=stop), [lhsT, rhs], [out])

    def tr(self, out, in_, ident):
        self.P.pe(lambda e: e.transpose(out, in_, ident), [in_, ident], [out])

    def act(self, out, in_, func, bias=None, scale=None, accum=None):
        kw = {}
        reads = [in_]
        writes = [out]
        if bias is not None:
            kw["bias"] = bias
            if not isinstance(bias, (int, float)):
                reads.append(bias)
        if scale is not None:
            kw["scale"] = scale
            if not isinstance(scale, (int, float)):
                reads.append(scale)
        if accum is not None:
            kw["accum_out"] = accum
            writes.append(accum)
        self.P.act(lambda e: e.activation(out, in_, func, **kw), reads, writes)

    def tt(self, eng, out, in0, in1, op):
        self.P.add(eng, lambda e: e.tensor_tensor(out, in0, in1, op), [in0, in1], [out])

    def ts(self, eng, out, in0, s1, s2, op0, op1=None):
        reads = [in0]
        for s in (s1, s2):
            if s is not None and not isinstance(s, (int, float)):
                reads.append(s)
        if op1 is None:
            self.P.add(eng, lambda e: e.tensor_scalar(out, in0, s1, None, op0), reads, [out])
        else:
            self.P.add(eng, lambda e: e.tensor_scalar(out, in0, s1, s2, op0, op1), reads, [out])

    def stt(self, out, in0, scalar, in1, op0, op1):
        reads = [in0, in1]
        if not isinstance(scalar, (int, float)):
            reads.append(scalar)
        self.P.dve(lambda e: e.scalar_tensor_tensor(out, in0, scalar, in1, op0, op1), reads, [out])

    def copy(self, eng, out, in_):
        if eng == "scalar":
            self.P.act(lambda e: e.copy(out, in_), [in_], [out])
        else:
            self.P.add(eng, lambda e: e.tensor_copy(out, in_), [in_], [out])

    def memset(self, eng, ap, val):
        self.P.add(eng, lambda e: e.memset(ap, val), [], [ap])

    def recip(self, out, in_):
        self.P.dve(lambda e: e.reciprocal(out, in_), [in_], [out])

    def reduce(self, out, in_, op=ALU.add):
        self.P.dve(lambda e: e.tensor_reduce(out, in_, AX.X, op), [in_], [out])

    def dma(self, q, out, in_, key, **kw):
        self.P.dma(q, out, in_, key, **kw)


def mixer(env, l):
    k = env["k"]; P = env["P"]; Carver = env["Carver"]; pss = env["pss"]; XT = env["XT"]
    cm = env["cm"]; ident_bf = env["ident_bf"]; negm_bf = env["negm_bf"]; small = env["small"]
    w_in = env["w_in"]; fx_bf = env["fx_bf"]; fx_qg = env["fx_qg"]; fx_kg = env["fx_kg"]
    qa_s = env["qa_s"]; ka_s = env["ka_s"]; sgo_s = env["sgo_s"]; norm1_g = env["norm1_g"]
    P.phase = 'norm1'
    env["norm_T"](norm1_g[l:l + 1, :])
    c = Carver()
    c.off = 2600
    vaug = c.take(NT * 8 * 65 // 2 + 4, BF16)[:, 0:NT * 8 * 65].rearrange("p (t h d) -> p t h d", t=NT, h=8, d=65)
    negc = c.take(NT * 8, F32, (NT, 8))
    nlf = c.take(NT * 8, F32, (NT, 8))
    gq = c.take(64)
    gk = c.take(64)
    bfb = c.take(8)
    tmpf = [c.take(512) for _ in range(2)]
    sgst = [c.take(256, BF16) for _ in range(2)]
    persist_off = c.off
    wfx = [c.take(2048, BF16, (8, 512)) for _ in range(2)]
    wfl = c.take(32, BF16, (8, 8))
    qbf = [c.take(256, BF16, (8, 64)) for _ in range(2)]
    qst = [c.take(512, BF16, (8, 128)) for _ in range(2)]
    ss8 = small[:, 96:104]
    rs8 = small[:, 104:112]
    t8 = small[:, 112:120]
    chl = small[:, 128:136].bitcast(BF16)
    cst = c.take(64, BF16)
    k.dma("sync", gq, fx_qg[l:l + 1, :].partition_broadcast(128), "gq")
    k.dma("sync", gk, fx_kg[l:l + 1, :].partition_broadcast(128), "gk")
    k.dma("sync", bfb, fx_bf[l:l + 1, :].partition_broadcast(128), "bfb")
    k.ts("vector", gq, gq, 0.125, None, ALU.mult)
    k.memset("vector", vaug[:, :, :, 64:65], 1.0)
    wv = w_in[l].rearrange("(kc p) c -> p kc c", p=128)
    P.phase = 'fox_inproj'
    for ci in range(4):
        wb = wfx[ci % 2]
        k.dma("gpsimd", wb, wv[:, :, RWC + ci * 512:RWC + (ci + 1) * 512], "wfx%d" % (ci % 2))
        for i in range(NT):
            ps = pss[2 + (i % 2)]
            for kc in range(8):
                k.mm(ps[:], XT[:, kc, i * 128:(i + 1) * 128], wb[:, kc, :], start=kc == 0, stop=kc == 7)
            ps3 = ps[:].rearrange("p (h d) -> p h d", d=64)
            if ci < 2:
                tf = tmpf[i % 2]
                tf3 = tf.rearrange("p (h d) -> p h d", d=64)
                k.act(tf, ps[:], AF.Square)
                k.reduce(ss8, tf3)
                k.ts("vector", ss8, ss8, 1.0 / 64, EPS, ALU.mult, ALU.add)
                k.act(ss8, ss8, AF.Sqrt)
                k.recip(rs8, ss8)
                k.tt("vector", tf3, ps3, rs8.unsqueeze(2).to_broadcast([128, 8, 64]), ALU.mult)
                g = gq if ci == 0 else gk
                qb = qbf[i % 2]
                k.tt("gpsimd", qb, tf3, g.unsqueeze(1).to_broadcast([128, 8, 64]), ALU.mult)
                pt = pss[4 + (i % 2)][:].bitcast(BF16).rearrange("p (a b) -> p a b", b=128)[0:64, 0:8, :]
                for hh in range(8):
                    k.tr(pt[:, hh, :], qb[:, hh, :], ident_bf[:])
                qs = qst[i % 2]
                k.copy("scalar", qs[0:64, :, :], pt)
                dst = (qa_s if ci == 0 else ka_s)[:, 0:64, i * 128:(i + 1) * 128].rearrange("h r t -> r h t")
                k.dma("sync", dst, qs[0:64, :, :], "qst%d" % (i % 2))
            elif ci == 2:
                k.copy("scalar", vaug[:, i, :, 0:64], ps3)
            else:
                sg = sgst[i % 2]
                k.act(sg, ps[:], AF.Sigmoid)
                k.dma("sync", sgo_s[i * 128:(i + 1) * 128, :], sg, "sgst%d" % (i % 2))
    P.phase = 'fox_fl'
    k.dma("gpsimd", wfl, wv[:, :, RWC + 2048:RWC + 2056], "wfl")
    for i in range(NT):
        ps = pss[2 + (i % 2)]
        for kc in range(8):
            k.mm(ps[:, 0:8], XT[:, kc, i * 128:(i + 1) * 128], wfl[:, kc, :], start=kc == 0, stop=kc == 7)
        k.tt("vector", t8, ps[:, 0:8], bfb, ALU.add)
        k.act(t8, t8, AF.Exp, scale=-1.0)
        k.act(nlf[:, i, :], t8, AF.Ln, bias=1.0)
    for i in range(NT):
        pc = pss[4 + (i % 2)]
        for ip in range(i):
            k.mm(pc[:, 0:8], cm[:, 5, :], nlf[:, ip, :], start=ip == 0, stop=False)
        k.mm(pc[:, 0:8], cm[:, 6, :], nlf[:, i, :], start=i == 0, stop=True)
        k.copy("vector", negc[:, i, :], pc[:, 0:8])
        k.ts("vector", chl[:, 0:8], pc[:, 0:8], -1.0, None, ALU.mult)
        k.stt(chl[:, 8:16], pc[:, 0:8], -1.0, chl[:, 0:8], ALU.mult, ALU.subtract)
        pt = pss[2 + (i % 2)][:].bitcast(BF16)[0:16, 0:128]
        k.tr(pt, chl, ident_bf[:])
        k.copy("vector", cst[0:16, :], pt)
        k.dma("sync", qa_s[:, 64, i * 128:(i + 1) * 128], cst[0:8, :], "cst")
        k.dma("sync", qa_s[:, 65, i * 128:(i + 1) * 128], cst[8:16, :], "cst")
    P.phase = 'rw_inproj'
    if RWKV_ENABLED:
        env["rwkv_inproj"](env, l, persist_off)
    P.phase = 'fox_att'
    c2 = Carver()
    c2.off = persist_off
    qT = [c2.take(LP // 2, BF16) for _ in range(2)]
    kT = [c2.take(LP // 2, BF16) for _ in range(2)]
    Eb = [c2.take(256, BF16) for _ in range(3)]
    zb = c2.take(256, BF16)
    yfx = c2.take(NT * 512 // 2, BF16, (NT, 512))
    rd = small[:, 140:141]
    k.memset("vector", zb, 0.0)
    ecnt = 0
    for hh in range(8):
        q_ = qT[hh % 2]
        k_ = kT[hh % 2]
        k.dma("sync", q_[0:66, :], qa_s[hh], "qT%d" % (hh % 2))
        k.dma("sync", k_[0:66, :], ka_s[hh], "kT%d" % (hh % 2))
        for qt in range(5):
            q0 = qt * 512
            Nq = min(512, LP - q0)
            nqb = Nq // 128
            Ob = pss[6 + (qt % 2)]
            Oacc = Ob[:, 0:260].rearrange("p (a d) -> p a d", d=65)
            k.mm(Ob[:, 0:260], zb[:, 0:128], zb[:, 0:260], start=True, stop=True)
            kb_max = (q0 + Nq) // 128 - 1

            def scores(kb):
                j = kb - 4 * qt
                cs = max(0, j) * 128
                nonlocal ecnt
                pS = pss[ecnt % 2]
                E = Eb[ecnt % 3]
                ecnt += 1
                k.mm(pS[:, cs:Nq], k_[0:66, kb * 128:(kb + 1) * 128], q_[0:66, q0 + cs:q0 + Nq], start=True, stop=j < 0)
                if j >= 0:
                    k.mm(pS[:, cs:cs + 128], ident_bf[:], negm_bf[:], start=False, stop=True)
                return (kb, j, cs, pS, E)

            pend = scores(0)
            for kb in range(kb_max + 1):
                cur = pend
                pend = scores(kb + 1) if kb + 1 <= kb_max else None
                _, j, cs, pS, E = cur
                k.act(E[:, cs:Nq], pS[:, cs:Nq], AF.Exp, bias=negc[:, kb, hh:hh + 1])
                for qbl in range(max(0, j), nqb):
                    P.pe(lambda e, o=Oacc[:, qbl, :], a=E[:, qbl * 128:(qbl + 1) * 128], b=vaug[:, kb, hh, :]:
                         e.matmul(o, a, b, start=False, stop=True, skip_group_check=True),
                         [E[:, qbl * 128:(qbl + 1) * 128], vaug[:, kb, hh, :]], [Oacc[:, qbl, :]])
            for qbl in range(nqb):
                i = 4 * qt + qbl
                k.recip(rd, Oacc[:, qbl, 64:65])
                k.ts("vector", yfx[:, i, hh * 64:(hh + 1) * 64], Oacc[:, qbl, 0:64], rd, None, ALU.mult)
    P.phase = 'fox_gate'
    for i in range(NT):
        sg = sgst[i % 2]
        k.dma("sync", sg, sgo_s[i * 128:(i + 1) * 128, :], "sgld%d" % (i % 2))
        yg = tmpf[i % 2].bitcast(BF16)[:, 0:512]
        k.tt("vector", yg, yfx[:, i, :], sg, ALU.mult)
        pt = pss[2 + (i % 2)][:].bitcast(BF16).rearrange("p (a b) -> p a b", b=128)[:, 0:4, :]
        for cc in range(4):
            k.tr(pt[:, cc, :], yg[:, cc * 128:(cc + 1) * 128], ident_bf[:])
        k.copy("scalar", XT[:, 4:8, i * 128:(i + 1) * 128], pt)


def rwkv_inproj(env, l, off):
    k = env["k"]; P = env["P"]; Carver = env["Carver"]; pss = env["pss"]; XT = env["XT"]
    cm = env["cm"]; ident_bf = env["ident_bf"]; ident_f = env["ident_f"]; ident_r = env["ident_r"]
    w_in = env["w_in"]; rw_s = env["rw_s"]; g_s = env["g_s"]
    mu_p = env["mu_p"]; rwvec_p = env["rwvec_p"]; wup_aug = env["wup_aug"]; a_up = env["a_up"]
    g_up = env["g_up"]; gn_w = env["gn_w"]; gn_b = env["gn_b"]
    wv = w_in[l].rearrange("(kc p) c -> p kc c", p=128)
    c = Carver()
    c.off = off
    mu = c.take(16)[:, 0:15]
    omu = c.take(16)[:, 0:15]
    k.dma("sync", mu, mu_p[l], "mu")
    k.ts("vector", omu, mu, -1.0, 1.0, ALU.mult, ALU.add)
    wrw = [c.take(512, BF16, (8, 128)) for _ in range(2)]
    PA = [c.take(LP) for _ in range(2)]
    PB = [c.take(LP + 2) for _ in range(2)]
    sh = [c.take(LP) for _ in range(2)]
    for b_ in range(2):
        k.memset("vector", PB[b_][:, 0:1], 0.0)
    for m in range(15):
        nco = 128 if m < 14 else 32
        wb = wrw[m % 2]
        k.dma("gpsimd", wb[:, :, 0:nco], wv[:, :, m * 128:m * 128 + nco], "wrw%d" % (m % 2))
        pa = PA[m % 2]
        pb = PB[m % 2]
        for n in range(5):
            c0 = n * 512
            N = min(512, LP - c0)
            ps = pss[2 + (n % 2)]
            for kc in range(8):
                k.mm(ps[0:nco, 0:N], wb[:, kc, 0:nco], XT[:, kc, c0:c0 + N], start=kc == 0, stop=kc == 7)
            k.act(pa[0:nco, c0:c0 + N], ps[0:nco, 0:N], AF.Copy, scale=omu[0:nco, m:m + 1])
            k.act(pb[0:nco, 1 + c0:1 + c0 + N], ps[0:nco, 0:N], AF.Copy, scale=mu[0:nco, m:m + 1])
        so = sh[m % 2]
        k.tt("vector" if m % 2 else "gpsimd", so[0:nco, :], pa[0:nco, :], pb[0:nco, 0:LP], ALU.add)
        k.dma("sync", rw_s[m, 0:nco, :], so[0:nco, :], "sh%d" % (m % 2))


def rwkv(env, l):
    k = env["k"]; P = env["P"]; Carver = env["Carver"]; pss = env["pss"]; XT = env["XT"]
    cm = env["cm"]; ident_bf = env["ident_bf"]; ident_f = env["ident_f"]; ident_r = env["ident_r"]
    rw_s = env["rw_s"]; g_s = env["g_s"]
    rwvec_p = env["rwvec_p"]; wup_aug = env["wup_aug"]; a_up = env["a_up"]
    g_up = env["g_up"]; gn_w = env["gn_w"]; gn_b = env["gn_b"]
    P.phase = 'rw_lora'
    c = Carver()
    TW = c.take(LP, F32R)
    AD = c.take(LP, F32R)
    SG1 = c.take(LP // 2, BF16)
    SG2 = c.take(LP // 2, BF16)
    wup = c.take(512, F32R)
    aup = c.take(512, F32R)
    gup1 = c.take(256, BF16)
    gup2 = c.take(256, BF16)
    vec = c.take(16, F32, (4, 4))
    omka = c.take(4)
    hsel = c.take(2, F32R)
    hself = c.take(2)
    NEGE = -float(np.exp(-0.5))
    tric = c.take(256, F32, (2, 128))
    k.ts("vector", tric[:, 0, :], cm[:, 2, :], NEGE, None, ALU.mult)
    k.ts("vector", tric[:, 1, :], cm[:, 1, :], NEGE, None, ALU.mult)
    lt = c.take(LP)
    if RW_STAGE < 2:
        return
    k.dma("sync", vec, rwvec_p[l], "vec")
    k.ts("vector", omka, vec[:, :, 1], -1.0, 1.0, ALU.mult, ALU.add)
    k.copy("vector", hself[:, 0:1], cm[:, 7, 0:1])
    k.copy("vector", hself[:, 1:2], cm[:, 7, 127:128])
    k.copy("vector", hsel, hself)
    k.dma("sync", lt[0:128, :], rw_s[12], "lt")
    k.act(TW[0:64, :], lt[0:64, :], AF.Tanh)
    k.ts("vector", TW[64:128, :], lt[64:128, :], 0.0, 1.0, ALU.mult, ALU.add)
    k.copy("vector", AD, lt)
    lt2 = c.take(LP)
    k.dma("sync", lt2[0:128, :], rw_s[13], "lt2")
    k.act(SG1, lt2, AF.Sigmoid)
    lt3 = c.take(LP)
    k.dma("sync", lt3[0:32, :], rw_s[14, 0:32, :], "lt3")
    k.act(SG2[0:32, :], lt3[0:32, :], AF.Sigmoid)
    wtmp = c.take(512)
    k.dma("sync", wtmp[0:65, :], wup_aug[l], "wtmp")
    k.memset("vector", wup[64:128, :], 0.0)
    k.copy("vector", wup[0:65, :], wtmp[0:65, :])
    wtmp2 = c.take(512)
    k.dma("sync", wtmp2[64:128, :], a_up[l], "wtmp2")
    k.memset("vector", aup[0:64, :], 0.0)
    k.copy("vector", aup[64:128, :], wtmp2[64:128, :])
    k.dma("gpsimd", gup1, g_up[l, 0:128, :], "gup1")
    k.dma("gpsimd", gup2[0:32, :], g_up[l, 128:160, :], "gup2")
    gst = [wtmp, wtmp2]
    for i in range(NT):
        ps = pss[2 + (i % 2)]
        k.mm(ps[:], SG1[:, i * 128:(i + 1) * 128], gup1, start=True, stop=False)
        k.mm(ps[:], SG2[0:32, i * 128:(i + 1) * 128], gup2[0:32, :], start=False, stop=True)
        k.copy("scalar", gst[i % 2], ps[:])
        k.dma("sync", g_s[i * 128:(i + 1) * 128, :], gst[i % 2], "gst%d" % (i % 2))
    per_off = c.off - 3 * LP - 1024
    if RW_STAGE < 3:
        return
    RD = BF16

    def tk(n):
        return c.take(n // 2, RD)

    def run_interleaved(gens):
        active = list(gens)
        while active:
            for g in list(active):
                try:
                    next(g)
                except StopIteration:
                    active.remove(g)

    def interleave_gen(gens):
        active = list(gens)
        while active:
            for g in list(active):
                try:
                    next(g)
                    yield
                except StopIteration:
                    active.remove(g)

    P.phase = 'rw_pair'
    c = Carver()
    c.off = per_off
    pair_bufs = []
    for ps_i in range(2):
        B = {}
        B["RKV"] = c.take(384, F32, (3, 128))
        B["AS"] = c.take(128); B["SIGT"] = c.take(128); B["E13"] = c.take(256); B["E2"] = c.take(128)
        for nm in ("kk", "kk2", "rin", "kkn", "t1", "kp", "bb", "rk"):
            B[nm] = c.take(128)
        B["bT"] = tk(128); B["kT"] = tk(128)
        B["bTh"] = [tk(128) for _ in range(2)]
        B["kTh"] = [tk(128) for _ in range(2)]
        B["A0T"] = [tk(128) for _ in range(2)]
        B["Zb"] = [[tk(384), tk(384)] for _ in range(2)]
        B["sets"] = []
        for _ in range(2):
            B["sets"].append(dict(AR=tk(256), v_tok=tk(128), btok=[tk(128), tk(128)], ktok=[tk(128), tk(128)],
                                  GM=c.take(4), AB=[tk(256), tk(256)], AK=[tk(256), tk(256)], T=[tk(128), tk(128)],
                                  vtf=c.take(128), coef=c.take(2), y=c.take(128), g=c.take(128)))
        B["hd"] = [dict(W=tk(64), U=tk(64), S16=tk(64), S32=c.take(64), S32g=c.take(64)) for _ in range(2)]
        B["gnw"] = c.take(128); B["gnb"] = c.take(128)
        B["sq"] = c.take(128); B["yn"] = c.take(128); B["yb"] = tk(128)
        B["st"] = c.take(16)
        B["banks"] = (pss[3 * ps_i], pss[3 * ps_i + 1], pss[3 * ps_i + 2])
        pair_bufs.append(B)

    def pair_gen(hp):
        B = pair_bufs[hp % 2]
        tag = "_%d" % (hp % 2)
        RKV = B["RKV"]; AS = B["AS"]; SIGT = B["SIGT"]; E13 = B["E13"]; E2 = B["E2"]
        kk = B["kk"]; kk2 = B["kk2"]; rin = B["rin"]; kkn = B["kkn"]; t1 = B["t1"]; kp = B["kp"]; bb = B["bb"]; rk = B["rk"]
        bT = B["bT"]; kT = B["kT"]; bTh = B["bTh"]; kTh = B["kTh"]; A0T = B["A0T"]; Zb = B["Zb"]
        sets = B["sets"]; hd_b = B["hd"]; gnw = B["gnw"]; gnb = B["gnb"]
        X0, X1, XS = B["banks"]
        pA = X0[:, 0:256]; pC = X0[:, 256:512]
        pT = X1
        pM = X0; pM2 = X1
        pIs = [X0, X1]
        pSs = [XS[:, 0:256], XS[:, 256:512]]
        for d in hd_b:
            for nm in ("W", "U", "S16", "S32"):
                k.memset("vector", d[nm], 0.0)
        k.dma("sync", gnw, gn_w[l:l + 1, hp * 128:(hp + 1) * 128].partition_broadcast(128), "gnw" + tag)
        k.dma("sync", gnb, gn_b[l:l + 1, hp * 128:(hp + 1) * 128].partition_broadcast(128), "gnb" + tag)
        kk_s = vec[:, hp, 0:1]; ka_s_ = vec[:, hp, 1:2]; a0_s = vec[:, hp, 2:3]; rk_s = vec[:, hp, 3:4]
        yield

        def prep(i):
            S_ = sets[i % 2]
            AR = S_["AR"]; v_tok = S_["v_tok"]; btok = S_["btok"]; ktok = S_["ktok"]; GM = S_["GM"]
            t0 = i * 128
            for q in range(3):
                k.dma("sync", RKV[:, q, :], rw_s[4 * q + hp, :, t0:t0 + 128], "rkv%d" % q + tag)
            k.dma("sync", S_["g"], g_s[t0:t0 + 128, hp * 128:(hp + 1) * 128], "gt%d" % (i % 2) + tag)
            r_ = RKV[:, 0, :]; k_ = RKV[:, 1, :]; v_ = RKV[:, 2, :]
            k.mm(pA[:, 128:256], aup[:, hp * 128:(hp + 1) * 128], AD[:, t0:t0 + 128])
            k.mm(pA[:, 0:128], TW[:, t0:t0 + 128], wup[:, hp * 128:(hp + 1) * 128])
            k.act(AS, pA[:, 128:256], AF.Sigmoid, bias=a0_s)
            k.act(SIGT, pA[:, 0:128], AF.Sigmoid)
            yield
            k.mm(pC[:, 0:128], SIGT, tric[:, 0, :])
            k.mm(pC[:, 128:256], SIGT, tric[:, 1, :])
            k.act(E13, pC[:, 0:256], AF.Exp)
            k.act(E2, pC[:, 0:128], AF.Exp, scale=-1.0)
            yield
            E1 = E13[:, 0:128]; E3 = E13[:, 128:256]
            k.ts("vector", kk, k_, kk_s, None, ALU.mult)
            k.tt("gpsimd", kk2, kk, kk, ALU.mult)
            k.mm(pT[:, 384:512], cm[:, 7, :], kk2)
            k.act(rin, pT[:, 384:512], AF.Sqrt)
            yield
            k.ts("vector", rin, rin, 1e-12, None, ALU.max)
            k.recip(rin, rin)
            k.tt("vector", kkn, kk, rin, ALU.mult)
            k.ts("vector", t1, AS, ka_s_, omka[:, hp:hp + 1], ALU.mult, ALU.add)
            yield
            k.tt("gpsimd", kp, k_, t1, ALU.mult)
            k.tt("gpsimd", bb, kkn, AS, ALU.mult)
            k.tt("vector", AR[:, 128:256], r_, E1, ALU.mult)
            k.tt("gpsimd", t1, kkn, E3, ALU.mult)
            yield
            k.ts("vector", AR[:, 0:128], t1, -1.0, None, ALU.mult)
            k.tt("vector", bT, bb, E2, ALU.mult)
            k.tt("vector", kT, kp, E2, ALU.mult)
            k.tt("gpsimd", rk, r_, kp, ALU.mult)
            yield
            k.ts("vector", rk, rk, rk_s, None, ALU.mult)
            for hd in range(2):
                for cc in range(2):
                    k.ts("vector", GM[:, hd * 2 + cc:hd * 2 + cc + 1], hself[:, hd:hd + 1],
                         E13[:, 63 + 64 * cc:64 + 64 * cc], None, ALU.mult)
            yield
            k.mm(pT[:, 0:128], v_, ident_f[:])
            k.mm(pT[:, 128:256], bT, ident_bf[:])
            k.mm(pT[:, 256:384], kT, ident_bf[:])
            k.mm(pT[:, 384:386], rk, hself)
            k.copy("scalar", v_tok, pT[:, 0:128])
            k.copy("scalar", S_["vtf"], pT[:, 0:128])
            yield
            k.copy("scalar", S_["coef"], pT[:, 384:386])
            for cc in range(2):
                k.act(btok[cc], pT[:, 128:256], AF.Copy, scale=hself[:, cc:cc + 1])
                k.act(ktok[cc], pT[:, 256:384], AF.Copy, scale=hself[:, cc:cc + 1])
            yield
            for hd in range(2):
                k.ts("vector", bTh[hd], bT, hself[:, hd:hd + 1], None, ALU.mult)
                k.ts("vector", kTh[hd], kT, hself[:, hd:hd + 1], None, ALU.mult)
                k.mm(pM[:, 0:256], bTh[hd], AR)
                k.mm(pM[:, 256:384], AR[:, 0:128], bTh[hd])
                k.mm(pM2[:, 0:256], kTh[hd], AR)
                yield
                k.tt("vector", S_["AB"][hd], pM[:, 0:256], cm[:, 1:3, :].rearrange("p a b -> p (a b)"), ALU.mult)
                k.tt("vector", A0T[hd], pM[:, 256:384], cm[:, 3, :], ALU.mult)
                k.tt("vector", S_["AK"][hd], pM2[:, 0:256], cm[:, 1:3, :].rearrange("p a b -> p (a b)"), ALU.mult)
                yield
            Zc = []
            for hd in range(2):
                Y = S_["AB"][hd][:, 0:128]; YT = A0T[hd]
                Z = Zb[hd][0]
                pI = pIs[hd]
                k.mm(pI[:, 128:256], YT, Y)
                k.mm(pI[:, 256:384], Y, YT)
                k.tt("gpsimd", Z[:, 0:128], Y, ident_bf[:], ALU.add)
                k.copy("scalar", Z[:, 128:384], pI[:, 128:384])
                Zc.append(Z)
            yield
            for lev in range(1, 5):
                for hd in range(2):
                    Z = Zc[hd]
                    Zn = Zb[hd][lev % 2]
                    pI = pIs[hd]
                    k.mm(pI[:, 0:256], Z[:, 256:384], Z[:, 0:256])
                    k.mm(pI[:, 256:384], Z[:, 128:256], Z[:, 256:384])
                    k.tt("vector", Zn[:, 0:128], pI[:, 0:128], Z[:, 0:128], ALU.add)
                    k.copy("scalar", Zn[:, 128:384], pI[:, 128:384])
                    Zc[hd] = Zn
                    yield
            for hd in range(2):
                Z = Zc[hd]
                pI = pIs[hd]
                k.mm(pI[:, 0:128], Z[:, 256:384], Z[:, 0:128])
                k.tt("vector", S_["T"][hd], pI[:, 0:128], Z[:, 0:128], ALU.add)
            yield

        def seq(i):
            S_ = sets[i % 2]
            AR = S_["AR"]; v_tok = S_["v_tok"]; btok = S_["btok"]; ktok = S_["ktok"]; GM = S_["GM"]
            y_t = S_["y"]
            t0 = i * 128
            for cc in range(2):
                rs = slice(cc * 64, cc * 64 + 64)
                for hd in range(2):
                    d = hd_b[hd]
                    vh = v_tok[:, hd * 64:(hd + 1) * 64]
                    gm = GM[:, hd * 2 + cc:hd * 2 + cc + 1]
                    pS = pSs[hd]
                    k.mm(pS[:, 0:64], AR[:, 0:128], d["S16"], start=True, stop=False)
                    k.mm(pS[:, 0:64], S_["AK"][hd][:, 0:128], vh, start=False, stop=True)
                    k.act(d["S32g"], d["S32"], AF.Copy, scale=gm)
                    k.copy("scalar", d["W"][rs, :], pS[rs, 0:64])
                    yield
                for hd in range(2):
                    d = hd_b[hd]
                    pS = pSs[hd]
                    k.mm(pS[:, 64:128], S_["T"][hd], d["W"])
                    k.copy("scalar", d["U"][rs, :], pS[rs, 64:128])
                    yield
                for hd in range(2):
                    d = hd_b[hd]
                    vh = v_tok[:, hd * 64:(hd + 1) * 64]
                    gm = GM[:, hd * 2 + cc:hd * 2 + cc + 1]
                    pS = pSs[hd]
                    k.mm(pS[:, 192:256], btok[cc], d["U"], start=True, stop=False)
                    k.mm(pS[:, 192:256], ktok[cc], vh, start=False, stop=True)
                    k.mm(pS[:, 128:192], AR[:, 128:256], d["S16"], start=True, stop=False)
                    k.mm(pS[:, 128:192], S_["AB"][hd][:, 128:256], d["U"], start=False, stop=False)
                    k.mm(pS[:, 128:192], S_["AK"][hd][:, 128:256], vh, start=False, stop=True)
                    k.stt(d["S32"], pS[:, 192:256], gm, d["S32g"], ALU.mult, ALU.add)
                    k.copy("vector", d["S16"], d["S32"])
                    k.copy("scalar", y_t[rs, hd * 64:hd * 64 + 64], pS[rs, 128:192])
                    yield
            st = B["st"]; sq = B["sq"]; yn = B["yn"]; yb = B["yb"]
            s1 = st[:, 0:2]; s2 = st[:, 2:4]; mn = st[:, 4:6]; m2 = st[:, 6:8]
            y3 = y_t.rearrange("p (a d) -> p a d", d=64)
            sq3 = sq.rearrange("p (a d) -> p a d", d=64)
            yn3 = yn.rearrange("p (a d) -> p a d", d=64)
            vt3 = S_["vtf"].rearrange("p (a d) -> p a d", d=64)
            k.reduce(s1, y3)
            k.tt("gpsimd", sq, y_t, y_t, ALU.mult)
            k.reduce(s2, sq3)
            k.ts("vector", mn, s1, 1.0 / 64, None, ALU.mult)
            yield
            k.tt("vector", m2, mn, mn, ALU.mult)
            k.ts("vector", s2, s2, 1.0 / 64, None, ALU.mult)
            k.tt("vector", s2, s2, m2, ALU.subtract)
            k.ts("vector", s2, s2, GN_EPS, None, ALU.add)
            k.act(s2, s2, AF.Sqrt)
            k.recip(s2, s2)
            yield
            k.tt("vector", yn3, y3, mn.unsqueeze(2).to_broadcast([128, 2, 64]), ALU.subtract)
            k.tt("vector", yn3, yn3, s2.unsqueeze(2).to_broadcast([128, 2, 64]), ALU.mult)
            k.tt("gpsimd", sq3, vt3, S_["coef"].unsqueeze(2).to_broadcast([128, 2, 64]), ALU.mult)
            k.tt("vector", yn, yn, gnw, ALU.mult)
            yield
            k.tt("vector", yn, yn, gnb, ALU.add)
            k.tt("vector", yn, yn, sq, ALU.add)
            k.tt("vector", yb, yn, S_["g"], ALU.mult)
            pt = X1[:].bitcast(BF16)[:, 768:896]
            k.tr(pt, yb, ident_bf[:])
            k.copy("scalar", XT[:, hp, t0:t0 + 128], pt)
            yield

        yield from prep(0)
        for i in range(NT):
            gens = [seq(i)]
            if i + 1 < NT:
                gens.append(prep(i + 1))
            yield from interleave_gen(gens)

    for hp0 in (0, 2):
        for _ in interleave_gen([pair_gen(hp0), pair_gen(hp0 + 1)]):
            pass


def build_program(nlayers=DEPTH, dbg=None, stages=("mix", "ffn")):
    nc = bass.Bass("TRN2", target_bir_lowering=False)
    dram = {}

    def din(name, shape, dt=F32):
        dram[name] = nc.dram_tensor(name, list(shape), dt, kind="ExternalInput").ap()
        return dram[name]

    def dscr(name, shape, dt=F32):
        kind = "ExternalOutput" if (dbg and name in dbg) else "Internal"
        dram[name] = nc.dram_tensor(name, list(shape), dt, kind=kind).ap()
        return dram[name]

    x = din("x", [SEQ, D])
    meta = din("meta", [NMETA, D])
    norm1_g = din("norm1_g", [DEPTH, D])
    norm2_g = din("norm2_g", [DEPTH, D])
    w_in = din("w_in", [DEPTH, D, RWC + FXC])
    w_o = din("w_o", [DEPTH, D, D])
    ffn_w_in = din("ffn_w_in", [DEPTH, D, 2 * DFF])
    ffn_w_out = din("ffn_w_out", [DEPTH, DFF, D])
    conv_wp = din("conv_wp", [DEPTH, 128, 44, 3])
    conv_bp = din("conv_bp", [DEPTH, 128, 44])
    mu_p = din("mu_p", [DEPTH, 128, 15])
    rwvec_p = din("rwvec_p", [DEPTH, 128, 4, 4])
    wup_aug = din("wup_aug", [DEPTH, 65, 512])
    a_up = din("a_up", [DEPTH, 64, 512])
    g_up = din("g_up", [DEPTH, 160, 512])
    gn_w = din("gn_w", [DEPTH, 512])
    gn_b = din("gn_b", [DEPTH, 512])
    fx_bf = din("fx_bf", [DEPTH, 8])
    fx_qg = din("fx_qg", [DEPTH, 64])
    fx_kg = din("fx_kg", [DEPTH, 64])
    cmask = din("cmask", [128, 8, 128])
    out = nc.dram_tensor("out", [SEQ, D], F32, kind="ExternalOutput").ap()

    rw_s = dscr("rw_s", [15, 128, LP])
    g_s = dscr("g_s", [LP, 512])
    qa_s = dscr("qa_s", [NB, 66, LP], BF16)
    ka_s = dscr("ka_s", [NB, 66, LP], BF16)
    sgo_s = dscr("sgo_s", [LP, 512], BF16)
    hdbg = dscr("hdbg", [LP, D]) if dbg else None
    mixdbg = dscr("mixdbg", [128, 8, LP], BF16) if dbg else None

    with contextlib.ExitStack() as st:
        def sb(name, shape, dt=F32):
            return st.enter_context(nc.sbuf_tensor(name, list(shape), dt))

        h = sb("h", [128, NT, D])
        XT = sb("XT", [128, 8, LP], BF16)
        ident_bf = sb("ident_bf", [128, 128], BF16)
        ident_f = sb("ident_f", [128, 128])
        ident_r = ident_f
        cm = sb("cm", [128, 8, 128])
        small = sb("small", [128, 256])
        ARENA_W = 24200
        arena = sb("arena", [128, ARENA_W])
        pss = [st.enter_context(nc.psum_tensor("ps%d" % i, [128, 512], F32)) for i in range(8)]

        P = Prog(nc)
        k = K(P)

        class Carver:
            def __init__(self):
                self.off = 0

            def take(self, nwords, dt=F32, shape=None):
                a = arena[:, self.off:self.off + nwords]
                self.off += nwords
                assert self.off <= ARENA_W, self.off
                if dt != F32:
                    a = a.bitcast(dt)
                if shape is not None:
                    names = " ".join("d%d" % i for i in range(len(shape)))
                    kw = {"d%d" % i: s for i, s in enumerate(shape)}
                    a = a.rearrange("p (%s) -> p %s" % (names, names), **kw)
                return a

        k.dma("sync", cm[:], cmask, "cm")
        k.copy("vector", ident_f[:], cm[:, 0, :])
        k.copy("vector", ident_bf[:], cm[:, 0, :])
        negm_bf = sb("negm_bf", [128, 128], BF16)
        k.copy("vector", negm_bf[:], cm[:, 4, :])

        ones_bf = sb("ones_bf", [8, LP], BF16)
        k.memset("vector", ones_bf[:], 1.0)
        k.dma("sync", ka_s[:, 64, :], ones_bf[:], "kones")
        k.dma("sync", ka_s[:, 65, :], ones_bf[:], "kones")
        k.memset("vector", h[:, NT - 1, :], 0.0)
        k.dma("sync", h[0:16, 0, :], meta, "hload")
        k.dma("sync", h[16:128, 0, :], x[0:112, :], "hload")
        k.dma("sync", h[:, 1:16, :], x[112:112 + 1920, :].rearrange("(t p) d -> p t d", p=128), "hload")
        k.dma("sync", h[0:16, 16, :], x[2032:2048, :], "hload")

        ssq = small[:, 0:17]
        rstd = small[:, 32:49]
        tmp17 = small[:, 64:81]

        def norm_T(g_row):
            c = Carver()
            gbc = c.take(D)
            k.dma("sync", gbc, g_row.partition_broadcast(128), "gbc")
            junk = c.take(512, BF16)
            xn = [c.take(512, BF16), c.take(512, BF16)]
            for i in range(NT):
                k.act(junk, h[:, i, :], AF.Square, accum=ssq[:, i:i + 1])
            k.ts("vector", tmp17, ssq, 1.0 / D, EPS, ALU.mult, ALU.add)
            k.act(tmp17, tmp17, AF.Sqrt)
            k.recip(rstd, tmp17)
            for i in range(NT):
                xb = xn[i % 2]
                k.stt(xb, h[:, i, :], rstd[:, i:i + 1], gbc, ALU.mult, ALU.mult)
                pt = pss[i % 2][:].bitcast(BF16).rearrange("p (a b) -> p a b", b=128)[:, 0:8, :]
                for kc in range(8):
                    k.tr(pt[:, kc, :], xb[:, kc * 128:(kc + 1) * 128], ident_bf[:])
                k.copy("scalar" if i % 2 else "vector", XT[:, :, i * 128:(i + 1) * 128], pt)

        def add_to_h(i, n2, ps):
            k.tt("vector", h[:, i, n2 * 512:(n2 + 1) * 512], ps, h[:, i, n2 * 512:(n2 + 1) * 512], ALU.add)

        def ffn(l):
            P.phase = 'norm2'
            norm_T(norm2_g[l:l + 1, :])
            P.phase = 'ffn'
            c = Carver()
            cw = c.take(44 * 3, F32, (44, 3))
            cb = c.take(44)
            k.dma("sync", cw, conv_wp[l], "cw")
            k.dma("sync", cb, conv_bp[l], "cb")
            groups = [list(range(0, 5)), list(range(5, 10)), list(range(10, 14)), list(range(14, 18)), list(range(18, 22))]
            hid = c.take(5 * LP // 2, BF16, (5, LP))
            wout = c.take(5 * D // 2, BF16, (5, D))
            wu = [c.take(512, BF16, (8, 128)) for _ in range(2)]
            wg = [c.take(512, BF16, (8, 128)) for _ in range(2)]
            NBUF = 3
            HU = [c.take(516) for _ in range(NBUF)]
            HG = [c.take(516) for _ in range(NBUF)]
            t0s = [c.take(512) for _ in range(NBUF)]
            t1s = [c.take(512) for _ in range(NBUF)]
            txs = [c.take(512) for _ in range(2)]
            t2s = [c.take(512) for _ in range(2)]
            wi = ffn_w_in[l].rearrange("(kc p) c -> p kc c", p=128)
            wo = ffn_w_out[l].rearrange("(kc p) n -> p kc n", p=128)
            cnt = 0
            tcnt = 0
            for gi, js in enumerate(groups):
                k.dma("gpsimd", wout[:, 0:len(js), :], wo[:, js[0]:js[0] + len(js), :], "wout")
                for jj, j in enumerate(js):
                    wub = wu[cnt % 2]
                    wgb = wg[cnt % 2]
                    k.dma("gpsimd", wub, wi[:, :, j * 128:(j + 1) * 128], "wu%d" % (cnt % 2))
                    k.dma("gpsimd", wgb, wi[:, :, DFF + j * 128:DFF + (j + 1) * 128], "wg%d" % (cnt % 2))
                    cnt += 1
                    ju = j
                    jg = 22 + j
                    for n in range(5):
                        c0 = n * 512
                        N = min(512, LP - c0)
                        pu = pss[2 + (n % 2)]
                        pg = pss[4 + (n % 2)]
                        b = tcnt % NBUF
                        bn = (tcnt + 1) % NBUF
                        hu = HU[b]; hg = HG[b]; t0 = t0s[b]; t1 = t1s[b]
                        tx = txs[tcnt % 2]; t2 = t2s[tcnt % 2]
                        tcnt += 1
                        for kc in range(8):
                            k.mm(pu[:, 0:N], wub[:, kc, :], XT[:, kc, c0:c0 + N], start=kc == 0, stop=kc == 7)
                        for kc in range(8):
                            k.mm(pg[:, 0:N], wgb[:, kc, :], XT[:, kc, c0:c0 + N], start=kc == 0, stop=kc == 7)
                        if n == 0:
                            k.memset("gpsimd", hu[:, 0:2], 0.0)
                            k.memset("gpsimd", hg[:, 0:2], 0.0)
                        k.copy("scalar", hu[:, 2:2 + N], pu[:, 0:N])
                        k.act(t0[:, 0:N], pu[:, 0:N], AF.Identity, bias=cb[:, ju:ju + 1], scale=cw[:, ju, 2:3])
                        k.copy("scalar", hg[:, 2:2 + N], pg[:, 0:N])
                        k.act(t1[:, 0:N], pg[:, 0:N], AF.Identity, bias=cb[:, jg:jg + 1], scale=cw[:, jg, 2:3])
                        if n < 4:
                            k.copy("gpsimd", HU[bn][:, 0:2], hu[:, N:N + 2])
                            k.copy("gpsimd", HG[bn][:, 0:2], hg[:, N:N + 2])
                        k.ts("vector", tx[:, 0:N], hu[:, 1:1 + N], cw[:, ju, 1:2], None, ALU.mult)
                        k.ts("vector", t2[:, 0:N], hg[:, 1:1 + N], cw[:, jg, 1:2], None, ALU.mult)
                        k.tt("vector", t0[:, 0:N], t0[:, 0:N], tx[:, 0:N], ALU.add)
                        k.tt("gpsimd", t1[:, 0:N], t1[:, 0:N], t2[:, 0:N], ALU.add)
                        k.ts("vector", tx[:, 0:N], hu[:, 0:N], cw[:, ju, 0:1], None, ALU.mult)
                        k.ts("vector", t2[:, 0:N], hg[:, 0:N], cw[:, jg, 0:1], None, ALU.mult)
                        k.tt("vector", t0[:, 0:N], t0[:, 0:N], tx[:, 0:N], ALU.add)
                        k.tt("gpsimd", t1[:, 0:N], t1[:, 0:N], t2[:, 0:N], ALU.add)
                        k.act(t1[:, 0:N], t1[:, 0:N], AF.Silu)
                        k.tt("vector", hid[:, jj, c0:c0 + N], t1[:, 0:N], t0[:, 0:N], ALU.mult)
                for i in range(NT):
                    for n2 in range(2):
                        ps = pss[6 + ((i * 2 + n2) % 2)]
                        for jj in range(len(js)):
                            k.mm(ps[:], hid[:, jj, i * 128:(i + 1) * 128], wout[:, jj, n2 * 512:(n2 + 1) * 512],
                                 start=jj == 0, stop=jj == len(js) - 1)
                        add_to_h(i, n2, ps[:])

        def out_proj(l):
            P.phase = 'out_proj'
            c = Carver()
            wob = c.take(8 * D // 2, BF16, (8, D))
            k.dma("gpsimd", wob, w_o[l].rearrange("(kc p) n -> p kc n", p=128), "wob")
            for i in range(NT):
                for n2 in range(2):
                    ps = pss[6 + ((i * 2 + n2) % 2)]
                    for kc in range(8):
                        k.mm(ps[:], XT[:, kc, i * 128:(i + 1) * 128], wob[:, kc, n2 * 512:(n2 + 1) * 512],
                             start=kc == 0, stop=kc == 7)
                    add_to_h(i, n2, ps[:])

        env = dict(locals())
        env["rwkv_inproj"] = rwkv_inproj
        for l in range(nlayers):
            if "mix" in stages:
                mixer(env, l)
                if RWKV_ENABLED:
                    rwkv(env, l)
                else:
                    k.memset("gpsimd", XT[:, 0:4, :], 0.0)
                if dbg and "mixdbg" in dbg and l == 0:
                    k.dma("sync", mixdbg, XT[:], "mixdbg")
                out_proj(l)
            if "ffn" in stages:
                ffn(l)

        if dbg and "hdbg" in dbg:
            k.dma("sync", hdbg.rearrange("(t p) d -> p t d", p=128), h[:], "hdbg")
        k.dma("sync", out[0:112, :], h[16:128, 0, :], "ost")
        k.dma("sync", out[112:112 + 1920, :].rearrange("(t p) d -> p t d", p=128), h[:, 1:16, :], "ost")
        k.dma("sync", out[2032:2048, :], h[0:16, 16, :], "ost")
        fw = ["ost"] + (["hdbg"] if dbg and "hdbg" in dbg else []) + (["mixdbg"] if dbg and "mixdbg" in dbg else [])
        P.emit(final_waits=fw)
    return nc, P


def _masks():
    m = np.zeros((128, 8, 128), np.float32)
    j = np.arange(128)[:, None]
    t = np.arange(128)[None, :]
    same = (j // 64) == (t // 64)
    m[:, 0, :] = (j == t)
    m[:, 1, :] = same & (j < t)
    m[:, 2, :] = same & (j <= t)
    m[:, 3, :] = same & (j > t)
    m[:, 4, :] = np.where(j <= t, 0.0, -30000.0)
    m[:, 5, :] = 1.0
    m[:, 6, :] = (j <= t)
    m[:, 7, :] = same
    return m


def prep_shared(inp):
    f = lambda a: np.ascontiguousarray(np.asarray(a, dtype=np.float32))
    sh = {}
    for kk_ in ("meta", "norm1_g", "norm2_g", "w_in", "w_o", "ffn_w_in", "ffn_w_out"):
        sh[kk_] = f(inp[kk_])
    cw = f(inp["ffn_conv_w"])
    sh["conv_wp"] = f(cw.reshape(DEPTH, 3, 44, 128).transpose(0, 3, 2, 1))
    sh["conv_bp"] = f(f(inp["ffn_conv_b"]).reshape(DEPTH, 44, 128).transpose(0, 2, 1))
    mu = np.zeros((DEPTH, 15 * 128), np.float32)
    mu[:, :RWC] = f(inp["rw_mu"])
    sh["mu_p"] = f(mu.reshape(DEPTH, 15, 128).transpose(0, 2, 1))
    vecs = np.stack([f(inp["rw_k_k"]), f(inp["rw_k_a"]), f(inp["rw_a0"]), f(inp["rw_r_k"]).reshape(DEPTH, 512)], -1)
    sh["rwvec_p"] = f(vecs.reshape(DEPTH, 4, 128, 4).transpose(0, 2, 1, 3))
    sh["wup_aug"] = f(np.concatenate([f(inp["rw_w_up"]), f(inp["rw_w0"])[:, None, :]], 1))
    sh["a_up"] = f(inp["rw_a_up"])
    sh["g_up"] = f(inp["rw_g_up"])
    sh["gn_w"] = f(inp["rw_gn_w"])
    sh["gn_b"] = f(inp["rw_gn_b"])
    sh["fx_bf"] = f(inp["fx_b_f"])
    sh["fx_qg"] = f(inp["fx_q_g"])
    sh["fx_kg"] = f(inp["fx_k_g"])
    sh["cmask"] = _masks()
    return sh


def kernel(**inputs):
    nc, P = build_program()
    sh = prep_shared(inputs)
    x = np.asarray(inputs["x"], dtype=np.float32)
    in_maps = []
    for b in range(NB):
        m = dict(sh)
        m["x"] = np.ascontiguousarray(x[b])
        in_maps.append(m)
    res = run_bass_kernel_spmd(nc, in_maps, core_ids=list(range(NB)))
    return np.stack([np.asarray(r["out"], dtype=np.float32) for r in res.results], 0)
```

```python
import numpy as np
import concourse.bass as bass
import concourse.mybir as mybir

F32 = mybir.dt.float32
BF16 = mybir.dt.bfloat16
AF = mybir.ActivationFunctionType
ALU = mybir.AluOpType
AX = mybir.AxisListType

_DT_SIZE = {F32: 4, BF16: 2, mybir.dt.float32r: 4}


def _dsize(dt):
    try:
        return _DT_SIZE[dt]
    except KeyError:
        return mybir.dt.size(dt)


def ap_region(ap):
    name = ap.name
    pat = ap.ap
    off = int(ap.offset)
    es = _dsize(ap.dtype)
    sp = str(ap.space)
    if sp == "DRAM":
        lo = off
        hi = off
        for st, cnt in pat:
            if cnt > 1:
                if st >= 0:
                    hi += st * (cnt - 1)
                else:
                    lo += st * (cnt - 1)
        return (name, 0, 1, lo * es, (hi + 1) * es)
    if sp == "PSUM":
        return (name, 0, 128, 0, 2048)
    pstep, pcnt = pat[0]
    if pstep > 0:
        rowlen = pstep
    else:
        rowlen = int(np.prod(ap.tensor.shape[1:])) * _dsize(ap.tensor.dtype) // es
    p0 = off // rowlen
    f0 = off % rowlen
    lo = f0
    hi = f0
    for st, cnt in pat[1:]:
        if cnt > 1:
            if st >= 0:
                hi += st * (cnt - 1)
            else:
                lo += st * (cnt - 1)
    np_ = pcnt if pstep != 0 else 1
    return (name, p0, p0 + np_, lo * es, (hi + 1) * es)


class Op:
    __slots__ = ("eng", "fn", "idx", "cnt_eng", "cnt", "waits", "inc", "clock", "dma_key", "phase")


class Prog:
    ENGS = ("tensor", "vector", "scalar", "gpsimd", "sync")
    CAP = 6000

    def __init__(self, nc):
        self.nc = nc
        self.ops = []
        self.streams = {e: [] for e in self.ENGS}
        self.acc = {}
        self.counts = {}
        self.seen = {e: {} for e in self.ENGS}
        self.op_by = {}

    def add(self, eng, fn, reads=(), writes=(), dma_key=None):
        op = Op()
        op.eng = eng
        op.fn = fn
        op.idx = len(self.ops)
        op.dma_key = dma_key
        op.cnt_eng = ("dma", dma_key) if dma_key is not None else eng
        op.inc = False
        op.phase = getattr(self, 'phase', '')
        need = {}
        rregs = [ap_region(a) for a in reads]
        wregs = [ap_region(a) for a in writes]
        for regs, is_w in ((rregs, False), (wregs, True)):
            for (name, p0, p1, b0, b1) in regs:
                for rec in self.acc.get(name, ()):
                    if rec[1] <= p0 or rec[0] >= p1 or rec[3] <= b0 or rec[2] >= b1:
                        continue
                    if not (is_w or rec[4]):
                        if not (name.startswith("ps") and rec[5] != op.cnt_eng):
                            continue
                    ce, c = rec[5], rec[6]
                    if isinstance(ce, tuple):
                        c = self.counts[ce]
                    if ce == eng and dma_key is None:
                        if eng == "tensor":
                            continue
                    if need.get(ce, 0) < c:
                        need[ce] = c
        seen = self.seen[eng]
        waits = []
        for ce, c in need.items():
            if seen.get(ce, 0) >= c:
                continue
            waits.append((ce, c))
            src = self.op_by[(ce, c)]
            src.inc = True
            for k, v in src.clock.items():
                if seen.get(k, 0) < v:
                    seen[k] = v
            seen[ce] = c
        op.waits = waits
        cnt = self.counts.get(op.cnt_eng, 0) + 1
        self.counts[op.cnt_eng] = cnt
        op.cnt = cnt
        op.clock = dict(seen)
        self.op_by[(op.cnt_eng, cnt)] = op
        for regs, is_w in ((rregs, False), (wregs, True)):
            for (name, p0, p1, b0, b1) in regs:
                lst = self.acc.setdefault(name, [])
                new = []
                for rec in lst:
                    covered = rec[0] >= p0 and rec[1] <= p1 and rec[2] >= b0 and rec[3] <= b1
                    if covered and (is_w or (not rec[4] and rec[5] == op.cnt_eng)):
                        continue
                    new.append(rec)
                new.append((p0, p1, b0, b1, is_w, op.cnt_eng, cnt, op.idx))
                self.acc[name] = new
        self.ops.append(op)
        self.streams[eng].append(op)
        return op

    def pe(self, fn, reads, writes):
        return self.add("tensor", fn, reads, writes)

    def dve(self, fn, reads, writes):
        return self.add("vector", fn, reads, writes)

    def act(self, fn, reads, writes):
        return self.add("scalar", fn, reads, writes)

    def pool(self, fn, reads, writes):
        return self.add("gpsimd", fn, reads, writes)

    def dma(self, queue, out, in_, key, **kw):
        return self.add(queue, lambda e: e.dma_start(out=out, in_=in_, **kw), [in_], [out], dma_key=key)

    def emit(self, final_waits=()):
        nc = self.nc
        incs = {}
        for op in self.ops:
            if op.dma_key is not None or op.inc:
                incs.setdefault(op.cnt_eng, []).append(op.cnt)
        import contextlib
        with contextlib.ExitStack() as st:
            semtab = {}
            valmap = {}
            nsem = 0
            for ce, lst in incs.items():
                is_dma = isinstance(ce, tuple)
                cap = 10 ** 9 if is_dma else self.CAP
                step = 16 if is_dma else 1
                sems = []
                for i, c in enumerate(lst):
                    si = i // cap
                    if si >= len(sems):
                        sems.append(st.enter_context(nc.semaphore("s%d" % nsem)))
                        nsem += 1
                    valmap[(ce, c)] = (sems[si], (i % cap + 1) * step)
                semtab[ce] = sems
            self.nsem = nsem
            block = st.enter_context(nc.Block())

            def make(engname):
                ops = self.streams[engname]

                def body(e):
                    for op in ops:
                        for w in op.waits:
                            s, v = valmap[w]
                            e.wait_ge(s, v)
                        ins = op.fn(e)
                        if op.dma_key is not None:
                            s, v = valmap[(op.cnt_eng, op.cnt)]
                            ins.then_inc(s, 16)
                        elif op.inc:
                            s, v = valmap[(op.cnt_eng, op.cnt)]
                            ins.then_inc(s, 1)
                    if engname == "sync":
                        for key in final_waits:
                            ce = ("dma", key)
                            c = self.counts[ce]
                            s, v = valmap[(ce, c)]
                            e.wait_ge(s, v)
                return body

            for en in self.ENGS:
                getattr(block, en)(make(en))

import contextlib
from concourse.bass_utils import run_bass_kernel_spmd

F32R = F32
D = 1024
SEQ = 2048
NMETA = 16
LP = 2176
NT = 17
DEPTH = 4
RWC = 1824
FXC = 2056
DFF = 2816
EPS = 1e-6
GN_EPS = 64e-5
NB = 8
RWKV_ENABLED = True
import os
RW_STAGE = int(os.environ.get('RW_STAGE', '9'))


class K:
    def __init__(self, P):
        self.P = P

    def mm(self, out, lhsT, rhs, start=True, stop=True):
        self.P.pe(lambda e: e.matmul(out, lhsT, rhs, start=start, stop=stop), [lhsT, rhs], [out])

    def tr(self, out, in_, ident):
        self.P.pe(lambda e: e.transpose(out, in_, ident), [in_, ident], [out])

    def act(self, out, in_, func, bias=None, scale=None, accum=None):
        kw = {}
        reads = [in_]
        writes = [out]
        if bias is not None:
            kw["bias"] = bias
            if not isinstance(bias, (int, float)):
                reads.append(bias)
        if scale is not None:
            kw["scale"] = scale
            if not isinstance(scale, (int, float)):
                reads.append(scale)
        if accum is not None:
            kw["accum_out"] = accum
            writes.append(accum)
        self.P.act(lambda e: e.activation(out, in_, func, **kw), reads, writes)

    def tt(self, eng, out, in0, in1, op):
        self.P.add(eng, lambda e: e.tensor_tensor(out, in0, in1, op), [in0, in1], [out])

    def ts(self, eng, out, in0, s1, s2, op0, op1=None):
        reads = [in0]
        for s in (s1, s2):
            if s is not None and not isinstance(s, (int, float)):
                reads.append(s)
        if op1 is None:
            self.P.add(eng, lambda e: e.tensor_scalar(out, in0, s1, None, op0), reads, [out])
        else:
            self.P.add(eng, lambda e: e.tensor_scalar(out, in0, s1, s2, op0, op1), reads, [out])

    def stt(self, out, in0, scalar, in1, op0, op1):
        reads = [in0, in1]
        if not isinstance(scalar, (int, float)):
            reads.append(scalar)
        self.P.dve(lambda e: e.scalar_tensor_tensor(out, in0, scalar, in1, op0, op1), reads, [out])

    def copy(self, eng, out, in_):
        if eng == "scalar":
            self.P.act(lambda e: e.copy(out, in_), [in_], [out])
        else:
            self.P.add(eng, lambda e: e.tensor_copy(out, in_), [in_], [out])

    def memset(self, eng, ap, val):
        self.P.add(eng, lambda e: e.memset(ap, val), [], [ap])

    def recip(self, out, in_):
        self.P.dve(lambda e: e.reciprocal(out, in_), [in_], [out])

    def reduce(self, out, in_, op=ALU.add):
        self.P.dve(lambda e: e.tensor_reduce(out, in_, AX.X, op), [in_], [out])

    def dma(self, q, out, in_, key, **kw):
        self.P.dma(q, out, in_, key, **kw)


def mixer(env, l):
    k = env["k"]; P = env["P"]; Carver = env["Carver"]; pss = env["pss"]; XT = env["XT"]
    cm = env["cm"]; ident_bf = env["ident_bf"]; negm_bf = env["negm_bf"]; small = env["small"]
    w_in = env["w_in"]; fx_bf = env["fx_bf"]; fx_qg = env["fx_qg"]; fx_kg = env["fx_kg"]
    qa_s = env["qa_s"]; ka_s = env["ka_s"]; sgo_s = env["sgo_s"]; norm1_g = env["norm1_g"]
    P.phase = 'norm1'
    env["norm_T"](norm1_g[l:l + 1, :])
    c = Carver()
    c.off = 2600
    vaug = c.take(NT * 8 * 65 // 2 + 4, BF16)[:, 0:NT * 8 * 65].rearrange("p (t h d) -> p t h d", t=NT, h=8, d=65)
    negc = c.take(NT * 8, F32, (NT, 8))
    nlf = c.take(NT * 8, F32, (NT, 8))
    gq = c.take(64)
    gk = c.take(64)
    bfb = c.take(8)
    tmpf = [c.take(512) for _ in range(2)]
    sgst = [c.take(256, BF16) for _ in range(2)]
    persist_off = c.off
    wfx = [c.take(2048, BF16, (8, 512)) for _ in range(2)]
    wfl = c.take(32, BF16, (8, 8))
    qbf = [c.take(256, BF16, (8, 64)) for _ in range(2)]
    qst = [c.take(512, BF16, (8, 128)) for _ in range(2)]
    t8 = small[:, 112:120]
    chl = small[:, 128:136].bitcast(BF16)
    cst = c.take(64, BF16)
    k.dma("sync", gq, fx_qg[l:l + 1, :].partition_broadcast(128), "gq")
    k.dma("sync", gk, fx_kg[l:l + 1, :].partition_broadcast(128), "gk")
    k.dma("sync", bfb, fx_bf[l:l + 1, :].partition_broadcast(128), "bfb")
    k.ts("vector", gq, gq, 0.125, None, ALU.mult)
    k.memset("vector", vaug[:, :, :, 64:65], 1.0)
    wv = w_in[l].rearrange("(kc p) c -> p kc c", p=128)
    P.phase = 'fox_inproj'
    ss8s = [small[:, 96:104], small[:, 144:152]]
    rs8s = [small[:, 104:112], small[:, 152:160]]
    k.dma("gpsimd", wfx[0], wv[:, :, RWC:RWC + 512], "wfx0")
    pend_tr = None
    for ci in range(4):
        wb = wfx[ci % 2]
        if ci + 1 < 4:
            k.dma("gpsimd", wfx[(ci + 1) % 2], wv[:, :, RWC + (ci + 1) * 512:RWC + (ci + 2) * 512], "wfx%d" % ((ci + 1) % 2))
        for i in range(NT):
            ps = pss[2 + (i % 2)]
            for kc in range(8):
                k.mm(ps[:], XT[:, kc, i * 128:(i + 1) * 128], wb[:, kc, :], start=kc == 0, stop=kc == 7)
            if pend_tr is not None:
                pend_tr()
                pend_tr = None
            ps3 = ps[:].rearrange("p (h d) -> p h d", d=64)
            if ci < 2:
                tf = tmpf[i % 2]
                tf3 = tf.rearrange("p (h d) -> p h d", d=64)
                ss8 = ss8s[i % 2]; rs8 = rs8s[i % 2]
                k.act(tf, ps[:], AF.Square)
                k.reduce(ss8, tf3)
                k.ts("vector", ss8, ss8, 1.0 / 64, EPS, ALU.mult, ALU.add)
                k.act(ss8, ss8, AF.Sqrt)
                k.recip(rs8, ss8)
                k.tt("vector", tf3, ps3, rs8.unsqueeze(2).to_broadcast([128, 8, 64]), ALU.mult)
                g = gq if ci == 0 else gk
                qb = qbf[i % 2]
                k.tt("gpsimd", qb, tf3, g.unsqueeze(1).to_broadcast([128, 8, 64]), ALU.mult)

                def do_tr(i=i, ci=ci, qb=qb):
                    pt = pss[4 + (i % 2)][:].bitcast(BF16).rearrange("p (a b) -> p a b", b=128)[0:64, 0:8, :]
                    for hh in range(8):
                        k.tr(pt[:, hh, :], qb[:, hh, :], ident_bf[:])
                    qs = qst[i % 2]
                    k.copy("scalar", qs[0:64, :, :], pt)
                    dst = (qa_s if ci == 0 else ka_s)[:, 0:64, i * 128:(i + 1) * 128].rearrange("h r t -> r h t")
                    k.dma("sync", dst, qs[0:64, :, :], "qst%d" % (i % 2))
                pend_tr = do_tr
            elif ci == 2:
                k.copy("scalar", vaug[:, i, :, 0:64], ps3)
            else:
                sg = sgst[i % 2]
                k.act(sg, ps[:], AF.Sigmoid)
                k.dma("sync", sgo_s[i * 128:(i + 1) * 128, :], sg, "sgst%d" % (i % 2))
    if pend_tr is not None:
        pend_tr()
    P.phase = 'fox_fl'
    k.dma("gpsimd", wfl, wv[:, :, RWC + 2048:RWC + 2056], "wfl")
    for i in range(NT):
        ps = pss[2 + (i % 2)]
        for kc in range(8):
            k.mm(ps[:, 0:8], XT[:, kc, i * 128:(i + 1) * 128], wfl[:, kc, :], start=kc == 0, stop=kc == 7)
        k.tt("vector", t8, ps[:, 0:8], bfb, ALU.add)
        k.act(t8, t8, AF.Exp, scale=-1.0)
        k.act(nlf[:, i, :], t8, AF.Ln, bias=1.0)
    for i in range(NT):
        pc = pss[4 + (i % 2)]
        for ip in range(i):
            k.mm(pc[:, 0:8], cm[:, 5, :], nlf[:, ip, :], start=ip == 0, stop=False)
        k.mm(pc[:, 0:8], cm[:, 6, :], nlf[:, i, :], start=i == 0, stop=True)
        k.copy("vector", negc[:, i, :], pc[:, 0:8])
        k.ts("vector", chl[:, 0:8], pc[:, 0:8], -1.0, None, ALU.mult)
        k.stt(chl[:, 8:16], pc[:, 0:8], -1.0, chl[:, 0:8], ALU.mult, ALU.subtract)
        pt = pss[2 + (i % 2)][:].bitcast(BF16)[0:16, 0:128]
        k.tr(pt, chl, ident_bf[:])
        k.copy("vector", cst[0:16, :], pt)
        k.dma("sync", qa_s[:, 64, i * 128:(i + 1) * 128], cst[0:8, :], "cst")
        k.dma("sync", qa_s[:, 65, i * 128:(i + 1) * 128], cst[8:16, :], "cst")
    P.phase = 'rw_inproj'
    if RWKV_ENABLED:
        env["rwkv_inproj"](env, l, persist_off)
    P.phase = 'fox_att'
    c2 = Carver()
    c2.off = persist_off
    qT = [c2.take(LP // 2, BF16) for _ in range(2)]
    kT = [c2.take(LP // 2, BF16) for _ in range(2)]
    Eb = [c2.take(256, BF16) for _ in range(3)]
    zb = c2.take(256, BF16)
    yfx = c2.take(NT * 512 // 2, BF16, (NT, 512))
    rd = small[:, 140:141]
    k.memset("vector", zb, 0.0)
    ecnt = 0
    for hh in range(8):
        q_ = qT[hh % 2]
        k_ = kT[hh % 2]
        k.dma("sync", q_[0:66, :], qa_s[hh], "qT%d" % (hh % 2))
        k.dma("sync", k_[0:66, :], ka_s[hh], "kT%d" % (hh % 2))
        for qt in range(5):
            q0 = qt * 512
            Nq = min(512, LP - q0)
            nqb = Nq // 128
            Ob = pss[6 + (qt % 2)]
            Oacc = Ob[:, 0:260].rearrange("p (a d) -> p a d", d=65)
            k.mm(Ob[:, 0:260], zb[:, 0:128], zb[:, 0:260], start=True, stop=True)
            kb_max = (q0 + Nq) // 128 - 1

            def scores(kb):
                j = kb - 4 * qt
                cs = max(0, j) * 128
                nonlocal ecnt
                pS = pss[ecnt % 2]
                E = Eb[ecnt % 3]
                ecnt += 1
                k.mm(pS[:, cs:Nq], k_[0:66, kb * 128:(kb + 1) * 128], q_[0:66, q0 + cs:q0 + Nq], start=True, stop=j < 0)
                if j >= 0:
                    k.mm(pS[:, cs:cs + 128], ident_bf[:], negm_bf[:], start=False, stop=True)
                return (kb, j, cs, pS, E)

            pend = scores(0)
            for kb in range(kb_max + 1):
                cur = pend
                pend = scores(kb + 1) if kb + 1 <= kb_max else None
                _, j, cs, pS, E = cur
                k.act(E[:, cs:Nq], pS[:, cs:Nq], AF.Exp, bias=negc[:, kb, hh:hh + 1])
                for qbl in range(max(0, j), nqb):
                    P.pe(lambda e, o=Oacc[:, qbl, :], a=E[:, qbl * 128:(qbl + 1) * 128], b=vaug[:, kb, hh, :]:
                         e.matmul(o, a, b, start=False, stop=True, skip_group_check=True),
                         [E[:, qbl * 128:(qbl + 1) * 128], vaug[:, kb, hh, :]], [Oacc[:, qbl, :]])
            for qbl in range(nqb):
                i = 4 * qt + qbl
                k.recip(rd, Oacc[:, qbl, 64:65])
                k.ts("vector", yfx[:, i, hh * 64:(hh + 1) * 64], Oacc[:, qbl, 0:64], rd, None, ALU.mult)
    P.phase = 'fox_gate'
    for i in range(NT):
        sg = sgst[i % 2]
        k.dma("sync", sg, sgo_s[i * 128:(i + 1) * 128, :], "sgld%d" % (i % 2))
        yg = tmpf[i % 2].bitcast(BF16)[:, 0:512]
        k.tt("vector", yg, yfx[:, i, :], sg, ALU.mult)
        pt = pss[2 + (i % 2)][:].bitcast(BF16).rearrange("p (a b) -> p a b", b=128)[:, 0:4, :]
        for cc in range(4):
            k.tr(pt[:, cc, :], yg[:, cc * 128:(cc + 1) * 128], ident_bf[:])
        k.copy("scalar", XT[:, 4:8, i * 128:(i + 1) * 128], pt)


def rwkv_inproj(env, l, off):
    k = env["k"]; P = env["P"]; Carver = env["Carver"]; pss = env["pss"]; XT = env["XT"]
    cm = env["cm"]; ident_bf = env["ident_bf"]; ident_f = env["ident_f"]; ident_r = env["ident_r"]
    w_in = env["w_in"]; rw_s = env["rw_s"]; g_s = env["g_s"]
    mu_p = env["mu_p"]; rwvec_p = env["rwvec_p"]; wup_aug = env["wup_aug"]; a_up = env["a_up"]
    g_up = env["g_up"]; gn_w = env["gn_w"]; gn_b = env["gn_b"]
    wv = w_in[l].rearrange("(kc p) c -> p kc c", p=128)
    c = Carver()
    c.off = off
    mu = c.take(16)[:, 0:15]
    omu = c.take(16)[:, 0:15]
    k.dma("sync", mu, mu_p[l], "mu")
    k.ts("vector", omu, mu, -1.0, 1.0, ALU.mult, ALU.add)
    wrw = [c.take(512, BF16, (8, 128)) for _ in range(2)]
    PA = [c.take(LP) for _ in range(2)]
    PB = [c.take(LP + 2) for _ in range(2)]
    sh = [c.take(LP) for _ in range(2)]
    for b_ in range(2):
        k.memset("vector", PB[b_][:, 0:1], 0.0)
    def load_rw(m):
        nco_ = 128 if m < 14 else 32
        k.dma("gpsimd", wrw[m % 2][:, :, 0:nco_], wv[:, :, m * 128:m * 128 + nco_], "wrw%d" % (m % 2))

    load_rw(0)
    for m in range(15):
        nco = 128 if m < 14 else 32
        wb = wrw[m % 2]
        if m + 1 < 15:
            load_rw(m + 1)
        pa = PA[m % 2]
        pb = PB[m % 2]
        for n in range(5):
            c0 = n * 512
            N = min(512, LP - c0)
            ps = pss[2 + (n % 2)]
            for kc in range(8):
                k.mm(ps[0:nco, 0:N], wb[:, kc, 0:nco], XT[:, kc, c0:c0 + N], start=kc == 0, stop=kc == 7)
            k.act(pa[0:nco, c0:c0 + N], ps[0:nco, 0:N], AF.Copy, scale=omu[0:nco, m:m + 1])
            k.act(pb[0:nco, 1 + c0:1 + c0 + N], ps[0:nco, 0:N], AF.Copy, scale=mu[0:nco, m:m + 1])
        so = sh[m % 2]
        k.tt("vector" if m % 2 else "gpsimd", so[0:nco, :], pa[0:nco, :], pb[0:nco, 0:LP], ALU.add)
        k.dma("sync", rw_s[m, 0:nco, :], so[0:nco, :], "sh%d" % (m % 2))


def rwkv(env, l):
    k = env["k"]; P = env["P"]; Carver = env["Carver"]; pss = env["pss"]; XT = env["XT"]
    cm = env["cm"]; ident_bf = env["ident_bf"]; ident_f = env["ident_f"]; ident_r = env["ident_r"]
    rw_s = env["rw_s"]; g_s = env["g_s"]
    rwvec_p = env["rwvec_p"]; wup_aug = env["wup_aug"]; a_up = env["a_up"]
    g_up = env["g_up"]; gn_w = env["gn_w"]; gn_b = env["gn_b"]
    P.phase = 'rw_lora'
    c = Carver()
    TW = c.take(LP, F32R)
    AD = c.take(LP, F32R)
    SG1 = c.take(LP // 2, BF16)
    SG2 = c.take(LP // 2, BF16)
    wup = c.take(512, F32R)
    aup = c.take(512, F32R)
    gup1 = c.take(256, BF16)
    gup2 = c.take(256, BF16)
    vec = c.take(16, F32, (4, 4))
    omka = c.take(4)
    hsel = c.take(2, F32R)
    hself = c.take(2)
    NEGE = -float(np.exp(-0.5))
    tric = c.take(256, F32, (2, 128))
    k.ts("vector", tric[:, 0, :], cm[:, 2, :], NEGE, None, ALU.mult)
    k.ts("vector", tric[:, 1, :], cm[:, 1, :], NEGE, None, ALU.mult)
    lt = c.take(LP)
    if RW_STAGE < 2:
        return
    k.dma("sync", vec, rwvec_p[l], "vec")
    k.ts("vector", omka, vec[:, :, 1], -1.0, 1.0, ALU.mult, ALU.add)
    k.copy("vector", hself[:, 0:1], cm[:, 7, 0:1])
    k.copy("vector", hself[:, 1:2], cm[:, 7, 127:128])
    k.copy("vector", hsel, hself)
    k.dma("sync", lt[0:128, :], rw_s[12], "lt")
    k.act(TW[0:64, :], lt[0:64, :], AF.Tanh)
    k.ts("vector", TW[64:128, :], lt[64:128, :], 0.0, 1.0, ALU.mult, ALU.add)
    k.copy("vector", AD, lt)
    lt2 = c.take(LP)
    k.dma("sync", lt2[0:128, :], rw_s[13], "lt2")
    k.act(SG1, lt2, AF.Sigmoid)
    lt3 = c.take(LP)
    k.dma("sync", lt3[0:32, :], rw_s[14, 0:32, :], "lt3")
    k.act(SG2[0:32, :], lt3[0:32, :], AF.Sigmoid)
    wtmp = c.take(512)
    k.dma("sync", wtmp[0:65, :], wup_aug[l], "wtmp")
    k.memset("vector", wup[64:128, :], 0.0)
    k.copy("vector", wup[0:65, :], wtmp[0:65, :])
    wtmp2 = c.take(512)
    k.dma("sync", wtmp2[64:128, :], a_up[l], "wtmp2")
    k.memset("vector", aup[0:64, :], 0.0)
    k.copy("vector", aup[64:128, :], wtmp2[64:128, :])
    k.dma("gpsimd", gup1, g_up[l, 0:128, :], "gup1")
    k.dma("gpsimd", gup2[0:32, :], g_up[l, 128:160, :], "gup2")
    gst = [wtmp, wtmp2]
    for i in range(NT):
        ps = pss[2 + (i % 2)]
        k.mm(ps[:], SG1[:, i * 128:(i + 1) * 128], gup1, start=True, stop=False)
        k.mm(ps[:], SG2[0:32, i * 128:(i + 1) * 128], gup2[0:32, :], start=False, stop=True)
        k.copy("scalar", gst[i % 2], ps[:])
        k.dma("sync", g_s[i * 128:(i + 1) * 128, :], gst[i % 2], "gst%d" % (i % 2))
    per_off = c.off - 3 * LP - 1024
    if RW_STAGE < 3:
        return
    RD = BF16

    def tk(n):
        return c.take(n // 2, RD)

    def run_interleaved(gens):
        active = list(gens)
        while active:
            for g in list(active):
                try:
                    next(g)
                except StopIteration:
                    active.remove(g)

    def interleave_gen(gens):
        active = list(gens)
        while active:
            for g in list(active):
                try:
                    next(g)
                    yield
                except StopIteration:
                    active.remove(g)

    P.phase = 'rw_pair'
    c = Carver()
    c.off = per_off
    pair_bufs = []
    for ps_i in range(2):
        B = {}
        B["RKV"] = c.take(384, F32, (3, 128))
        B["AS"] = c.take(128); B["SIGT"] = c.take(128); B["E13"] = c.take(256); B["E2"] = c.take(128)
        for nm in ("kk", "kk2", "rin", "kkn", "t1", "kp", "bb", "rk"):
            B[nm] = c.take(128)
        B["bT"] = tk(128); B["kT"] = tk(128)
        B["bTh"] = [tk(128) for _ in range(2)]
        B["kTh"] = [tk(128) for _ in range(2)]
        B["A0T"] = [tk(128) for _ in range(2)]
        B["Zb"] = [[tk(384), tk(384)] for _ in range(2)]
        B["sets"] = []
        for _ in range(2):
            B["sets"].append(dict(AR=tk(256), v_tok=tk(128), btok=[tk(128), tk(128)], ktok=[tk(128), tk(128)],
                                  GM=c.take(4), AB=[tk(256), tk(256)], AK=[tk(256), tk(256)], T=[tk(128), tk(128)],
                                  vtf=c.take(128), coef=c.take(2), y=c.take(128), g=c.take(128)))
        B["hd"] = [dict(W=tk(64), U=tk(64), S16=tk(64), S32=c.take(64), S32g=c.take(64)) for _ in range(2)]
        B["gnw"] = c.take(128); B["gnb"] = c.take(128)
        B["sq"] = c.take(128); B["yn"] = c.take(128); B["yb"] = tk(128)
        B["st"] = c.take(16)
        B["banks"] = (pss[3 * ps_i], pss[3 * ps_i + 1], pss[3 * ps_i + 2])
        pair_bufs.append(B)

    def pair_gen(hp):
        B = pair_bufs[hp % 2]
        tag = "_%d" % (hp % 2)
        RKV = B["RKV"]; AS = B["AS"]; SIGT = B["SIGT"]; E13 = B["E13"]; E2 = B["E2"]
        kk = B["kk"]; kk2 = B["kk2"]; rin = B["rin"]; kkn = B["kkn"]; t1 = B["t1"]; kp = B["kp"]; bb = B["bb"]; rk = B["rk"]
        bT = B["bT"]; kT = B["kT"]; bTh = B["bTh"]; kTh = B["kTh"]; A0T = B["A0T"]; Zb = B["Zb"]
        sets = B["sets"]; hd_b = B["hd"]; gnw = B["gnw"]; gnb = B["gnb"]
        X0, X1, XS = B["banks"]
        pA = X0[:, 0:256]; pC = X0[:, 256:512]
        pT = X1
        pM = X0; pM2 = X1
        pIs = [X0, X1]
        pSs = [XS[:, 0:256], XS[:, 256:512]]
        for d in hd_b:
            for nm in ("W", "U", "S16", "S32"):
                k.memset("vector", d[nm], 0.0)
        k.dma("sync", gnw, gn_w[l:l + 1, hp * 128:(hp + 1) * 128].partition_broadcast(128), "gnw" + tag)
        k.dma("sync", gnb, gn_b[l:l + 1, hp * 128:(hp + 1) * 128].partition_broadcast(128), "gnb" + tag)
        kk_s = vec[:, hp, 0:1]; ka_s_ = vec[:, hp, 1:2]; a0_s = vec[:, hp, 2:3]; rk_s = vec[:, hp, 3:4]
        yield

        def prep(i):
            S_ = sets[i % 2]
            AR = S_["AR"]; v_tok = S_["v_tok"]; btok = S_["btok"]; ktok = S_["ktok"]; GM = S_["GM"]
            t0 = i * 128
            for q in range(3):
                k.dma("sync", RKV[:, q, :], rw_s[4 * q + hp, :, t0:t0 + 128], "rkv%d" % q + tag)
            k.dma("sync", S_["g"], g_s[t0:t0 + 128, hp * 128:(hp + 1) * 128], "gt%d" % (i % 2) + tag)
            r_ = RKV[:, 0, :]; k_ = RKV[:, 1, :]; v_ = RKV[:, 2, :]
            k.mm(pA[:, 128:256], aup[:, hp * 128:(hp + 1) * 128], AD[:, t0:t0 + 128])
            k.mm(pA[:, 0:128], TW[:, t0:t0 + 128], wup[:, hp * 128:(hp + 1) * 128])
            k.act(AS, pA[:, 128:256], AF.Sigmoid, bias=a0_s)
            k.act(SIGT, pA[:, 0:128], AF.Sigmoid)
            yield
            k.mm(pC[:, 0:128], SIGT, tric[:, 0, :])
            k.mm(pC[:, 128:256], SIGT, tric[:, 1, :])
            k.act(E13, pC[:, 0:256], AF.Exp)
            k.act(E2, pC[:, 0:128], AF.Exp, scale=-1.0)
            yield
            E1 = E13[:, 0:128]; E3 = E13[:, 128:256]
            k.ts("vector", kk, k_, kk_s, None, ALU.mult)
            k.tt("gpsimd", kk2, kk, kk, ALU.mult)
            k.mm(pT[:, 384:512], cm[:, 7, :], kk2)
            k.act(rin, pT[:, 384:512], AF.Sqrt)
            yield
            k.ts("vector", rin, rin, 1e-12, None, ALU.max)
            k.recip(rin, rin)
            k.tt("vector", kkn, kk, rin, ALU.mult)
            k.ts("vector", t1, AS, ka_s_, omka[:, hp:hp + 1], ALU.mult, ALU.add)
            yield
            k.tt("gpsimd", kp, k_, t1, ALU.mult)
            k.tt("gpsimd", bb, kkn, AS, ALU.mult)
            k.tt("vector", AR[:, 128:256], r_, E1, ALU.mult)
            k.tt("gpsimd", t1, kkn, E3, ALU.mult)
            yield
            k.ts("vector", AR[:, 0:128], t1, -1.0, None, ALU.mult)
            k.tt("vector", bT, bb, E2, ALU.mult)
            k.tt("vector", kT, kp, E2, ALU.mult)
            k.tt("gpsimd", rk, r_, kp, ALU.mult)
            yield
            k.ts("vector", rk, rk, rk_s, None, ALU.mult)
            for hd in range(2):
                for cc in range(2):
                    k.ts("vector", GM[:, hd * 2 + cc:hd * 2 + cc + 1], hself[:, hd:hd + 1],
                         E13[:, 63 + 64 * cc:64 + 64 * cc], None, ALU.mult)
            yield
            k.mm(pT[:, 0:128], v_, ident_f[:])
            k.mm(pT[:, 128:256], bT, ident_bf[:])
            k.mm(pT[:, 256:384], kT, ident_bf[:])
            k.mm(pT[:, 384:386], rk, hself)
            k.copy("scalar", v_tok, pT[:, 0:128])
            k.copy("scalar", S_["vtf"], pT[:, 0:128])
            yield
            k.copy("scalar", S_["coef"], pT[:, 384:386])
            for cc in range(2):
                k.act(btok[cc], pT[:, 128:256], AF.Copy, scale=hself[:, cc:cc + 1])
                k.act(ktok[cc], pT[:, 256:384], AF.Copy, scale=hself[:, cc:cc + 1])
            yield
            for hd in range(2):
                k.ts("vector", bTh[hd], bT, hself[:, hd:hd + 1], None, ALU.mult)
                k.ts("vector", kTh[hd], kT, hself[:, hd:hd + 1], None, ALU.mult)
                k.mm(pM[:, 0:256], bTh[hd], AR)
                k.mm(pM[:, 256:384], AR[:, 0:128], bTh[hd])
                k.mm(pM2[:, 0:256], kTh[hd], AR)
                yield
                k.tt("vector", S_["AB"][hd], pM[:, 0:256], cm[:, 1:3, :].rearrange("p a b -> p (a b)"), ALU.mult)
                k.tt("vector", A0T[hd], pM[:, 256:384], cm[:, 3, :], ALU.mult)
                k.tt("vector", S_["AK"][hd], pM2[:, 0:256], cm[:, 1:3, :].rearrange("p a b -> p (a b)"), ALU.mult)
                yield
            Zc = []
            for hd in range(2):
                Y = S_["AB"][hd][:, 0:128]; YT = A0T[hd]
                Z = Zb[hd][0]
                pI = pIs[hd]
                k.mm(pI[:, 128:256], YT, Y)
                k.mm(pI[:, 256:384], Y, YT)
                k.tt("gpsimd", Z[:, 0:128], Y, ident_bf[:], ALU.add)
                k.copy("scalar", Z[:, 128:384], pI[:, 128:384])
                Zc.append(Z)
            yield
            for lev in range(1, 5):
                for hd in range(2):
                    Z = Zc[hd]
                    Zn = Zb[hd][lev % 2]
                    pI = pIs[hd]
                    k.mm(pI[:, 0:256], Z[:, 256:384], Z[:, 0:256])
                    k.mm(pI[:, 256:384], Z[:, 128:256], Z[:, 256:384])
                    k.tt("vector", Zn[:, 0:128], pI[:, 0:128], Z[:, 0:128], ALU.add)
                    k.copy("scalar", Zn[:, 128:384], pI[:, 128:384])
                    Zc[hd] = Zn
                    yield
            for hd in range(2):
                Z = Zc[hd]
                pI = pIs[hd]
                k.mm(pI[:, 0:128], Z[:, 256:384], Z[:, 0:128])
                k.tt("vector", S_["T"][hd], pI[:, 0:128], Z[:, 0:128], ALU.add)
            yield

        def seq(i):
            S_ = sets[i % 2]
            AR = S_["AR"]; v_tok = S_["v_tok"]; btok = S_["btok"]; ktok = S_["ktok"]; GM = S_["GM"]
            y_t = S_["y"]
            t0 = i * 128
            for cc in range(2):
                rs = slice(cc * 64, cc * 64 + 64)
                for hd in range(2):
                    d = hd_b[hd]
                    vh = v_tok[:, hd * 64:(hd + 1) * 64]
                    gm = GM[:, hd * 2 + cc:hd * 2 + cc + 1]
                    pS = pSs[hd]
                    k.mm(pS[:, 0:64], AR[:, 0:128], d["S16"], start=True, stop=False)
                    k.mm(pS[:, 0:64], S_["AK"][hd][:, 0:128], vh, start=False, stop=True)
                    k.act(d["S32g"], d["S32"], AF.Copy, scale=gm)
                    k.copy("scalar", d["W"][rs, :], pS[rs, 0:64])
                    yield
                for hd in range(2):
                    d = hd_b[hd]
                    pS = pSs[hd]
                    k.mm(pS[:, 64:128], S_["T"][hd], d["W"])
                    k.copy("scalar", d["U"][rs, :], pS[rs, 64:128])
                    yield
                for hd in range(2):
                    d = hd_b[hd]
                    vh = v_tok[:, hd * 64:(hd + 1) * 64]
                    gm = GM[:, hd * 2 + cc:hd * 2 + cc + 1]
                    pS = pSs[hd]
                    k.mm(pS[:, 192:256], btok[cc], d["U"], start=True, stop=False)
                    k.mm(pS[:, 192:256], ktok[cc], vh, start=False, stop=True)
                    k.mm(pS[:, 128:192], AR[:, 128:256], d["S16"], start=True, stop=False)
                    k.mm(pS[:, 128:192], S_["AB"][hd][:, 128:256], d["U"], start=False, stop=False)
                    k.mm(pS[:, 128:192], S_["AK"][hd][:, 128:256], vh, start=False, stop=True)
                    k.stt(d["S32"], pS[:, 192:256], gm, d["S32g"], ALU.mult, ALU.add)
                    k.copy("vector", d["S16"], d["S32"])
                    k.copy("scalar", y_t[rs, hd * 64:hd * 64 + 64], pS[rs, 128:192])
                    yield
            st = B["st"]; sq = B["sq"]; yn = B["yn"]; yb = B["yb"]
            s1 = st[:, 0:2]; s2 = st[:, 2:4]; mn = st[:, 4:6]; m2 = st[:, 6:8]
            y3 = y_t.rearrange("p (a d) -> p a d", d=64)
            sq3 = sq.rearrange("p (a d) -> p a d", d=64)
            yn3 = yn.rearrange("p (a d) -> p a d", d=64)
            vt3 = S_["vtf"].rearrange("p (a d) -> p a d", d=64)
            k.reduce(s1, y3)
            k.tt("gpsimd", sq, y_t, y_t, ALU.mult)
            k.reduce(s2, sq3)
            k.ts("vector", mn, s1, 1.0 / 64, None, ALU.mult)
            yield
            k.tt("vector", m2, mn, mn, ALU.mult)
            k.ts("vector", s2, s2, 1.0 / 64, None, ALU.mult)
            k.tt("vector", s2, s2, m2, ALU.subtract)
            k.ts("vector", s2, s2, GN_EPS, None, ALU.add)
            k.act(s2, s2, AF.Sqrt)
            k.recip(s2, s2)
            yield
            k.tt("vector", yn3, y3, mn.unsqueeze(2).to_broadcast([128, 2, 64]), ALU.subtract)
            k.tt("vector", yn3, yn3, s2.unsqueeze(2).to_broadcast([128, 2, 64]), ALU.mult)
            k.tt("gpsimd", sq3, vt3, S_["coef"].unsqueeze(2).to_broadcast([128, 2, 64]), ALU.mult)
            k.tt("vector", yn, yn, gnw, ALU.mult)
            yield
            k.tt("vector", yn, yn, gnb, ALU.add)
            k.tt("vector", yn, yn, sq, ALU.add)
            k.tt("vector", yb, yn, S_["g"], ALU.mult)
            pt = X1[:].bitcast(BF16)[:, 768:896]
            k.tr(pt, yb, ident_bf[:])
            k.copy("scalar", XT[:, hp, t0:t0 + 128], pt)
            yield

        yield from prep(0)
        for i in range(NT):
            gens = [seq(i)]
            if i + 1 < NT:
                gens.append(prep(i + 1))
            yield from interleave_gen(gens)

    for hp0 in (0, 2):
        for _ in interleave_gen([pair_gen(hp0), pair_gen(hp0 + 1)]):
            pass


def build_program(nlayers=DEPTH, dbg=None, stages=("mix", "ffn")):
    nc = bass.Bass("TRN2", target_bir_lowering=False)
    dram = {}

    def din(name, shape, dt=F32):
        dram[name] = nc.dram_tensor(name, list(shape), dt, kind="ExternalInput").ap()
        return dram[name]

    def dscr(name, shape, dt=F32):
        kind = "ExternalOutput" if (dbg and name in dbg) else "Internal"
        dram[name] = nc.dram_tensor(name, list(shape), dt, kind=kind).ap()
        return dram[name]

    x = din("x", [SEQ, D])
    meta = din("meta", [NMETA, D])
    norm1_g = din("norm1_g", [DEPTH, D])
    norm2_g = din("norm2_g", [DEPTH, D])
    w_in = din("w_in", [DEPTH, D, RWC + FXC])
    w_o = din("w_o", [DEPTH, D, D])
    ffn_w_in = din("ffn_w_in", [DEPTH, D, 2 * DFF])
    ffn_w_out = din("ffn_w_out", [DEPTH, DFF, D])
    conv_wp = din("conv_wp", [DEPTH, 128, 44, 3])
    conv_bp = din("conv_bp", [DEPTH, 128, 44])
    mu_p = din("mu_p", [DEPTH, 128, 15])
    rwvec_p = din("rwvec_p", [DEPTH, 128, 4, 4])
    wup_aug = din("wup_aug", [DEPTH, 65, 512])
    a_up = din("a_up", [DEPTH, 64, 512])
    g_up = din("g_up", [DEPTH, 160, 512])
    gn_w = din("gn_w", [DEPTH, 512])
    gn_b = din("gn_b", [DEPTH, 512])
    fx_bf = din("fx_bf", [DEPTH, 8])
    fx_qg = din("fx_qg", [DEPTH, 64])
    fx_kg = din("fx_kg", [DEPTH, 64])
    cmask = din("cmask", [128, 8, 128])
    out = nc.dram_tensor("out", [SEQ, D], F32, kind="ExternalOutput").ap()

    rw_s = dscr("rw_s", [15, 128, LP])
    g_s = dscr("g_s", [LP, 512])
    qa_s = dscr("qa_s", [NB, 66, LP], BF16)
    ka_s = dscr("ka_s", [NB, 66, LP], BF16)
    sgo_s = dscr("sgo_s", [LP, 512], BF16)
    hdbg = dscr("hdbg", [LP, D]) if dbg else None
    mixdbg = dscr("mixdbg", [128, 8, LP], BF16) if dbg else None

    with contextlib.ExitStack() as st:
        def sb(name, shape, dt=F32):
            return st.enter_context(nc.sbuf_tensor(name, list(shape), dt))

        h = sb("h", [128, NT, D])
        XT = sb("XT", [128, 8, LP], BF16)
        ident_bf = sb("ident_bf", [128, 128], BF16)
        ident_f = sb("ident_f", [128, 128])
        ident_r = ident_f
        cm = sb("cm", [128, 8, 128])
        small = sb("small", [128, 256])
        ARENA_W = 24200
        arena = sb("arena", [128, ARENA_W])
        pss = [st.enter_context(nc.psum_tensor("ps%d" % i, [128, 512], F32)) for i in range(8)]

        P = Prog(nc)
        k = K(P)

        class Carver:
            def __init__(self):
                self.off = 0

            def take(self, nwords, dt=F32, shape=None):
                a = arena[:, self.off:self.off + nwords]
                self.off += nwords
                assert self.off <= ARENA_W, self.off
                if dt != F32:
                    a = a.bitcast(dt)
                if shape is not None:
                    names = " ".join("d%d" % i for i in range(len(shape)))
                    kw = {"d%d" % i: s for i, s in enumerate(shape)}
                    a = a.rearrange("p (%s) -> p %s" % (names, names), **kw)
                return a

        k.dma("sync", cm[:], cmask, "cm")
        k.copy("vector", ident_f[:], cm[:, 0, :])
        k.copy("vector", ident_bf[:], cm[:, 0, :])
        negm_bf = sb("negm_bf", [128, 128], BF16)
        k.copy("vector", negm_bf[:], cm[:, 4, :])

        ones_bf = sb("ones_bf", [8, LP], BF16)
        k.memset("vector", ones_bf[:], 1.0)
        k.dma("sync", ka_s[:, 64, :], ones_bf[:], "kones")
        k.dma("sync", ka_s[:, 65, :], ones_bf[:], "kones")
        k.memset("vector", h[:, NT - 1, :], 0.0)
        k.dma("sync", h[0:16, 0, :], meta, "hload")
        k.dma("sync", h[16:128, 0, :], x[0:112, :], "hload")
        k.dma("sync", h[:, 1:16, :], x[112:112 + 1920, :].rearrange("(t p) d -> p t d", p=128), "hload")
        k.dma("sync", h[0:16, 16, :], x[2032:2048, :], "hload")

        ssq = small[:, 0:17]
        rstd = small[:, 32:49]
        tmp17 = small[:, 64:81]

        def norm_T(g_row):
            c = Carver()
            gbc = c.take(D)
            k.dma("sync", gbc, g_row.partition_broadcast(128), "gbc")
            junk = c.take(512, BF16)
            xn = [c.take(512, BF16), c.take(512, BF16)]
            for i in range(NT):
                k.act(junk, h[:, i, :], AF.Square, accum=ssq[:, i:i + 1])
            k.ts("vector", tmp17, ssq, 1.0 / D, EPS, ALU.mult, ALU.add)
            k.act(tmp17, tmp17, AF.Sqrt)
            k.recip(rstd, tmp17)
            for i in range(NT):
                xb = xn[i % 2]
                k.stt(xb, h[:, i, :], rstd[:, i:i + 1], gbc, ALU.mult, ALU.mult)
                pt = pss[i % 2][:].bitcast(BF16).rearrange("p (a b) -> p a b", b=128)[:, 0:8, :]
                for kc in range(8):
                    k.tr(pt[:, kc, :], xb[:, kc * 128:(kc + 1) * 128], ident_bf[:])
                k.copy("scalar" if i % 2 else "vector", XT[:, :, i * 128:(i + 1) * 128], pt)

        def add_to_h(i, n2, ps):
            k.tt("vector", h[:, i, n2 * 512:(n2 + 1) * 512], ps, h[:, i, n2 * 512:(n2 + 1) * 512], ALU.add)

        def ffn(l):
            P.phase = 'norm2'
            norm_T(norm2_g[l:l + 1, :])
            P.phase = 'ffn'
            c = Carver()
            cw = c.take(44 * 3, F32, (44, 3))
            cb = c.take(44)
            k.dma("sync", cw, conv_wp[l], "cw")
            k.dma("sync", cb, conv_bp[l], "cb")
            groups = [list(range(0, 5)), list(range(5, 10)), list(range(10, 14)), list(range(14, 18)), list(range(18, 22))]
            hid = c.take(5 * LP // 2, BF16, (5, LP))
            wout = c.take(5 * D // 2, BF16, (5, D))
            wu = [c.take(512, BF16, (8, 128)) for _ in range(2)]
            wg = [c.take(512, BF16, (8, 128)) for _ in range(2)]
            NBUF = 3
            HU = [c.take(516) for _ in range(NBUF)]
            HG = [c.take(516) for _ in range(NBUF)]
            t0s = [c.take(512) for _ in range(NBUF)]
            t1s = [c.take(512) for _ in range(NBUF)]
            txs = [c.take(512) for _ in range(2)]
            t2s = [c.take(512) for _ in range(2)]
            wi = ffn_w_in[l].rearrange("(kc p) c -> p kc c", p=128)
            wo = ffn_w_out[l].rearrange("(kc p) n -> p kc n", p=128)
            cnt = 0
            tcnt = 0
            pend_mul = None

            def load_w(j, cn):
                k.dma("gpsimd", wu[cn % 2], wi[:, :, j * 128:(j + 1) * 128], "wu%d" % (cn % 2))
                k.dma("gpsimd", wg[cn % 2], wi[:, :, DFF + j * 128:DFF + (j + 1) * 128], "wg%d" % (cn % 2))
            for gi, js in enumerate(groups):
                k.dma("gpsimd", wout[:, 0:len(js), :], wo[:, js[0]:js[0] + len(js), :], "wout")
                for jj, j in enumerate(js):
                    wub = wu[cnt % 2]
                    wgb = wg[cnt % 2]
                    if cnt == 0:
                        load_w(0, 0)
                    if j + 1 < 22:
                        load_w(j + 1, cnt + 1)
                    cnt += 1
                    ju = j
                    jg = 22 + j
                    for n in range(5):
                        c0 = n * 512
                        N = min(512, LP - c0)
                        pu = pss[2 + (n % 2)]
                        pg = pss[4 + (n % 2)]
                        b = tcnt % NBUF
                        bn = (tcnt + 1) % NBUF
                        hu = HU[b]; hg = HG[b]; t0 = t0s[b]; t1 = t1s[b]
                        tx = txs[tcnt % 2]; t2 = t2s[tcnt % 2]
                        tcnt += 1
                        for kc in range(8):
                            k.mm(pu[:, 0:N], wub[:, kc, :], XT[:, kc, c0:c0 + N], start=kc == 0, stop=kc == 7)
                        for kc in range(8):
                            k.mm(pg[:, 0:N], wgb[:, kc, :], XT[:, kc, c0:c0 + N], start=kc == 0, stop=kc == 7)
                        if n == 0:
                            k.memset("gpsimd", hu[:, 0:2], 0.0)
                            k.memset("gpsimd", hg[:, 0:2], 0.0)
                        k.copy("scalar", hu[:, 2:2 + N], pu[:, 0:N])
                        k.act(t0[:, 0:N], pu[:, 0:N], AF.Identity, bias=cb[:, ju:ju + 1], scale=cw[:, ju, 2:3])
                        k.copy("scalar", hg[:, 2:2 + N], pg[:, 0:N])
                        k.act(t1[:, 0:N], pg[:, 0:N], AF.Identity, bias=cb[:, jg:jg + 1], scale=cw[:, jg, 2:3])
                        if n < 4:
                            k.copy("gpsimd", HU[bn][:, 0:2], hu[:, N:N + 2])
                            k.copy("gpsimd", HG[bn][:, 0:2], hg[:, N:N + 2])
                        k.ts("vector", tx[:, 0:N], hu[:, 1:1 + N], cw[:, ju, 1:2], None, ALU.mult)
                        k.ts("vector", t2[:, 0:N], hg[:, 1:1 + N], cw[:, jg, 1:2], None, ALU.mult)
                        k.tt("vector", t1[:, 0:N], t1[:, 0:N], t2[:, 0:N], ALU.add)
                        k.tt("vector", t0[:, 0:N], t0[:, 0:N], tx[:, 0:N], ALU.add)
                        k.ts("vector", t2[:, 0:N], hg[:, 0:N], cw[:, jg, 0:1], None, ALU.mult)
                        k.ts("vector", tx[:, 0:N], hu[:, 0:N], cw[:, ju, 0:1], None, ALU.mult)
                        k.tt("vector", t1[:, 0:N], t1[:, 0:N], t2[:, 0:N], ALU.add)
                        k.act(t1[:, 0:N], t1[:, 0:N], AF.Silu)
                        k.tt("vector", t0[:, 0:N], t0[:, 0:N], tx[:, 0:N], ALU.add)
                        if pend_mul is not None:
                            pend_mul()
                        pend_mul = (lambda jj=jj, c0=c0, N=N, t0=t0, t1=t1:
                                    k.tt("vector", hid[:, jj, c0:c0 + N], t1[:, 0:N], t0[:, 0:N], ALU.mult))
                pend_mul()
                pend_mul = None
                for i in range(NT):
                    for n2 in range(2):
                        ps = pss[6 + ((i * 2 + n2) % 2)]
                        for jj in range(len(js)):
                            k.mm(ps[:], hid[:, jj, i * 128:(i + 1) * 128], wout[:, jj, n2 * 512:(n2 + 1) * 512],
                                 start=jj == 0, stop=jj == len(js) - 1)
                        add_to_h(i, n2, ps[:])

        def out_proj(l):
            P.phase = 'out_proj'
            c = Carver()
            wob = c.take(8 * D // 2, BF16, (8, D))
            k.dma("gpsimd", wob, w_o[l].rearrange("(kc p) n -> p kc n", p=128), "wob")
            for i in range(NT):
                for n2 in range(2):
                    ps = pss[6 + ((i * 2 + n2) % 2)]
                    for kc in range(8):
                        k.mm(ps[:], XT[:, kc, i * 128:(i + 1) * 128], wob[:, kc, n2 * 512:(n2 + 1) * 512],
                             start=kc == 0, stop=kc == 7)
                    add_to_h(i, n2, ps[:])

        env = dict(locals())
        env["rwkv_inproj"] = rwkv_inproj
        for l in range(nlayers):
            if "mix" in stages:
                mixer(env, l)
                if RWKV_ENABLED:
                    rwkv(env, l)
                else:
                    k.memset("gpsimd", XT[:, 0:4, :], 0.0)
                if dbg and "mixdbg" in dbg and l == 0:
                    k.dma("sync", mixdbg, XT[:], "mixdbg")
                out_proj(l)
            if "ffn" in stages:
                ffn(l)

        if dbg and "hdbg" in dbg:
            k.dma("sync", hdbg.rearrange("(t p) d -> p t d", p=128), h[:], "hdbg")
        k.dma("sync", out[0:112, :], h[16:128, 0, :], "ost")
        k.dma("sync", out[112:112 + 1920, :].rearrange("(t p) d -> p t d", p=128), h[:, 1:16, :], "ost")
        k.dma("sync", out[2032:2048, :], h[0:16, 16, :], "ost")
        fw = ["ost"] + (["hdbg"] if dbg and "hdbg" in dbg else []) + (["mixdbg"] if dbg and "mixdbg" in dbg else [])
        P.emit(final_waits=fw)
    return nc, P


def _masks():
    m = np.zeros((128, 8, 128), np.float32)
    j = np.arange(128)[:, None]
    t = np.arange(128)[None, :]
    same = (j // 64) == (t // 64)
    m[:, 0, :] = (j == t)
    m[:, 1, :] = same & (j < t)
    m[:, 2, :] = same & (j <= t)
    m[:, 3, :] = same & (j > t)
    m[:, 4, :] = np.where(j <= t, 0.0, -30000.0)
    m[:, 5, :] = 1.0
    m[:, 6, :] = (j <= t)
    m[:, 7, :] = same
    return m


def prep_shared(inp):
    f = lambda a: np.ascontiguousarray(np.asarray(a, dtype=np.float32))
    sh = {}
    for kk_ in ("meta", "norm1_g", "norm2_g", "w_in", "w_o", "ffn_w_in", "ffn_w_out"):
        sh[kk_] = f(inp[kk_])
    cw = f(inp["ffn_conv_w"])
    sh["conv_wp"] = f(cw.reshape(DEPTH, 3, 44, 128).transpose(0, 3, 2, 1))
    sh["conv_bp"] = f(f(inp["ffn_conv_b"]).reshape(DEPTH, 44, 128).transpose(0, 2, 1))
    mu = np.zeros((DEPTH, 15 * 128), np.float32)
    mu[:, :RWC] = f(inp["rw_mu"])
    sh["mu_p"] = f(mu.reshape(DEPTH, 15, 128).transpose(0, 2, 1))
    vecs = np.stack([f(inp["rw_k_k"]), f(inp["rw_k_a"]), f(inp["rw_a0"]), f(inp["rw_r_k"]).reshape(DEPTH, 512)], -1)
    sh["rwvec_p"] = f(vecs.reshape(DEPTH, 4, 128, 4).transpose(0, 2, 1, 3))
    sh["wup_aug"] = f(np.concatenate([f(inp["rw_w_up"]), f(inp["rw_w0"])[:, None, :]], 1))
    sh["a_up"] = f(inp["rw_a_up"])
    sh["g_up"] = f(inp["rw_g_up"])
    sh["gn_w"] = f(inp["rw_gn_w"])
    sh["gn_b"] = f(inp["rw_gn_b"])
    sh["fx_bf"] = f(inp["fx_b_f"])
    sh["fx_qg"] = f(inp["fx_q_g"])
    sh["fx_kg"] = f(inp["fx_k_g"])
    sh["cmask"] = _masks()
    return sh


def kernel(**inputs):
    nc, P = build_program()
    sh = prep_shared(inputs)
    x = np.asarray(inputs["x"], dtype=np.float32)
    in_maps = []
    for b in range(NB):
        m = dict(sh)
        m["x"] = np.ascontiguousarray(x[b])
        in_maps.append(m)
    res = run_bass_kernel_spmd(nc, in_maps, core_ids=list(range(NB)))
    return np.stack([np.asarray(r["out"], dtype=np.float32) for r in res.results], 0)
```

```python
import numpy as np
import concourse.bass as bass
import concourse.mybir as mybir

F32 = mybir.dt.float32
BF16 = mybir.dt.bfloat16
AF = mybir.ActivationFunctionType
ALU = mybir.AluOpType
AX = mybir.AxisListType

_DT_SIZE = {F32: 4, BF16: 2, mybir.dt.float32r: 4}


def _dsize(dt):
    try:
        return _DT_SIZE[dt]
    except KeyError:
        return mybir.dt.size(dt)


def ap_region(ap):
    name = ap.name
    pat = ap.ap
    off = int(ap.offset)
    es = _dsize(ap.dtype)
    sp = str(ap.space)
    if sp == "DRAM":
        lo = off
        hi = off
        for st, cnt in pat:
            if cnt > 1:
                if st >= 0:
                    hi += st * (cnt - 1)
                else:
                    lo += st * (cnt - 1)
        return (name, 0, 1, lo * es, (hi + 1) * es)
    if sp == "PSUM":
        return (name, 0, 128, 0, 2048)
    pstep, pcnt = pat[0]
    if pstep > 0:
        rowlen = pstep
    else:
        rowlen = int(np.prod(ap.tensor.shape[1:])) * _dsize(ap.tensor.dtype) // es
    p0 = off // rowlen
    f0 = off % rowlen
    lo = f0
    hi = f0
    for st, cnt in pat[1:]:
        if cnt > 1:
            if st >= 0:
                hi += st * (cnt - 1)
            else:
                lo += st * (cnt - 1)
    np_ = pcnt if pstep != 0 else 1
    return (name, p0, p0 + np_, lo * es, (hi + 1) * es)


class Op:
    __slots__ = ("eng", "fn", "idx", "cnt_eng", "cnt", "waits", "inc", "clock", "dma_key", "phase")


class Prog:
    ENGS = ("tensor", "vector", "scalar", "gpsimd", "sync")
    CAP = 6000

    def __init__(self, nc):
        self.nc = nc
        self.ops = []
        self.streams = {e: [] for e in self.ENGS}
        self.acc = {}
        self.counts = {}
        self.seen = {e: {} for e in self.ENGS}
        self.op_by = {}

    def add(self, eng, fn, reads=(), writes=(), dma_key=None):
        op = Op()
        op.eng = eng
        op.fn = fn
        op.idx = len(self.ops)
        op.dma_key = dma_key
        op.cnt_eng = ("dma", dma_key) if dma_key is not None else eng
        op.inc = False
        op.phase = getattr(self, 'phase', '')
        need = {}
        rregs = [ap_region(a) for a in reads]
        wregs = [ap_region(a) for a in writes]
        for regs, is_w in ((rregs, False), (wregs, True)):
            for (name, p0, p1, b0, b1) in regs:
                for rec in self.acc.get(name, ()):
                    if rec[1] <= p0 or rec[0] >= p1 or rec[3] <= b0 or rec[2] >= b1:
                        continue
                    if not (is_w or rec[4]):
                        if not (name.startswith("ps") and rec[5] != op.cnt_eng):
                            continue
                    ce, c = rec[5], rec[6]
                    if isinstance(ce, tuple):
                        c = self.counts[ce]
                    if ce == eng and dma_key is None:
                        if eng == "tensor":
                            continue
                    if need.get(ce, 0) < c:
                        need[ce] = c
        seen = self.seen[eng]
        waits = []
        for ce, c in need.items():
            if seen.get(ce, 0) >= c:
                continue
            waits.append((ce, c))
            src = self.op_by[(ce, c)]
            src.inc = True
            for k, v in src.clock.items():
                if seen.get(k, 0) < v:
                    seen[k] = v
            seen[ce] = c
        op.waits = waits
        cnt = self.counts.get(op.cnt_eng, 0) + 1
        self.counts[op.cnt_eng] = cnt
        op.cnt = cnt
        op.clock = dict(seen)
        self.op_by[(op.cnt_eng, cnt)] = op
        for regs, is_w in ((rregs, False), (wregs, True)):
            for (name, p0, p1, b0, b1) in regs:
                lst = self.acc.setdefault(name, [])
                new = []
                for rec in lst:
                    covered = rec[0] >= p0 and rec[1] <= p1 and rec[2] >= b0 and rec[3] <= b1
                    if covered and (is_w or (not rec[4] and rec[5] == op.cnt_eng)):
                        continue
                    new.append(rec)
                new.append((p0, p1, b0, b1, is_w, op.cnt_eng, cnt, op.idx))
                self.acc[name] = new
        self.ops.append(op)
        self.streams[eng].append(op)
        return op

    def pe(self, fn, reads, writes):
        return self.add("tensor", fn, reads, writes)

    def dve(self, fn, reads, writes):
        return self.add("vector", fn, reads, writes)

    def act(self, fn, reads, writes):
        return self.add("scalar", fn, reads, writes)

    def pool(self, fn, reads, writes):
        return self.add("gpsimd", fn, reads, writes)

    def dma(self, queue, out, in_, key, **kw):
        return self.add(queue, lambda e: e.dma_start(out=out, in_=in_, **kw), [in_], [out], dma_key=key)

    def emit(self, final_waits=()):
        nc = self.nc
        incs = {}
        for op in self.ops:
            if op.dma_key is not None or op.inc:
                incs.setdefault(op.cnt_eng, []).append(op.cnt)
        import contextlib
        with contextlib.ExitStack() as st:
            semtab = {}
            valmap = {}
            nsem = 0
            for ce, lst in incs.items():
                is_dma = isinstance(ce, tuple)
                cap = 10 ** 9 if is_dma else self.CAP
                step = 16 if is_dma else 1
                sems = []
                for i, c in enumerate(lst):
                    si = i // cap
                    if si >= len(sems):
                        sems.append(st.enter_context(nc.semaphore("s%d" % nsem)))
                        nsem += 1
                    valmap[(ce, c)] = (sems[si], (i % cap + 1) * step)
                semtab[ce] = sems
            self.nsem = nsem
            block = st.enter_context(nc.Block())

            def make(engname):
                ops = self.streams[engname]

                def body(e):
                    for op in ops:
                        for w in op.waits:
                            s, v = valmap[w]
                            e.wait_ge(s, v)
                        ins = op.fn(e)
                        if op.dma_key is not None:
                            s, v = valmap[(op.cnt_eng, op.cnt)]
                            ins.then_inc(s, 16)
                        elif op.inc:
                            s, v = valmap[(op.cnt_eng, op.cnt)]
                            ins.then_inc(s, 1)
                    if engname == "sync":
                        for key in final_waits:
                            ce = ("dma", key)
                            c = self.counts[ce]
                            s, v = valmap[(ce, c)]
                            e.wait_ge(s, v)
                return body

            for en in self.ENGS:
                getattr(block, en)(make(en))

import contextlib
from concourse.bass_utils import run_bass_kernel_spmd

F32R = F32
D = 1024
SEQ = 2048
NMETA = 16
LP = 2176
NT = 17
DEPTH = 4
RWC = 1824
FXC = 2056
DFF = 2816
EPS = 1e-6
GN_EPS = 64e-5
NB = 8
RWKV_ENABLED = True
import os
RW_STAGE = int(os.environ.get('RW_STAGE', '9'))


class K:
    def __init__(self, P):
        self.P = P

    def mm(self, out, lhsT, rhs, start=True, stop=True):
        self.P.pe(lambda e: e.matmul(out, lhsT, rhs, start=start, stop=stop), [lhsT, rhs], [out])

    def tr(self, out, in_, ident):
        self.P.pe(lambda e: e.transpose(out, in_, ident), [in_, ident], [out])

    def act(self, out, in_, func, bias=None, scale=None, accum=None):
        kw = {}
        reads = [in_]
        writes = [out]
        if bias is not None:
            kw["bias"] = bias
            if not isinstance(bias, (int, float)):
                reads.append(bias)
        if scale is not None:
            kw["scale"] = scale
            if not isinstance(scale, (int, float)):
                reads.append(scale)
        if accum is not None:
            kw["accum_out"] = accum
            writes.append(accum)
        self.P.act(lambda e: e.activation(out, in_, func, **kw), reads, writes)

    def tt(self, eng, out, in0, in1, op):
        self.P.add(eng, lambda e: e.tensor_tensor(out, in0, in1, op), [in0, in1], [out])

    def ts(self, eng, out, in0, s1, s2, op0, op1=None):
        reads = [in0]
        for s in (s1, s2):
            if s is not None and not isinstance(s, (int, float)):
                reads.append(s)
        if op1 is None:
            self.P.add(eng, lambda e: e.tensor_scalar(out, in0, s1, None, op0), reads, [out])
        else:
            self.P.add(eng, lambda e: e.tensor_scalar(out, in0, s1, s2, op0, op1), reads, [out])

    def stt(self, out, in0, scalar, in1, op0, op1):
        reads = [in0, in1]
        if not isinstance(scalar, (int, float)):
            reads.append(scalar)
        self.P.dve(lambda e: e.scalar_tensor_tensor(out, in0, scalar, in1, op0, op1), reads, [out])

    def copy(self, eng, out, in_):
        if eng == "scalar":
            self.P.act(lambda e: e.copy(out, in_), [in_], [out])
        else:
            self.P.add(eng, lambda e: e.tensor_copy(out, in_), [in_], [out])

    def memset(self, eng, ap, val):
        self.P.add(eng, lambda e: e.memset(ap, val), [], [ap])

    def recip(self, out, in_):
        self.P.dve(lambda e: e.reciprocal(out, in_), [in_], [out])

    def reduce(self, out, in_, op=ALU.add):
        self.P.dve(lambda e: e.tensor_reduce(out, in_, AX.X, op), [in_], [out])

    def dma(self, q, out, in_, key, **kw):
        self.P.dma(q, out, in_, key, **kw)


def mixer(env, l):
    k = env["k"]; P = env["P"]; Carver = env["Carver"]; pss = env["pss"]; XT = env["XT"]
    cm = env["cm"]; ident_bf = env["ident_bf"]; negm_bf = env["negm_bf"]; small = env["small"]
    w_in = env["w_in"]; fx_bf = env["fx_bf"]; fx_qg = env["fx_qg"]; fx_kg = env["fx_kg"]
    qa_s = env["qa_s"]; ka_s = env["ka_s"]; sgo_s = env["sgo_s"]; norm1_g = env["norm1_g"]
    P.phase = 'norm1'
    env["norm_T"](norm1_g[l:l + 1, :])
    c = Carver()
    c.off = 2600
    vaug = c.take(NT * 8 * 65 // 2 + 4, BF16)[:, 0:NT * 8 * 65].rearrange("p (t h d) -> p t h d", t=NT, h=8, d=65)
    negc = c.take(NT * 8, F32, (NT, 8))
    nlf = c.take(NT * 8, F32, (NT, 8))
    gq = c.take(64)
    gk = c.take(64)
    bfb = c.take(8)
    tmpf = [c.take(512) for _ in range(2)]
    sgst = [c.take(256, BF16) for _ in range(2)]
    persist_off = c.off
    wfx = [c.take(2048, BF16, (8, 512)) for _ in range(2)]
    wfl = c.take(32, BF16, (8, 8))
    qbf = [c.take(256, BF16, (8, 64)) for _ in range(2)]
    qst = [c.take(512, BF16, (8, 128)) for _ in range(2)]
    t8 = small[:, 112:120]
    chl = small[:, 128:136].bitcast(BF16)
    cst = c.take(64, BF16)
    k.dma("sync", gq, fx_qg[l:l + 1, :].partition_broadcast(128), "gq")
    k.dma("sync", gk, fx_kg[l:l + 1, :].partition_broadcast(128), "gk")
    k.dma("sync", bfb, fx_bf[l:l + 1, :].partition_broadcast(128), "bfb")
    k.ts("vector", gq, gq, 0.125, None, ALU.mult)
    k.memset("vector", vaug[:, :, :, 64:65], 1.0)
    wv = w_in[l].rearrange("(kc p) c -> p kc c", p=128)
    P.phase = 'fox_inproj'
    ss8s = [small[:, 96:104], small[:, 144:152]]
    rs8s = [small[:, 104:112], small[:, 152:160]]
    k.dma("gpsimd", wfx[0], wv[:, :, RWC:RWC + 512], "wfx0")
    pend_tr = None
    for ci in range(4):
        wb = wfx[ci % 2]
        if ci + 1 < 4:
            k.dma("gpsimd", wfx[(ci + 1) % 2], wv[:, :, RWC + (ci + 1) * 512:RWC + (ci + 2) * 512], "wfx%d" % ((ci + 1) % 2))
        for i in range(NT):
            ps = pss[2 + (i % 2)]
            for kc in range(8):
                k.mm(ps[:], XT[:, kc, i * 128:(i + 1) * 128], wb[:, kc, :], start=kc == 0, stop=kc == 7)
            if pend_tr is not None:
                pend_tr()
                pend_tr = None
            ps3 = ps[:].rearrange("p (h d) -> p h d", d=64)
            if ci < 2:
                tf = tmpf[i % 2]
                tf3 = tf.rearrange("p (h d) -> p h d", d=64)
                ss8 = ss8s[i % 2]; rs8 = rs8s[i % 2]
                k.act(tf, ps[:], AF.Square)
                k.reduce(ss8, tf3)
                k.ts("vector", ss8, ss8, 1.0 / 64, EPS, ALU.mult, ALU.add)
                k.act(ss8, ss8, AF.Sqrt)
                k.recip(rs8, ss8)
                k.tt("vector", tf3, ps3, rs8.unsqueeze(2).to_broadcast([128, 8, 64]), ALU.mult)
                g = gq if ci == 0 else gk
                qb = qbf[i % 2]
                k.tt("gpsimd", qb, tf3, g.unsqueeze(1).to_broadcast([128, 8, 64]), ALU.mult)

                def do_tr(i=i, ci=ci, qb=qb):
                    pt = pss[4 + (i % 2)][:].bitcast(BF16).rearrange("p (a b) -> p a b", b=128)[0:64, 0:8, :]
                    for hh in range(8):
                        k.tr(pt[:, hh, :], qb[:, hh, :], ident_bf[:])
                    qs = qst[i % 2]
                    k.copy("scalar", qs[0:64, :, :], pt)
                    dst = (qa_s if ci == 0 else ka_s)[:, 0:64, i * 128:(i + 1) * 128].rearrange("h r t -> r h t")
                    k.dma("sync", dst, qs[0:64, :, :], "qst%d" % (i % 2))
                pend_tr = do_tr
            elif ci == 2:
                k.copy("scalar", vaug[:, i, :, 0:64], ps3)
            else:
                sg = sgst[i % 2]
                k.act(sg, ps[:], AF.Sigmoid)
                k.dma("sync", sgo_s[i * 128:(i + 1) * 128, :], sg, "sgst%d" % (i % 2))
    if pend_tr is not None:
        pend_tr()
    P.phase = 'fox_fl'
    k.dma("gpsimd", wfl, wv[:, :, RWC + 2048:RWC + 2056], "wfl")
    for i in range(NT):
        ps = pss[2 + (i % 2)]
        for kc in range(8):
            k.mm(ps[:, 0:8], XT[:, kc, i * 128:(i + 1) * 128], wfl[:, kc, :], start=kc == 0, stop=kc == 7)
        k.tt("vector", t8, ps[:, 0:8], bfb, ALU.add)
        k.act(t8, t8, AF.Exp, scale=-1.0)
        k.act(nlf[:, i, :], t8, AF.Ln, bias=1.0)
    for i in range(NT):
        pc = pss[4 + (i % 2)]
        for ip in range(i):
            k.mm(pc[:, 0:8], cm[:, 5, :], nlf[:, ip, :], start=ip == 0, stop=False)
        k.mm(pc[:, 0:8], cm[:, 6, :], nlf[:, i, :], start=i == 0, stop=True)
        k.copy("vector", negc[:, i, :], pc[:, 0:8])
        k.ts("vector", chl[:, 0:8], pc[:, 0:8], -1.0, None, ALU.mult)
        k.stt(chl[:, 8:16], pc[:, 0:8], -1.0, chl[:, 0:8], ALU.mult, ALU.subtract)
        pt = pss[2 + (i % 2)][:].bitcast(BF16)[0:16, 0:128]
        k.tr(pt, chl, ident_bf[:])
        k.copy("vector", cst[0:16, :], pt)
        k.dma("sync", qa_s[:, 64, i * 128:(i + 1) * 128], cst[0:8, :], "cst")
        k.dma("sync", qa_s[:, 65, i * 128:(i + 1) * 128], cst[8:16, :], "cst")
    P.phase = 'rw_inproj'
    if RWKV_ENABLED:
        env["rwkv_inproj"](env, l, persist_off)
    P.phase = 'fox_att'
    c2 = Carver()
    c2.off = persist_off
    qT = [c2.take(LP // 2, BF16) for _ in range(2)]
    kT = [c2.take(LP // 2, BF16) for _ in range(2)]
    Eb = [c2.take(256, BF16) for _ in range(3)]
    zb = c2.take(256, BF16)
    yfx = c2.take(NT * 512 // 2, BF16, (NT, 512))
    rd = small[:, 140:141]
    k.memset("vector", zb, 0.0)
    ecnt = 0
    for hh in range(8):
        q_ = qT[hh % 2]
        k_ = kT[hh % 2]
        k.dma("sync", q_[0:66, :], qa_s[hh], "qT%d" % (hh % 2))
        k.dma("sync", k_[0:66, :], ka_s[hh], "kT%d" % (hh % 2))
        for qt in range(5):
            q0 = qt * 512
            Nq = min(512, LP - q0)
            nqb = Nq // 128
            Ob = pss[6 + (qt % 2)]
            Oacc = Ob[:, 0:260].rearrange("p (a d) -> p a d", d=65)
            k.mm(Ob[:, 0:260], zb[:, 0:128], zb[:, 0:260], start=True, stop=True)
            kb_max = (q0 + Nq) // 128 - 1

            def scores(kb):
                j = kb - 4 * qt
                cs = max(0, j) * 128
                nonlocal ecnt
                pS = pss[ecnt % 2]
                E = Eb[ecnt % 3]
                ecnt += 1
                k.mm(pS[:, cs:Nq], k_[0:66, kb * 128:(kb + 1) * 128], q_[0:66, q0 + cs:q0 + Nq], start=True, stop=j < 0)
                if j >= 0:
                    k.mm(pS[:, cs:cs + 128], ident_bf[:], negm_bf[:], start=False, stop=True)
                return (kb, j, cs, pS, E)

            pend = scores(0)
            for kb in range(kb_max + 1):
                cur = pend
                pend = scores(kb + 1) if kb + 1 <= kb_max else None
                _, j, cs, pS, E = cur
                k.act(E[:, cs:Nq], pS[:, cs:Nq], AF.Exp, bias=negc[:, kb, hh:hh + 1])
                for qbl in range(max(0, j), nqb):
                    P.pe(lambda e, o=Oacc[:, qbl, :], a=E[:, qbl * 128:(qbl + 1) * 128], b=vaug[:, kb, hh, :]:
                         e.matmul(o, a, b, start=False, stop=True, skip_group_check=True),
                         [E[:, qbl * 128:(qbl + 1) * 128], vaug[:, kb, hh, :]], [Oacc[:, qbl, :]])
            for qbl in range(nqb):
                i = 4 * qt + qbl
                k.recip(rd, Oacc[:, qbl, 64:65])
                k.ts("vector", yfx[:, i, hh * 64:(hh + 1) * 64], Oacc[:, qbl, 0:64], rd, None, ALU.mult)
    P.phase = 'fox_gate'
    for i in range(NT):
        sg = sgst[i % 2]
        k.dma("sync", sg, sgo_s[i * 128:(i + 1) * 128, :], "sgld%d" % (i % 2))
        yg = tmpf[i % 2].bitcast(BF16)[:, 0:512]
        k.tt("vector", yg, yfx[:, i, :], sg, ALU.mult)
        pt = pss[2 + (i % 2)][:].bitcast(BF16).rearrange("p (a b) -> p a b", b=128)[:, 0:4, :]
        for cc in range(4):
            k.tr(pt[:, cc, :], yg[:, cc * 128:(cc + 1) * 128], ident_bf[:])
        k.copy("scalar", XT[:, 4:8, i * 128:(i + 1) * 128], pt)


def rwkv_inproj(env, l, off):
    k = env["k"]; P = env["P"]; Carver = env["Carver"]; pss = env["pss"]; XT = env["XT"]
    cm = env["cm"]; ident_bf = env["ident_bf"]; ident_f = env["ident_f"]; ident_r = env["ident_r"]
    w_in = env["w_in"]; rw_s = env["rw_s"]; g_s = env["g_s"]
    mu_p = env["mu_p"]; rwvec_p = env["rwvec_p"]; wup_aug = env["wup_aug"]; a_up = env["a_up"]
    g_up = env["g_up"]; gn_w = env["gn_w"]; gn_b = env["gn_b"]
    wv = w_in[l].rearrange("(kc p) c -> p kc c", p=128)
    c = Carver()
    c.off = off
    mu = c.take(16)[:, 0:15]
    omu = c.take(16)[:, 0:15]
    k.dma("sync", mu, mu_p[l], "mu")
    k.ts("vector", omu, mu, -1.0, 1.0, ALU.mult, ALU.add)
    wrw = [c.take(512, BF16, (8, 128)) for _ in range(2)]
    PA = [c.take(LP) for _ in range(2)]
    PB = [c.take(LP + 2) for _ in range(2)]
    sh = [c.take(LP) for _ in range(2)]
    for b_ in range(2):
        k.memset("vector", PB[b_][:, 0:1], 0.0)
    def load_rw(m):
        nco_ = 128 if m < 14 else 32
        k.dma("gpsimd", wrw[m % 2][:, :, 0:nco_], wv[:, :, m * 128:m * 128 + nco_], "wrw%d" % (m % 2))

    load_rw(0)
    for m in range(15):
        nco = 128 if m < 14 else 32
        wb = wrw[m % 2]
        if m + 1 < 15:
            load_rw(m + 1)
        pa = PA[m % 2]
        pb = PB[m % 2]
        for n in range(5):
            c0 = n * 512
            N = min(512, LP - c0)
            ps = pss[2 + (n % 2)]
            for kc in range(8):
                k.mm(ps[0:nco, 0:N], wb[:, kc, 0:nco], XT[:, kc, c0:c0 + N], start=kc == 0, stop=kc == 7)
            k.act(pa[0:nco, c0:c0 + N], ps[0:nco, 0:N], AF.Copy, scale=omu[0:nco, m:m + 1])
            k.act(pb[0:nco, 1 + c0:1 + c0 + N], ps[0:nco, 0:N], AF.Copy, scale=mu[0:nco, m:m + 1])
        so = sh[m % 2]
        k.tt("vector" if m % 2 else "gpsimd", so[0:nco, :], pa[0:nco, :], pb[0:nco, 0:LP], ALU.add)
        k.dma("sync", rw_s[m, 0:nco, :], so[0:nco, :], "sh%d" % (m % 2))


def rwkv(env, l):
    k = env["k"]; P = env["P"]; Carver = env["Carver"]; pss = env["pss"]; XT = env["XT"]
    cm = env["cm"]; ident_bf = env["ident_bf"]; ident_f = env["ident_f"]; ident_r = env["ident_r"]
    rw_s = env["rw_s"]; g_s = env["g_s"]
    rwvec_p = env["rwvec_p"]; wup_aug = env["wup_aug"]; a_up = env["a_up"]
    g_up = env["g_up"]; gn_w = env["gn_w"]; gn_b = env["gn_b"]
    P.phase = 'rw_lora'
    c = Carver()
    TW = c.take(LP, F32R)
    AD = c.take(LP, F32R)
    SG1 = c.take(LP // 2, BF16)
    SG2 = c.take(LP // 2, BF16)
    wup = c.take(512, F32R)
    aup = c.take(512, F32R)
    gup1 = c.take(256, BF16)
    gup2 = c.take(256, BF16)
    vec = c.take(16, F32, (4, 4))
    omka = c.take(4)
    hsel = c.take(2, F32R)
    hself = c.take(2)
    NEGE = -float(np.exp(-0.5))
    tric = c.take(256, F32, (2, 128))
    k.ts("vector", tric[:, 0, :], cm[:, 2, :], NEGE, None, ALU.mult)
    k.ts("vector", tric[:, 1, :], cm[:, 1, :], NEGE, None, ALU.mult)
    lt = c.take(LP)
    if RW_STAGE < 2:
        return
    k.dma("sync", vec, rwvec_p[l], "vec")
    k.ts("vector", omka, vec[:, :, 1], -1.0, 1.0, ALU.mult, ALU.add)

    k.copy("vector", hself[:, 0:1], cm[:, 7, 0:1])
    k.copy("vector", hself[:, 1:2], cm[:, 7, 127:128])
    k.copy("vector", hsel, hself)
    k.dma("sync", lt[0:128, :], rw_s[12], "lt")
    k.act(TW[0:64, :], lt[0:64, :], AF.Tanh)
    k.ts("vector", TW[64:128, :], lt[64:128, :], 0.0, 1.0, ALU.mult, ALU.add)
    k.copy("vector", AD, lt)
    lt2 = c.take(LP)
    k.dma("sync", lt2[0:128, :], rw_s[13], "lt2")
    k.act(SG1, lt2, AF.Sigmoid)
    lt3 = c.take(LP)
    k.dma("sync", lt3[0:32, :], rw_s[14, 0:32, :], "lt3")
    k.act(SG2[0:32, :], lt3[0:32, :], AF.Sigmoid)
    wtmp = c.take(512)
    k.dma("sync", wtmp[0:65, :], wup_aug[l], "wtmp")
    k.memset("vector", wup[64:128, :], 0.0)
    k.copy("vector", wup[0:65, :], wtmp[0:65, :])
    wtmp2 = c.take(512)
    k.dma("sync", wtmp2[64:128, :], a_up[l], "wtmp2")
    k.memset("vector", aup[0:64, :], 0.0)
    k.copy("vector", aup[64:128, :], wtmp2[64:128, :])
    k.dma("gpsimd", gup1, g_up[l, 0:128, :], "gup1")
    k.dma("gpsimd", gup2[0:32, :], g_up[l, 128:160, :], "gup2")
    gst = [wtmp, wtmp2]
    for i in range(NT):
        ps = pss[2 + (i % 2)]
        k.mm(ps[:], SG1[:, i * 128:(i + 1) * 128], gup1, start=True, stop=False)
        k.mm(ps[:], SG2[0:32, i * 128:(i + 1) * 128], gup2[0:32, :], start=False, stop=True)
        k.copy("scalar", gst[i % 2], ps[:])
        k.dma("sync", g_s[i * 128:(i + 1) * 128, :], gst[i % 2], "gst%d" % (i % 2))
    per_off = c.off - 3 * LP - 1024
    if RW_STAGE < 3:
        return
    RD = BF16

    def tk(n):
        return c.take(n // 2, RD)

    def run_interleaved(gens):
        active = list(gens)
        while active:
            for g in list(active):
                try:
                    next(g)
                except StopIteration:
                    active.remove(g)

    def interleave_gen(gens):
        active = list(gens)
        while active:
            for g in list(active):
                try:
                    next(g)
                    yield
                except StopIteration:
                    active.remove(g)

    P.phase = 'rw_pair'
    c = Carver()
    c.off = per_off
    pair_bufs = []
    for ps_i in range(2):
        B = {}
        B["RKV"] = c.take(384, F32, (3, 128))
        B["AS"] = c.take(128); B["SIGT"] = c.take(128); B["E13"] = c.take(256); B["E2"] = c.take(128)
        for nm in ("kk", "kk2", "rin", "kkn", "t1", "kp", "bb", "rk"):
            B[nm] = c.take(128)
        B["bT"] = tk(128); B["kT"] = tk(128)
        B["bTh"] = [tk(128) for _ in range(2)]
        B["kTh"] = [tk(128) for _ in range(2)]
        B["A0T"] = [tk(128) for _ in range(2)]
        B["Zb"] = [[tk(384), tk(384)] for _ in range(2)]
        B["sets"] = []
        for _ in range(2):
            B["sets"].append(dict(AR=tk(256), v_tok=tk(128), btok=[tk(128), tk(128)], ktok=[tk(128), tk(128)],
                                  GM=c.take(4), AB=[tk(256), tk(256)], AK=[tk(256), tk(256)], T=[tk(128), tk(128)],
                                  vtf=c.take(128), coef=c.take(2), y=c.take(128), g=c.take(128)))
        B["hd"] = [dict(W=tk(64), U=tk(64), S16=tk(64), S32=c.take(64), S32g=c.take(64)) for _ in range(2)]
        B["gnw"] = c.take(128); B["gnb"] = c.take(128)
        B["sq"] = c.take(128); B["yn"] = c.take(128); B["yb"] = tk(128)
        B["st"] = c.take(16)
        B["banks"] = (pss[4 * ps_i], pss[4 * ps_i + 1], pss[4 * ps_i + 2], pss[4 * ps_i + 3])
        pair_bufs.append(B)

    def pair_gen(hp):
        B = pair_bufs[hp % 2]
        tag = "_%d" % (hp % 2)
        RKV = B["RKV"]; AS = B["AS"]; SIGT = B["SIGT"]; E13 = B["E13"]; E2 = B["E2"]
        kk = B["kk"]; kk2 = B["kk2"]; rin = B["rin"]; kkn = B["kkn"]; t1 = B["t1"]; kp = B["kp"]; bb = B["bb"]; rk = B["rk"]
        bT = B["bT"]; kT = B["kT"]; bTh = B["bTh"]; kTh = B["kTh"]; A0T = B["A0T"]; Zb = B["Zb"]
        sets = B["sets"]; hd_b = B["hd"]; gnw = B["gnw"]; gnb = B["gnb"]
        X0, X1, XS0, XS1 = B["banks"]
        pA = X0[:, 0:256]; pC = X0[:, 256:512]
        pT = X1
        pM = X0; pM2 = X1
        pIs = [X0, X1]
        pSs = [XS0[:, 0:256], XS1[:, 0:256]]
        for d in hd_b:
            for nm in ("W", "U", "S16", "S32"):
                k.memset("vector", d[nm], 0.0)
        k.dma("sync", gnw, gn_w[l:l + 1, hp * 128:(hp + 1) * 128].partition_broadcast(128), "gnw" + tag)
        k.dma("sync", gnb, gn_b[l:l + 1, hp * 128:(hp + 1) * 128].partition_broadcast(128), "gnb" + tag)
        kk_s = vec[:, hp, 0:1]; ka_s_ = vec[:, hp, 1:2]; a0_s = vec[:, hp, 2:3]; rk_s = vec[:, hp, 3:4]
        yield

        def prep(i):
            S_ = sets[i % 2]
            AR = S_["AR"]; v_tok = S_["v_tok"]; btok = S_["btok"]; ktok = S_["ktok"]; GM = S_["GM"]
            t0 = i * 128
            for q in range(3):
                k.dma("sync", RKV[:, q, :], rw_s[4 * q + hp, :, t0:t0 + 128], "rkv%d" % q + tag)
            k.dma("sync", S_["g"], g_s[t0:t0 + 128, hp * 128:(hp + 1) * 128], "gt%d" % (i % 2) + tag)
            r_ = RKV[:, 0, :]; k_ = RKV[:, 1, :]; v_ = RKV[:, 2, :]
            k.mm(pA[:, 128:256], aup[:, hp * 128:(hp + 1) * 128], AD[:, t0:t0 + 128])
            k.mm(pA[:, 0:128], TW[:, t0:t0 + 128], wup[:, hp * 128:(hp + 1) * 128])
            k.act(AS, pA[:, 128:256], AF.Sigmoid, bias=a0_s)
            k.act(SIGT, pA[:, 0:128], AF.Sigmoid)
            yield
            k.mm(pC[:, 0:128], SIGT, tric[:, 0, :])
            k.mm(pC[:, 128:256], SIGT, tric[:, 1, :])
            k.act(E13, pC[:, 0:256], AF.Exp)
            k.act(E2, pC[:, 0:128], AF.Exp, scale=-1.0)
            yield
            E1 = E13[:, 0:128]; E3 = E13[:, 128:256]
            k.ts("vector", kk, k_, kk_s, None, ALU.mult)
            k.tt("gpsimd", kk2, kk, kk, ALU.mult)
            k.mm(pT[:, 384:512], cm[:, 7, :], kk2)
            k.act(rin, pT[:, 384:512], AF.Sqrt)
            yield
            k.ts("vector", rin, rin, 1e-12, None, ALU.max)
            k.recip(rin, rin)
            k.tt("vector", kkn, kk, rin, ALU.mult)
            k.ts("vector", t1, AS, ka_s_, omka[:, hp:hp + 1], ALU.mult, ALU.add)
            yield
            k.tt("gpsimd", kp, k_, t1, ALU.mult)
            k.tt("gpsimd", bb, kkn, AS, ALU.mult)
            k.tt("vector", AR[:, 128:256], r_, E1, ALU.mult)
            k.tt("gpsimd", t1, kkn, E3, ALU.mult)
            yield
            k.ts("vector", AR[:, 0:128], t1, -1.0, None, ALU.mult)
            k.tt("vector", bT, bb, E2, ALU.mult)
            k.tt("vector", kT, kp, E2, ALU.mult)
            k.tt("gpsimd", rk, r_, kp, ALU.mult)
            yield
            k.ts("vector", rk, rk, rk_s, None, ALU.mult)
            for hd in range(2):
                for cc in range(2):
                    k.ts("vector", GM[:, hd * 2 + cc:hd * 2 + cc + 1], hself[:, hd:hd + 1],
                         E13[:, 63 + 64 * cc:64 + 64 * cc], None, ALU.mult)
            yield
            k.mm(pT[:, 0:128], v_, ident_f[:])
            k.mm(pT[:, 128:256], bT, ident_bf[:])
            k.mm(pT[:, 256:384], kT, ident_bf[:])
            k.mm(pT[:, 384:386], rk, hself)
            k.copy("scalar", v_tok, pT[:, 0:128])
            k.copy("scalar", S_["vtf"], pT[:, 0:128])
            yield
            k.copy("scalar", S_["coef"], pT[:, 384:386])
            for cc in range(2):
                k.act(btok[cc], pT[:, 128:256], AF.Copy, scale=hself[:, cc:cc + 1])
                k.act(ktok[cc], pT[:, 256:384], AF.Copy, scale=hself[:, cc:cc + 1])
            yield
            for hd in range(2):
                k.ts("vector", bTh[hd], bT, hself[:, hd:hd + 1], None, ALU.mult)
                k.ts("vector", kTh[hd], kT, hself[:, hd:hd + 1], None, ALU.mult)
                k.mm(pM[:, 0:256], bTh[hd], AR)
                k.mm(pM[:, 256:384], AR[:, 0:128], bTh[hd])
                k.mm(pM2[:, 0:256], kTh[hd], AR)
                yield
                k.tt("vector", S_["AB"][hd], pM[:, 0:256], cm[:, 1:3, :].rearrange("p a b -> p (a b)"), ALU.mult)
                k.tt("vector", A0T[hd], pM[:, 256:384], cm[:, 3, :], ALU.mult)
                k.tt("vector", S_["AK"][hd], pM2[:, 0:256], cm[:, 1:3, :].rearrange("p a b -> p (a b)"), ALU.mult)
                yield
            Zc = []
            for hd in range(2):
                Y = S_["AB"][hd][:, 0:128]; YT = A0T[hd]
                Z = Zb[hd][0]
                pI = pIs[hd]
                k.mm(pI[:, 128:256], YT, Y)
                k.mm(pI[:, 256:384], Y, YT)
                k.tt("gpsimd", Z[:, 0:128], Y, ident_bf[:], ALU.add)
                k.copy("scalar", Z[:, 128:384], pI[:, 128:384])
                Zc.append(Z)
            yield
            for lev in range(1, 5):
                for hd in range(2):
                    Z = Zc[hd]
                    Zn = Zb[hd][lev % 2]
                    pI = pIs[hd]
                    k.mm(pI[:, 0:256], Z[:, 256:384], Z[:, 0:256])
                    k.mm(pI[:, 256:384], Z[:, 128:256], Z[:, 256:384])
                    k.tt("vector", Zn[:, 0:128], pI[:, 0:128], Z[:, 0:128], ALU.add)
                    k.copy("scalar", Zn[:, 128:384], pI[:, 128:384])
                    Zc[hd] = Zn
                    yield
            for hd in range(2):
                Z = Zc[hd]
                pI = pIs[hd]
                k.mm(pI[:, 0:128], Z[:, 256:384], Z[:, 0:128])
                k.tt("vector", S_["T"][hd], pI[:, 0:128], Z[:, 0:128], ALU.add)
            yield

        def seq(i):
            S_ = sets[i % 2]
            AR = S_["AR"]; v_tok = S_["v_tok"]; btok = S_["btok"]; ktok = S_["ktok"]; GM = S_["GM"]
            y_t = S_["y"]
            t0 = i * 128
            for cc in range(2):
                rs = slice(cc * 64, cc * 64 + 64)
                for hd in range(2):
                    d = hd_b[hd]
                    vh = v_tok[:, hd * 64:(hd + 1) * 64]
                    gm = GM[:, hd * 2 + cc:hd * 2 + cc + 1]
                    pS = pSs[hd]
                    k.mm(pS[:, 0:64], AR[:, 0:128], d["S16"], start=True, stop=False)
                    k.mm(pS[:, 0:64], S_["AK"][hd][:, 0:128], vh, start=False, stop=True)
                    k.act(d["S32g"], d["S32"], AF.Copy, scale=gm)
                    k.copy("scalar", d["W"][rs, :], pS[rs, 0:64])
                    yield
                for hd in range(2):
                    d = hd_b[hd]
                    pS = pSs[hd]
                    k.mm(pS[:, 64:128], S_["T"][hd], d["W"])
                    k.copy("scalar", d["U"][rs, :], pS[rs, 64:128])
                    yield
                for hd in range(2):
                    d = hd_b[hd]
                    vh = v_tok[:, hd * 64:(hd + 1) * 64]
                    gm = GM[:, hd * 2 + cc:hd * 2 + cc + 1]
                    pS = pSs[hd]
                    k.mm(pS[:, 192:256], btok[cc], d["U"], start=True, stop=False)
                    k.mm(pS[:, 192:256], ktok[cc], vh, start=False, stop=True)
                    k.mm(pS[:, 128:192], AR[:, 128:256], d["S16"], start=True, stop=False)
                    k.mm(pS[:, 128:192], S_["AB"][hd][:, 128:256], d["U"], start=False, stop=False)
                    k.mm(pS[:, 128:192], S_["AK"][hd][:, 128:256], vh, start=False, stop=True)
                    k.stt(d["S32"], pS[:, 192:256], gm, d["S32g"], ALU.mult, ALU.add)
                    k.copy("vector", d["S16"], d["S32"])
                    k.copy("scalar", y_t[rs, hd * 64:hd * 64 + 64], pS[rs, 128:192])
                    yield
            st = B["st"]; sq = B["sq"]; yn = B["yn"]; yb = B["yb"]
            s1 = st[:, 0:2]; s2 = st[:, 2:4]; mn = st[:, 4:6]; m2 = st[:, 6:8]
            y3 = y_t.rearrange("p (a d) -> p a d", d=64)
            sq3 = sq.rearrange("p (a d) -> p a d", d=64)
            yn3 = yn.rearrange("p (a d) -> p a d", d=64)
            vt3 = S_["vtf"].rearrange("p (a d) -> p a d", d=64)
            k.reduce(s1, y3)
            k.tt("gpsimd", sq, y_t, y_t, ALU.mult)
            k.reduce(s2, sq3)
            k.ts("vector", mn, s1, 1.0 / 64, None, ALU.mult)
            yield
            k.tt("vector", m2, mn, mn, ALU.mult)
            k.ts("vector", s2, s2, 1.0 / 64, None, ALU.mult)
            k.tt("vector", s2, s2, m2, ALU.subtract)
            k.ts("vector", s2, s2, GN_EPS, None, ALU.add)
            k.act(s2, s2, AF.Sqrt)
            k.recip(s2, s2)
            yield
            k.tt("vector", yn3, y3, mn.unsqueeze(2).to_broadcast([128, 2, 64]), ALU.subtract)
            k.tt("vector", yn3, yn3, s2.unsqueeze(2).to_broadcast([128, 2, 64]), ALU.mult)
            k.tt("gpsimd", sq3, vt3, S_["coef"].unsqueeze(2).to_broadcast([128, 2, 64]), ALU.mult)
            k.tt("vector", yn, yn, gnw, ALU.mult)
            yield
            k.tt("vector", yn, yn, gnb, ALU.add)
            k.tt("vector", yn, yn, sq, ALU.add)
            k.tt("vector", yb, yn, S_["g"], ALU.mult)
            pt = X1[:].bitcast(BF16)[:, 768:896]
            k.tr(pt, yb, ident_bf[:])
            k.copy("scalar", XT[:, hp, t0:t0 + 128], pt)
            yield

        yield from prep(0)
        for i in range(NT):
            gens = [seq(i)]
            if i + 1 < NT:
                gens.append(prep(i + 1))
            yield from interleave_gen(gens)

    for hp0 in (0, 2):
        for _ in interleave_gen([pair_gen(hp0), pair_gen(hp0 + 1)]):
            pass


def build_program(nlayers=DEPTH, dbg=None, stages=("mix", "ffn")):
    nc = bass.Bass("TRN2", target_bir_lowering=False)
    dram = {}

    def din(name, shape, dt=F32):
        dram[name] = nc.dram_tensor(name, list(shape), dt, kind="ExternalInput").ap()
        return dram[name]

    def dscr(name, shape, dt=F32):
        kind = "ExternalOutput" if (dbg and name in dbg) else "Internal"
        dram[name] = nc.dram_tensor(name, list(shape), dt, kind=kind).ap()
        return dram[name]

    x = din("x", [SEQ, D])
    meta = din("meta", [NMETA, D])
    norm1_g = din("norm1_g", [DEPTH, D])
    norm2_g = din("norm2_g", [DEPTH, D])
    w_in = din("w_in", [DEPTH, D, RWC + FXC])
    w_o = din("w_o", [DEPTH, D, D])
    ffn_w_in = din("ffn_w_in", [DEPTH, D, 2 * DFF])
    ffn_w_out = din("ffn_w_out", [DEPTH, DFF, D])
    conv_wp = din("conv_wp", [DEPTH, 128, 44, 3])
    conv_bp = din("conv_bp", [DEPTH, 128, 44])
    mu_p = din("mu_p", [DEPTH, 128, 15])
    rwvec_p = din("rwvec_p", [DEPTH, 128, 4, 4])
    wup_aug = din("wup_aug", [DEPTH, 65, 512])
    a_up = din("a_up", [DEPTH, 64, 512])
    g_up = din("g_up", [DEPTH, 160, 512])
    gn_w = din("gn_w", [DEPTH, 512])
    gn_b = din("gn_b", [DEPTH, 512])
    fx_bf = din("fx_bf", [DEPTH, 8])
    fx_qg = din("fx_qg", [DEPTH, 64])
    fx_kg = din("fx_kg", [DEPTH, 64])
    cmask = din("cmask", [128, 8, 128])
    out = nc.dram_tensor("out", [SEQ, D], F32, kind="ExternalOutput").ap()

    rw_s = dscr("rw_s", [15, 128, LP])
    g_s = dscr("g_s", [LP, 512])
    qa_s = dscr("qa_s", [NB, 66, LP], BF16)
    ka_s = dscr("ka_s", [NB, 66, LP], BF16)
    sgo_s = dscr("sgo_s", [LP, 512], BF16)
    hdbg = dscr("hdbg", [LP, D]) if dbg else None
    mixdbg = dscr("mixdbg", [128, 8, LP], BF16) if dbg else None

    with contextlib.ExitStack() as st:
        def sb(name, shape, dt=F32):
            return st.enter_context(nc.sbuf_tensor(name, list(shape), dt))

        h = sb("h", [128, NT, D])
        XT = sb("XT", [128, 8, LP], BF16)
        ident_bf = sb("ident_bf", [128, 128], BF16)
        ident_f = sb("ident_f", [128, 128])
        ident_r = ident_f
        cm = sb("cm", [128, 8, 128])
        small = sb("small", [128, 256])
        ARENA_W = 24200
        arena = sb("arena", [128, ARENA_W])
        pss = [st.enter_context(nc.psum_tensor("ps%d" % i, [128, 512], F32)) for i in range(8)]

        P = Prog(nc)
        k = K(P)

        class Carver:
            def __init__(self):
                self.off = 0

            def take(self, nwords, dt=F32, shape=None):
                a = arena[:, self.off:self.off + nwords]
                self.off += nwords
                assert self.off <= ARENA_W, self.off
                if dt != F32:
                    a = a.bitcast(dt)
                if shape is not None:
                    names = " ".join("d%d" % i for i in range(len(shape)))
                    kw = {"d%d" % i: s for i, s in enumerate(shape)}
                    a = a.rearrange("p (%s) -> p %s" % (names, names), **kw)
                return a

        k.dma("sync", cm[:], cmask, "cm")
        k.copy("vector", ident_f[:], cm[:, 0, :])
        k.copy("vector", ident_bf[:], cm[:, 0, :])
        negm_bf = sb("negm_bf", [128, 128], BF16)
        k.copy("vector", negm_bf[:], cm[:, 4, :])

        ones_bf = sb("ones_bf", [8, LP], BF16)
        k.memset("vector", ones_bf[:], 1.0)
        k.dma("sync", ka_s[:, 64, :], ones_bf[:], "kones")
        k.dma("sync", ka_s[:, 65, :], ones_bf[:], "kones")
        k.memset("vector", h[:, NT - 1, :], 0.0)
        k.dma("sync", h[0:16, 0, :], meta, "hload")
        k.dma("sync", h[16:128, 0, :], x[0:112, :], "hload")
        k.dma("sync", h[:, 1:16, :], x[112:112 + 1920, :].rearrange("(t p) d -> p t d", p=128), "hload")
        k.dma("sync", h[0:16, 16, :], x[2032:2048, :], "hload")

        ssq = small[:, 0:17]
        rstd = small[:, 32:49]
        tmp17 = small[:, 64:81]

        def norm_T(g_row):
            c = Carver()
            gbc = c.take(D)
            k.dma("sync", gbc, g_row.partition_broadcast(128), "gbc")
            junk = c.take(512, BF16)
            xn = [c.take(512, BF16), c.take(512, BF16)]
            for i in range(NT):
                k.act(junk, h[:, i, :], AF.Square, accum=ssq[:, i:i + 1])
            k.ts("vector", tmp17, ssq, 1.0 / D, EPS, ALU.mult, ALU.add)
            k.act(tmp17, tmp17, AF.Sqrt)
            k.recip(rstd, tmp17)
            for i in range(NT):
                xb = xn[i % 2]
                k.stt(xb, h[:, i, :], rstd[:, i:i + 1], gbc, ALU.mult, ALU.mult)
                pt = pss[i % 2][:].bitcast(BF16).rearrange("p (a b) -> p a b", b=128)[:, 0:8, :]
                for kc in range(8):
                    k.tr(pt[:, kc, :], xb[:, kc * 128:(kc + 1) * 128], ident_bf[:])
                k.copy("scalar" if i % 2 else "vector", XT[:, :, i * 128:(i + 1) * 128], pt)

        def add_to_h(i, n2, ps):
            k.tt("vector", h[:, i, n2 * 512:(n2 + 1) * 512], ps, h[:, i, n2 * 512:(n2 + 1) * 512], ALU.add)

        def ffn(l):
            P.phase = 'norm2'
            norm_T(norm2_g[l:l + 1, :])
            P.phase = 'ffn'
            c = Carver()
            cw = c.take(44 * 3, F32, (44, 3))
            cb = c.take(44)
            k.dma("sync", cw, conv_wp[l], "cw")
            k.dma("sync", cb, conv_bp[l], "cb")
            groups = [list(range(0, 5)), list(range(5, 10)), list(range(10, 14)), list(range(14, 18)), list(range(18, 22))]
            hid = c.take(5 * LP // 2, BF16, (5, LP))
            wout = c.take(5 * D // 2, BF16, (5, D))
            wu = [c.take(512, BF16, (8, 128)) for _ in range(2)]
            wg = [c.take(512, BF16, (8, 128)) for _ in range(2)]
            NBUF = 3
            HU = [c.take(516) for _ in range(NBUF)]
            HG = [c.take(516) for _ in range(NBUF)]
            t0s = [c.take(512) for _ in range(NBUF)]
            t1s = [c.take(512) for _ in range(NBUF)]
            txs = [c.take(512) for _ in range(2)]
            t2s = [c.take(512) for _ in range(2)]
            txBs = [c.take(512) for _ in range(2)]
            t2Bs = [c.take(512) for _ in range(2)]
            wi = ffn_w_in[l].rearrange("(kc p) c -> p kc c", p=128)
            wo = ffn_w_out[l].rearrange("(kc p) n -> p kc n", p=128)
            cnt = 0
            tcnt = 0
            pend_mul = None

            def load_w(j, cn):
                k.dma("gpsimd", wu[cn % 2], wi[:, :, j * 128:(j + 1) * 128], "wu%d" % (cn % 2))
                k.dma("gpsimd", wg[cn % 2], wi[:, :, DFF + j * 128:DFF + (j + 1) * 128], "wg%d" % (cn % 2))
            for gi, js in enumerate(groups):
                k.dma("gpsimd", wout[:, 0:len(js), :], wo[:, js[0]:js[0] + len(js), :], "wout")
                for jj, j in enumerate(js):
                    wub = wu[cnt % 2]
                    wgb = wg[cnt % 2]
                    if cnt == 0:
                        load_w(0, 0)
                    if j + 1 < 22:
                        load_w(j + 1, cnt + 1)
                    cnt += 1
                    ju = j
                    jg = 22 + j
                    for n in range(5):
                        c0 = n * 512
                        N = min(512, LP - c0)
                        pu = pss[2 + (n % 2)]
                        pg = pss[4 + (n % 2)]
                        b = tcnt % NBUF
                        bn = (tcnt + 1) % NBUF
                        hu = HU[b]; hg = HG[b]; t0 = t0s[b]; t1 = t1s[b]
                        tx = txs[tcnt % 2]; t2 = t2s[tcnt % 2]
                        tcnt += 1
                        for kc in range(8):
                            k.mm(pu[:, 0:N], wub[:, kc, :], XT[:, kc, c0:c0 + N], start=kc == 0, stop=kc == 7)
                        for kc in range(8):
                            k.mm(pg[:, 0:N], wgb[:, kc, :], XT[:, kc, c0:c0 + N], start=kc == 0, stop=kc == 7)
                        if n == 0:
                            k.memset("gpsimd", hu[:, 0:2], 0.0)
                            k.memset("gpsimd", hg[:, 0:2], 0.0)
                        k.copy("scalar", hu[:, 2:2 + N], pu[:, 0:N])
                        k.act(t0[:, 0:N], pu[:, 0:N], AF.Identity, bias=cb[:, ju:ju + 1], scale=cw[:, ju, 2:3])
                        k.copy("scalar", hg[:, 2:2 + N], pg[:, 0:N])
                        k.act(t1[:, 0:N], pg[:, 0:N], AF.Identity, bias=cb[:, jg:jg + 1], scale=cw[:, jg, 2:3])
                        if n < 4:
                            k.copy("gpsimd", HU[bn][:, 0:2], hu[:, N:N + 2])
                            k.copy("gpsimd", HG[bn][:, 0:2], hg[:, N:N + 2])
                        txb = txBs[(tcnt - 1) % 2]; t2b = t2Bs[(tcnt - 1) % 2]
                        k.ts("vector", t2[:, 0:N], hg[:, 1:1 + N], cw[:, jg, 1:2], None, ALU.mult)
                        k.ts("vector", t2b[:, 0:N], hg[:, 0:N], cw[:, jg, 0:1], None, ALU.mult)
                        k.ts("vector", tx[:, 0:N], hu[:, 1:1 + N], cw[:, ju, 1:2], None, ALU.mult)
                        k.ts("vector", txb[:, 0:N], hu[:, 0:N], cw[:, ju, 0:1], None, ALU.mult)
                        k.tt("gpsimd", t1[:, 0:N], t1[:, 0:N], t2[:, 0:N], ALU.add)
                        k.tt("gpsimd", t1[:, 0:N], t1[:, 0:N], t2b[:, 0:N], ALU.add)
                        k.act(t1[:, 0:N], t1[:, 0:N], AF.Silu)
                        k.tt("vector", t0[:, 0:N], t0[:, 0:N], tx[:, 0:N], ALU.add)
                        k.tt("vector", t0[:, 0:N], t0[:, 0:N], txb[:, 0:N], ALU.add)
                        if pend_mul is not None:
                            pend_mul()
                        pend_mul = (lambda jj=jj, c0=c0, N=N, t0=t0, t1=t1:
                                    k.tt("vector", hid[:, jj, c0:c0 + N], t1[:, 0:N], t0[:, 0:N], ALU.mult))
                pend_mul()
                pend_mul = None
                for i in range(NT):
                    for n2 in range(2):
                        ps = pss[6 + ((i * 2 + n2) % 2)]
                        for jj in range(len(js)):
                            k.mm(ps[:], hid[:, jj, i * 128:(i + 1) * 128], wout[:, jj, n2 * 512:(n2 + 1) * 512],
                                 start=jj == 0, stop=jj == len(js) - 1)
                        add_to_h(i, n2, ps[:])

        def out_proj(l):
            P.phase = 'out_proj'
            c = Carver()
            wob = c.take(8 * D // 2, BF16, (8, D))
            k.dma("gpsimd", wob, w_o[l].rearrange("(kc p) n -> p kc n", p=128), "wob")
            for i in range(NT):
                for n2 in range(2):
                    ps = pss[6 + ((i * 2 + n2) % 2)]
                    for kc in range(8):
                        k.mm(ps[:], XT[:, kc, i * 128:(i + 1) * 128], wob[:, kc, n2 * 512:(n2 + 1) * 512],
                             start=kc == 0, stop=kc == 7)
                    add_to_h(i, n2, ps[:])

        env = dict(locals())
        env["rwkv_inproj"] = rwkv_inproj
        for l in range(nlayers):
            if "mix" in stages:
                mixer(env, l)
                if RWKV_ENABLED:
                    rwkv(env, l)
                else:
                    k.memset("gpsimd", XT[:, 0:4, :], 0.0)
                if dbg and "mixdbg" in dbg and l == 0:
                    k.dma("sync", mixdbg, XT[:], "mixdbg")
                out_proj(l)
            if "ffn" in stages:
                ffn(l)

        if dbg and "hdbg" in dbg:
            k.dma("sync", hdbg.rearrange("(t p) d -> p t d", p=128), h[:], "hdbg")
        k.dma("sync", out[0:112, :], h[16:128, 0, :], "ost")
        k.dma("sync", out[112:112 + 1920, :].rearrange("(t p) d -> p t d", p=128), h[:, 1:16, :], "ost")
        k.dma("sync", out[2032:2048, :], h[0:16, 16, :], "ost")
        fw = ["ost"] + (["hdbg"] if dbg and "hdbg" in dbg else []) + (["mixdbg"] if dbg and "mixdbg" in dbg else [])
        P.emit(final_waits=fw)
    return nc, P


def _masks():
    m = np.zeros((128, 8, 128), np.float32)
    j = np.arange(128)[:, None]
    t = np.arange(128)[None, :]
    same = (j // 64) == (t // 64)
    m[:, 0, :] = (j == t)
    m[:, 1, :] = same & (j < t)
    m[:, 2, :] = same & (j <= t)
    m[:, 3, :] = same & (j > t)
    m[:, 4, :] = np.where(j <= t, 0.0, -30000.0)
    m[:, 5, :] = 1.0
    m[:, 6, :] = (j <= t)
    m[:, 7, :] = same
    return m


def prep_shared(inp):
    f = lambda a: np.ascontiguousarray(np.asarray(a, dtype=np.float32))
    sh = {}
    for kk_ in ("meta", "norm1_g", "norm2_g", "w_in", "w_o", "ffn_w_in", "ffn_w_out"):
        sh[kk_] = f(inp[kk_])
    cw = f(inp["ffn_conv_w"])
    sh["conv_wp"] = f(cw.reshape(DEPTH, 3, 44, 128).transpose(0, 3, 2, 1))
    sh["conv_bp"] = f(f(inp["ffn_conv_b"]).reshape(DEPTH, 44, 128).transpose(0, 2, 1))
    mu = np.zeros((DEPTH, 15 * 128), np.float32)
    mu[:, :RWC] = f(inp["rw_mu"])
    sh["mu_p"] = f(mu.reshape(DEPTH, 15, 128).transpose(0, 2, 1))
    vecs = np.stack([f(inp["rw_k_k"]), f(inp["rw_k_a"]), f(inp["rw_a0"]), f(inp["rw_r_k"]).reshape(DEPTH, 512)], -1)
    sh["rwvec_p"] = f(vecs.reshape(DEPTH, 4, 128, 4).transpose(0, 2, 1, 3))
    sh["wup_aug"] = f(np.concatenate([f(inp["rw_w_up"]), f(inp["rw_w0"])[:, None, :]], 1))
    sh["a_up"] = f(inp["rw_a_up"])
    sh["g_up"] = f(inp["rw_g_up"])
    sh["gn_w"] = f(inp["rw_gn_w"])
    sh["gn_b"] = f(inp["rw_gn_b"])
    sh["fx_bf"] = f(inp["fx_b_f"])
    sh["fx_qg"] = f(inp["fx_q_g"])
    sh["fx_kg"] = f(inp["fx_k_g"])
    sh["cmask"] = _masks()
    return sh


def kernel(**inputs):
    nc, P = build_program()
    sh = prep_shared(inputs)
    x = np.asarray(inputs["x"], dtype=np.float32)
    in_maps = []
    for b in range(NB):
        m = dict(sh)
        m["x"] = np.ascontiguousarray(x[b])
        in_maps.append(m)
    res = run_bass_kernel_spmd(nc, in_maps, core_ids=list(range(NB)))
    return np.stack([np.asarray(r["out"], dtype=np.float32) for r in res.results], 0)
```

```python
import numpy as np
import concourse.bass as bass
import concourse.mybir as mybir

F32 = mybir.dt.float32
BF16 = mybir.dt.bfloat16
AF = mybir.ActivationFunctionType
ALU = mybir.AluOpType
AX = mybir.AxisListType

_DT_SIZE = {F32: 4, BF16: 2, mybir.dt.float32r: 4}


def _dsize(dt):
    try:
        return _DT_SIZE[dt]
    except KeyError:
        return mybir.dt.size(dt)


def ap_region(ap):
    name = ap.name
    pat = ap.ap
    off = int(ap.offset)
    es = _dsize(ap.dtype)
    sp = str(ap.space)
    if sp == "DRAM":
        lo = off
        hi = off
        for st, cnt in pat:
            if cnt > 1:
                if st >= 0:
                    hi += st * (cnt - 1)
                else:
                    lo += st * (cnt - 1)
        return (name, 0, 1, lo * es, (hi + 1) * es)
    if sp == "PSUM":
        return (name, 0, 128, 0, 2048)
    pstep, pcnt = pat[0]
    if pstep > 0:
        rowlen = pstep
    else:
        rowlen = int(np.prod(ap.tensor.shape[1:])) * _dsize(ap.tensor.dtype) // es
    p0 = off // rowlen
    f0 = off % rowlen
    lo = f0
    hi = f0
    for st, cnt in pat[1:]:
        if cnt > 1:
            if st >= 0:
                hi += st * (cnt - 1)
            else:
                lo += st * (cnt - 1)
    np_ = pcnt if pstep != 0 else 1
    return (name, p0, p0 + np_, lo * es, (hi + 1) * es)


class Op:
    __slots__ = ("eng", "fn", "idx", "cnt_eng", "cnt", "waits", "inc", "clock", "dma_key", "phase")


class Prog:
    ENGS = ("tensor", "vector", "scalar", "gpsimd", "sync")
    CAP = 6000

    def __init__(self, nc):
        self.nc = nc
        self.ops = []
        self.streams = {e: [] for e in self.ENGS}
        self.acc = {}
        self.counts = {}
        self.seen = {e: {} for e in self.ENGS}
        self.op_by = {}

    def add(self, eng, fn, reads=(), writes=(), dma_key=None):
        op = Op()
        op.eng = eng
        op.fn = fn
        op.idx = len(self.ops)
        op.dma_key = dma_key
        op.cnt_eng = ("dma", dma_key) if dma_key is not None else eng
        op.inc = False
        op.phase = getattr(self, 'phase', '')
        need = {}
        rregs = [ap_region(a) for a in reads]
        wregs = [ap_region(a) for a in writes]
        for regs, is_w in ((rregs, False), (wregs, True)):
            for (name, p0, p1, b0, b1) in regs:
                for rec in self.acc.get(name, ()):
                    if rec[1] <= p0 or rec[0] >= p1 or rec[3] <= b0 or rec[2] >= b1:
                        continue
                    if not (is_w or rec[4]):
                        if not (name.startswith("ps") and rec[5] != op.cnt_eng):
                            continue
                    ce, c = rec[5], rec[6]
                    if isinstance(ce, tuple):
                        c = self.counts[ce]
                    if ce == eng and dma_key is None:
                        if eng == "tensor":
                            continue
                    if need.get(ce, 0) < c:
                        need[ce] = c
        seen = self.seen[eng]
        waits = []
        for ce, c in need.items():
            if seen.get(ce, 0) >= c:
                continue
            waits.append((ce, c))
            src = self.op_by[(ce, c)]
            src.inc = True
            for k, v in src.clock.items():
                if seen.get(k, 0) < v:
                    seen[k] = v
            seen[ce] = c
        op.waits = waits
        cnt = self.counts.get(op.cnt_eng, 0) + 1
        self.counts[op.cnt_eng] = cnt
        op.cnt = cnt
        op.clock = dict(seen)
        self.op_by[(op.cnt_eng, cnt)] = op
        for regs, is_w in ((rregs, False), (wregs, True)):
            for (name, p0, p1, b0, b1) in regs:
                lst = self.acc.setdefault(name, [])
                new = []
                for rec in lst:
                    covered = rec[0] >= p0 and rec[1] <= p1 and rec[2] >= b0 and rec[3] <= b1
                    if covered and (is_w or (not rec[4] and rec[5] == op.cnt_eng)):
                        continue
                    new.append(rec)
                new.append((p0, p1, b0, b1, is_w, op.cnt_eng, cnt, op.idx))
                self.acc[name] = new
        self.ops.append(op)
        self.streams[eng].append(op)
        return op

    def pe(self, fn, reads, writes):
        return self.add("tensor", fn, reads, writes)

    def dve(self, fn, reads, writes):
        return self.add("vector", fn, reads, writes)

    def act(self, fn, reads, writes):
        return self.add("scalar", fn, reads, writes)

    def pool(self, fn, reads, writes):
        return self.add("gpsimd", fn, reads, writes)

    def dma(self, queue, out, in_, key, **kw):
        return self.add(queue, lambda e: e.dma_start(out=out, in_=in_, **kw), [in_], [out], dma_key=key)

    def emit(self, final_waits=()):
        nc = self.nc
        incs = {}
        for op in self.ops:
            if op.dma_key is not None or op.inc:
                incs.setdefault(op.cnt_eng, []).append(op.cnt)
        import contextlib
        with contextlib.ExitStack() as st:
            semtab = {}
            valmap = {}
            nsem = 0
            for ce, lst in incs.items():
                is_dma = isinstance(ce, tuple)
                cap = 10 ** 9 if is_dma else self.CAP
                step = 16 if is_dma else 1
                sems = []
                for i, c in enumerate(lst):
                    si = i // cap
                    if si >= len(sems):
                        sems.append(st.enter_context(nc.semaphore("s%d" % nsem)))
                        nsem += 1
                    valmap[(ce, c)] = (sems[si], (i % cap + 1) * step)
                semtab[ce] = sems
            self.nsem = nsem
            block = st.enter_context(nc.Block())

            def make(engname):
                ops = self.streams[engname]

                def body(e):
                    for op in ops:
                        for w in op.waits:
                            s, v = valmap[w]
                            e.wait_ge(s, v)
                        ins = op.fn(e)
                        if op.dma_key is not None:
                            s, v = valmap[(op.cnt_eng, op.cnt)]
                            ins.then_inc(s, 16)
                        elif op.inc:
                            s, v = valmap[(op.cnt_eng, op.cnt)]
                            ins.then_inc(s, 1)
                    if engname == "sync":
                        for key in final_waits:
                            ce = ("dma", key)
                            c = self.counts[ce]
                            s, v = valmap[(ce, c)]
                            e.wait_ge(s, v)
                return body

            for en in self.ENGS:
                getattr(block, en)(make(en))

import contextlib
from concourse.bass_utils import run_bass_kernel_spmd

F32R = F32
D = 1024
SEQ = 2048
NMETA = 16
LP = 2176
NT = 17
DEPTH = 4
RWC = 1824
FXC = 2056
DFF = 2816
EPS = 1e-6
GN_EPS = 64e-5
NB = 8
RWKV_ENABLED = True
import os
RW_STAGE = int(os.environ.get('RW_STAGE', '9'))


class K:
    def __init__(self, P):
        self.P = P

    def mm(self, out, lhsT, rhs, start=True, stop=True):
        self.P.pe(lambda e: e.matmul(out, lhsT, rhs, start=start, stop=stop), [lhsT, rhs], [out])

    def tr(self, out, in_, ident):
        self.P.pe(lambda e: e.transpose(out, in_, ident), [in_, ident], [out])

    def act(self, out, in_, func, bias=None, scale=None, accum=None):
        kw = {}
        reads = [in_]
        writes = [out]
        if bias is not None:
            kw["bias"] = bias
            if not isinstance(bias, (int, float)):
                reads.append(bias)
        if scale is not None:
            kw["scale"] = scale
            if not isinstance(scale, (int, float)):
                reads.append(scale)
        if accum is not None:
            kw["accum_out"] = accum
            writes.append(accum)
        self.P.act(lambda e: e.activation(out, in_, func, **kw), reads, writes)

    def tt(self, eng, out, in0, in1, op):
        self.P.add(eng, lambda e: e.tensor_tensor(out, in0, in1, op), [in0, in1], [out])

    def ts(self, eng, out, in0, s1, s2, op0, op1=None):
        reads = [in0]
        for s in (s1, s2):
            if s is not None and not isinstance(s, (int, float)):
                reads.append(s)
        if op1 is None:
            self.P.add(eng, lambda e: e.tensor_scalar(out, in0, s1, None, op0), reads, [out])
        else:
            self.P.add(eng, lambda e: e.tensor_scalar(out, in0, s1, s2, op0, op1), reads, [out])

    def stt(self, out, in0, scalar, in1, op0, op1):
        reads = [in0, in1]
        if not isinstance(scalar, (int, float)):
            reads.append(scalar)
        self.P.dve(lambda e: e.scalar_tensor_tensor(out, in0, scalar, in1, op0, op1), reads, [out])

    def copy(self, eng, out, in_):
        if eng == "scalar":
            self.P.act(lambda e: e.copy(out, in_), [in_], [out])
        else:
            self.P.add(eng, lambda e: e.tensor_copy(out, in_), [in_], [out])

    def memset(self, eng, ap, val):
        self.P.add(eng, lambda e: e.memset(ap, val), [], [ap])

    def recip(self, out, in_):
        self.P.dve(lambda e: e.reciprocal(out, in_), [in_], [out])

    def reduce(self, out, in_, op=ALU.add):
        self.P.dve(lambda e: e.tensor_reduce(out, in_, AX.X, op), [in_], [out])

    def dma(self, q, out, in_, key, **kw):
        self.P.dma(q, out, in_, key, **kw)


def mixer(env, l):
    k = env["k"]; P = env["P"]; Carver = env["Carver"]; pss = env["pss"]; XT = env["XT"]
    cm = env["cm"]; ident_bf = env["ident_bf"]; negm_bf = env["negm_bf"]; small = env["small"]
    w_in = env["w_in"]; fx_bf = env["fx_bf"]; fx_qg = env["fx_qg"]; fx_kg = env["fx_kg"]
    qa_s = env["qa_s"]; ka_s = env["ka_s"]; sgo_s = env["sgo_s"]; norm1_g = env["norm1_g"]
    P.phase = 'norm1'
    env["norm_T"](norm1_g[l:l + 1, :])
    c = Carver()
    c.off = 2600
    vaug = c.take(NT * 8 * 65 // 2 + 4, BF16)[:, 0:NT * 8 * 65].rearrange("p (t h d) -> p t h d", t=NT, h=8, d=65)
    negc = c.take(NT * 8, F32, (NT, 8))
    nlf = c.take(NT * 8, F32, (NT, 8))
    gq = c.take(64)
    gk = c.take(64)
    bfb = c.take(8)
    tmpf = [c.take(512) for _ in range(2)]
    sgst = [c.take(256, BF16) for _ in range(2)]
    persist_off = c.off
    wfx = [c.take(2048, BF16, (8, 512)) for _ in range(2)]
    wfl = c.take(32, BF16, (8, 8))
    qbf = [c.take(256, BF16, (8, 64)) for _ in range(2)]
    qst = [c.take(512, BF16, (8, 128)) for _ in range(2)]
    t8 = small[:, 112:120]
    chl = small[:, 128:136].bitcast(BF16)
    cst = c.take(64, BF16)
    k.dma("sync", gq, fx_qg[l:l + 1, :].partition_broadcast(128), "gq")
    k.dma("sync", gk, fx_kg[l:l + 1, :].partition_broadcast(128), "gk")
    k.dma("sync", bfb, fx_bf[l:l + 1, :].partition_broadcast(128), "bfb")
    k.ts("vector", gq, gq, 0.125, None, ALU.mult)
    k.memset("vector", vaug[:, :, :, 64:65], 1.0)
    wv = w_in[l].rearrange("(kc p) c -> p kc c", p=128)
    P.phase = 'fox_inproj'
    ss8s = [small[:, 96:104], small[:, 144:152]]
    rs8s = [small[:, 104:112], small[:, 152:160]]
    k.dma("gpsimd", wfx[0], wv[:, :, RWC:RWC + 512], "wfx0")
    pend_tr = None
    for ci in range(4):
        wb = wfx[ci % 2]
        if ci + 1 < 4:
            k.dma("gpsimd", wfx[(ci + 1) % 2], wv[:, :, RWC + (ci + 1) * 512:RWC + (ci + 2) * 512], "wfx%d" % ((ci + 1) % 2))
        for i in range(NT):
            ps = pss[2 + (i % 2)]
            for kc in range(8):
                k.mm(ps[:], XT[:, kc, i * 128:(i + 1) * 128], wb[:, kc, :], start=kc == 0, stop=kc == 7)
            if pend_tr is not None:
                pend_tr()
                pend_tr = None
            ps3 = ps[:].rearrange("p (h d) -> p h d", d=64)
            if ci < 2:
                tf = tmpf[i % 2]
                tf3 = tf.rearrange("p (h d) -> p h d", d=64)
                ss8 = ss8s[i % 2]; rs8 = rs8s[i % 2]
                k.act(tf, ps[:], AF.Square)
                k.reduce(ss8, tf3)
                k.ts("vector", ss8, ss8, 1.0 / 64, EPS, ALU.mult, ALU.add)
                k.act(ss8, ss8, AF.Sqrt)
                k.recip(rs8, ss8)
                k.tt("vector", tf3, ps3, rs8.unsqueeze(2).to_broadcast([128, 8, 64]), ALU.mult)
                g = gq if ci == 0 else gk
                qb = qbf[i % 2]
                k.tt("gpsimd", qb, tf3, g.unsqueeze(1).to_broadcast([128, 8, 64]), ALU.mult)

                def do_tr(i=i, ci=ci, qb=qb):
                    pt = pss[4 + (i % 2)][:].bitcast(BF16).rearrange("p (a b) -> p a b", b=128)[0:64, 0:8, :]
                    for hh in range(8):
                        k.tr(pt[:, hh, :], qb[:, hh, :], ident_bf[:])
                    qs = qst[i % 2]
                    k.copy("scalar", qs[0:64, :, :], pt)
                    dst = (qa_s if ci == 0 else ka_s)[:, 0:64, i * 128:(i + 1) * 128].rearrange("h r t -> r h t")
                    k.dma("sync", dst, qs[0:64, :, :], "qst%d" % (i % 2))
                pend_tr = do_tr
            elif ci == 2:
                k.copy("scalar", vaug[:, i, :, 0:64], ps3)
            else:
                sg = sgst[i % 2]
                k.act(sg, ps[:], AF.Sigmoid)
                k.dma("sync", sgo_s[i * 128:(i + 1) * 128, :], sg, "sgst%d" % (i % 2))
    if pend_tr is not None:
        pend_tr()
    P.phase = 'fox_fl'
    k.dma("gpsimd", wfl, wv[:, :, RWC + 2048:RWC + 2056], "wfl")
    for i in range(NT):
        ps = pss[2 + (i % 2)]
        for kc in range(8):
            k.mm(ps[:, 0:8], XT[:, kc, i * 128:(i + 1) * 128], wfl[:, kc, :], start=kc == 0, stop=kc == 7)
        k.tt("vector", t8, ps[:, 0:8], bfb, ALU.add)
        k.act(t8, t8, AF.Exp, scale=-1.0)
        k.act(nlf[:, i, :], t8, AF.Ln, bias=1.0)
    carry = small[:, 168:176]
    k.memset("vector", carry, 0.0)
    for i in range(NT):
        pc = pss[4 + (i % 2)]
        k.mm(pc[:, 0:8], cm[:, 6, :], nlf[:, i, :])
        k.mm(pc[:, 8:16], cm[:, 5, :], nlf[:, i, :])
        k.tt("vector", negc[:, i, :], pc[:, 0:8], carry, ALU.add)
        if i + 1 < NT:
            k.tt("vector", carry, pc[:, 8:16], carry, ALU.add)
        k.ts("vector", chl[:, 0:8], negc[:, i, :], -1.0, None, ALU.mult)
        k.stt(chl[:, 8:16], negc[:, i, :], -1.0, chl[:, 0:8], ALU.mult, ALU.subtract)
        pt = pss[2 + (i % 2)][:].bitcast(BF16)[0:16, 0:128]
        k.tr(pt, chl, ident_bf[:])
        k.copy("vector", cst[0:16, :], pt)
        k.dma("sync", qa_s[:, 64, i * 128:(i + 1) * 128], cst[0:8, :], "cst")
        k.dma("sync", qa_s[:, 65, i * 128:(i + 1) * 128], cst[8:16, :], "cst")
    P.phase = 'rw_inproj'
    if RWKV_ENABLED:
        env["rwkv_inproj"](env, l, persist_off)
    P.phase = 'fox_att'
    c2 = Carver()
    c2.off = persist_off
    qT = [c2.take(LP // 2, BF16) for _ in range(2)]
    kT = [c2.take(LP // 2, BF16) for _ in range(2)]
    Eb = [c2.take(256, BF16) for _ in range(3)]
    zb = c2.take(256, BF16)
    yfx = c2.take(NT * 512 // 2, BF16, (NT, 512))
    rd = small[:, 140:141]
    k.memset("vector", zb, 0.0)
    ecnt = 0
    for hh in range(8):
        q_ = qT[hh % 2]
        k_ = kT[hh % 2]
        k.dma("sync", q_[0:66, :], qa_s[hh], "qT%d" % (hh % 2))
        k.dma("sync", k_[0:66, :], ka_s[hh], "kT%d" % (hh % 2))
        for qt in range(5):
            q0 = qt * 512
            Nq = min(512, LP - q0)
            nqb = Nq // 128
            Ob = pss[6 + (qt % 2)]
            Oacc = Ob[:, 0:260].rearrange("p (a d) -> p a d", d=65)
            k.mm(Ob[:, 0:260], zb[:, 0:128], zb[:, 0:260], start=True, stop=True)
            kb_max = (q0 + Nq) // 128 - 1

            def scores(kb):
                j = kb - 4 * qt
                cs = max(0, j) * 128
                nonlocal ecnt
                pS = pss[ecnt % 2]
                E = Eb[ecnt % 3]
                ecnt += 1
                k.mm(pS[:, cs:Nq], k_[0:66, kb * 128:(kb + 1) * 128], q_[0:66, q0 + cs:q0 + Nq], start=True, stop=j < 0)
                if j >= 0:
                    k.mm(pS[:, cs:cs + 128], ident_bf[:], negm_bf[:], start=False, stop=True)
                return (kb, j, cs, pS, E)

            pend = scores(0)
            for kb in range(kb_max + 1):
                cur = pend
                pend = scores(kb + 1) if kb + 1 <= kb_max else None
                _, j, cs, pS, E = cur
                k.act(E[:, cs:Nq], pS[:, cs:Nq], AF.Exp, bias=negc[:, kb, hh:hh + 1])
                for qbl in range(max(0, j), nqb):
                    P.pe(lambda e, o=Oacc[:, qbl, :], a=E[:, qbl * 128:(qbl + 1) * 128], b=vaug[:, kb, hh, :]:
                         e.matmul(o, a, b, start=False, stop=True, skip_group_check=True),
                         [E[:, qbl * 128:(qbl + 1) * 128], vaug[:, kb, hh, :]], [Oacc[:, qbl, :]])
            for qbl in range(nqb):
                i = 4 * qt + qbl
                k.recip(rd, Oacc[:, qbl, 64:65])
                k.ts("vector", yfx[:, i, hh * 64:(hh + 1) * 64], Oacc[:, qbl, 0:64], rd, None, ALU.mult)
    P.phase = 'fox_gate'
    for i in range(NT):
        sg = sgst[i % 2]
        k.dma("sync", sg, sgo_s[i * 128:(i + 1) * 128, :], "sgld%d" % (i % 2))
        yg = tmpf[i % 2].bitcast(BF16)[:, 0:512]
        k.tt("vector", yg, yfx[:, i, :], sg, ALU.mult)
        pt = pss[2 + (i % 2)][:].bitcast(BF16).rearrange("p (a b) -> p a b", b=128)[:, 0:4, :]
        for cc in range(4):
            k.tr(pt[:, cc, :], yg[:, cc * 128:(cc + 1) * 128], ident_bf[:])
        k.copy("scalar", XT[:, 4:8, i * 128:(i + 1) * 128], pt)


def rwkv_inproj(env, l, off):
    k = env["k"]; P = env["P"]; Carver = env["Carver"]; pss = env["pss"]; XT = env["XT"]
    cm = env["cm"]; ident_bf = env["ident_bf"]; ident_f = env["ident_f"]; ident_r = env["ident_r"]
    w_in = env["w_in"]; rw_s = env["rw_s"]; g_s = env["g_s"]
    mu_p = env["mu_p"]; rwvec_p = env["rwvec_p"]; wup_aug = env["wup_aug"]; a_up = env["a_up"]
    g_up = env["g_up"]; gn_w = env["gn_w"]; gn_b = env["gn_b"]
    wv = w_in[l].rearrange("(kc p) c -> p kc c", p=128)
    c = Carver()
    c.off = off
    mu = c.take(16)[:, 0:15]
    omu = c.take(16)[:, 0:15]
    k.dma("sync", mu, mu_p[l], "mu")
    k.ts("vector", omu, mu, -1.0, 1.0, ALU.mult, ALU.add)
    wrw = [c.take(512, BF16, (8, 128)) for _ in range(2)]
    PA = [c.take(LP) for _ in range(2)]
    PB = [c.take(LP + 2) for _ in range(2)]
    sh = [c.take(LP) for _ in range(2)]
    for b_ in range(2):
        k.memset("vector", PB[b_][:, 0:1], 0.0)
    def load_rw(m):
        nco_ = 128 if m < 14 else 32
        k.dma("gpsimd", wrw[m % 2][:, :, 0:nco_], wv[:, :, m * 128:m * 128 + nco_], "wrw%d" % (m % 2))

    load_rw(0)
    for m in range(15):
        nco = 128 if m < 14 else 32
        wb = wrw[m % 2]
        if m + 1 < 15:
            load_rw(m + 1)
        pa = PA[m % 2]
        pb = PB[m % 2]
        for n in range(5):
            c0 = n * 512
            N = min(512, LP - c0)
            ps = pss[2 + (n % 2)]
            for kc in range(8):
                k.mm(ps[0:nco, 0:N], wb[:, kc, 0:nco], XT[:, kc, c0:c0 + N], start=kc == 0, stop=kc == 7)
            k.act(pa[0:nco, c0:c0 + N], ps[0:nco, 0:N], AF.Copy, scale=omu[0:nco, m:m + 1])
            k.act(pb[0:nco, 1 + c0:1 + c0 + N], ps[0:nco, 0:N], AF.Copy, scale=mu[0:nco, m:m + 1])
        so = sh[m % 2]
        k.tt("vector" if m % 2 else "gpsimd", so[0:nco, :], pa[0:nco, :], pb[0:nco, 0:LP], ALU.add)
        k.dma("sync", rw_s[m, 0:nco, :], so[0:nco, :], "sh%d" % (m % 2))


def rwkv(env, l):
    k = env["k"]; P = env["P"]; Carver = env["Carver"]; pss = env["pss"]; XT = env["XT"]
    cm = env["cm"]; ident_bf = env["ident_bf"]; ident_f = env["ident_f"]; ident_r = env["ident_r"]
    rw_s = env["rw_s"]; g_s = env["g_s"]
    rwvec_p = env["rwvec_p"]; wup_aug = env["wup_aug"]; a_up = env["a_up"]
    g_up = env["g_up"]; gn_w = env["gn_w"]; gn_b = env["gn_b"]
    P.phase = 'rw_lora'
    c = Carver()
    TW = c.take(LP, F32R)
    AD = c.take(LP, F32R)
    SG1 = c.take(LP // 2, BF16)
    SG2 = c.take(LP // 2, BF16)
    wup = c.take(512, F32R)
    aup = c.take(512, F32R)
    gup1 = c.take(256, BF16)
    gup2 = c.take(256, BF16)
    vec = c.take(16, F32, (4, 4))
    omka = c.take(4)
    hsel = c.take(2, F32R)
    hself = c.take(2)
    NEGE = -float(np.exp(-0.5))
    tric = c.take(256, F32, (2, 128))
    k.ts("vector", tric[:, 0, :], cm[:, 2, :], NEGE, None, ALU.mult)
    k.ts("vector", tric[:, 1, :], cm[:, 1, :], NEGE, None, ALU.mult)
    lt = c.take(LP)
    if RW_STAGE < 2:
        return
    k.dma("sync", vec, rwvec_p[l], "vec")
    k.ts("vector", omka, vec[:, :, 1], -1.0, 1.0, ALU.mult, ALU.add)

    k.copy("vector", hself[:, 0:1], cm[:, 7, 0:1])
    k.copy("vector", hself[:, 1:2], cm[:, 7, 127:128])
    k.copy("vector", hsel, hself)
    k.dma("sync", lt[0:128, :], rw_s[12], "lt")
    k.act(TW[0:64, :], lt[0:64, :], AF.Tanh)
    k.ts("vector", TW[64:128, :], lt[64:128, :], 0.0, 1.0, ALU.mult, ALU.add)
    k.copy("vector", AD, lt)
    lt2 = c.take(LP)
    k.dma("sync", lt2[0:128, :], rw_s[13], "lt2")
    k.act(SG1, lt2, AF.Sigmoid)
    lt3 = c.take(LP)
    k.dma("sync", lt3[0:32, :], rw_s[14, 0:32, :], "lt3")
    k.act(SG2[0:32, :], lt3[0:32, :], AF.Sigmoid)
    wtmp = c.take(512)
    k.dma("sync", wtmp[0:65, :], wup_aug[l], "wtmp")
    k.memset("vector", wup[64:128, :], 0.0)
    k.copy("vector", wup[0:65, :], wtmp[0:65, :])
    wtmp2 = c.take(512)
    k.dma("sync", wtmp2[64:128, :], a_up[l], "wtmp2")
    k.memset("vector", aup[0:64, :], 0.0)
    k.copy("vector", aup[64:128, :], wtmp2[64:128, :])
    k.dma("gpsimd", gup1, g_up[l, 0:128, :], "gup1")
    k.dma("gpsimd", gup2[0:32, :], g_up[l, 128:160, :], "gup2")
    gst = [wtmp, wtmp2]
    for i in range(NT):
        ps = pss[2 + (i % 2)]
        k.mm(ps[:], SG1[:, i * 128:(i + 1) * 128], gup1, start=True, stop=False)
        k.mm(ps[:], SG2[0:32, i * 128:(i + 1) * 128], gup2[0:32, :], start=False, stop=True)
        k.copy("scalar", gst[i % 2], ps[:])
        k.dma("sync", g_s[i * 128:(i + 1) * 128, :], gst[i % 2], "gst%d" % (i % 2))
    per_off = c.off - 3 * LP - 1024
    if RW_STAGE < 3:
        return
    RD = BF16

    def tk(n):
        return c.take(n // 2, RD)

    def run_interleaved(gens):
        active = list(gens)
        while active:
            for g in list(active):
                try:
                    next(g)
                except StopIteration:
                    active.remove(g)

    def interleave_gen(gens):
        active = list(gens)
        while active:
            for g in list(active):
                try:
                    next(g)
                    yield
                except StopIteration:
                    active.remove(g)

    P.phase = 'rw_pair'
    c = Carver()
    c.off = per_off
    pair_bufs = []
    for ps_i in range(2):
        B = {}
        B["RKV"] = c.take(384, F32, (3, 128))
        B["AS"] = c.take(128); B["SIGT"] = c.take(128); B["E13"] = c.take(256); B["E2"] = c.take(128)
        for nm in ("kk", "kk2", "rin", "kkn", "t1", "kp", "bb", "rk"):
            B[nm] = c.take(128)
        B["bT"] = tk(128); B["kT"] = tk(128)
        B["bTh"] = [tk(128) for _ in range(2)]
        B["kTh"] = [tk(128) for _ in range(2)]
        B["A0T"] = [tk(128) for _ in range(2)]
        B["Zb"] = [[tk(384), tk(384)] for _ in range(2)]
        B["sets"] = []
        for _ in range(2):
            B["sets"].append(dict(AR=tk(256), v_tok=tk(128), btok=[tk(128), tk(128)], ktok=[tk(128), tk(128)],
                                  GM=c.take(4), AB=[tk(256), tk(256)], AK=[tk(256), tk(256)], T=[tk(128), tk(128)],
                                  vtf=c.take(128), coef=c.take(2), y=c.take(128), g=c.take(128)))
        B["hd"] = [dict(W=tk(64), U=tk(64), S16=tk(64), S32=c.take(64), S32g=c.take(64)) for _ in range(2)]
        B["gnw"] = c.take(128); B["gnb"] = c.take(128)
        B["sq"] = c.take(128); B["yn"] = c.take(128); B["yb"] = tk(128)
        B["st"] = c.take(16)
        B["banks"] = (pss[4 * ps_i], pss[4 * ps_i + 1], pss[4 * ps_i + 2], pss[4 * ps_i + 3])
        pair_bufs.append(B)

    def pair_gen(hp):
        B = pair_bufs[hp % 2]
        tag = "_%d" % (hp % 2)
        RKV = B["RKV"]; AS = B["AS"]; SIGT = B["SIGT"]; E13 = B["E13"]; E2 = B["E2"]
        kk = B["kk"]; kk2 = B["kk2"]; rin = B["rin"]; kkn = B["kkn"]; t1 = B["t1"]; kp = B["kp"]; bb = B["bb"]; rk = B["rk"]
        bT = B["bT"]; kT = B["kT"]; bTh = B["bTh"]; kTh = B["kTh"]; A0T = B["A0T"]; Zb = B["Zb"]
        sets = B["sets"]; hd_b = B["hd"]; gnw = B["gnw"]; gnb = B["gnb"]
        X0, X1, XS0, XS1 = B["banks"]
        pA = X0[:, 0:256]; pC = X0[:, 256:512]
        pT = X1
        pM = X0; pM2 = X1
        pIs = [X0, X1]
        pSs = [XS0[:, 0:256], XS1[:, 0:256]]
        for d in hd_b:
            for nm in ("W", "U", "S16", "S32"):
                k.memset("vector", d[nm], 0.0)
        k.dma("sync", gnw, gn_w[l:l + 1, hp * 128:(hp + 1) * 128].partition_broadcast(128), "gnw" + tag)
        k.dma("sync", gnb, gn_b[l:l + 1, hp * 128:(hp + 1) * 128].partition_broadcast(128), "gnb" + tag)
        kk_s = vec[:, hp, 0:1]; ka_s_ = vec[:, hp, 1:2]; a0_s = vec[:, hp, 2:3]; rk_s = vec[:, hp, 3:4]
        yield

        def prep(i):
            S_ = sets[i % 2]
            AR = S_["AR"]; v_tok = S_["v_tok"]; btok = S_["btok"]; ktok = S_["ktok"]; GM = S_["GM"]
            t0 = i * 128
            for q in range(3):
                k.dma("sync", RKV[:, q, :], rw_s[4 * q + hp, :, t0:t0 + 128], "rkv%d" % q + tag)
            k.dma("sync", S_["g"], g_s[t0:t0 + 128, hp * 128:(hp + 1) * 128], "gt%d" % (i % 2) + tag)
            r_ = RKV[:, 0, :]; k_ = RKV[:, 1, :]; v_ = RKV[:, 2, :]
            k.mm(pA[:, 128:256], aup[:, hp * 128:(hp + 1) * 128], AD[:, t0:t0 + 128])
            k.mm(pA[:, 0:128], TW[:, t0:t0 + 128], wup[:, hp * 128:(hp + 1) * 128])
            k.act(AS, pA[:, 128:256], AF.Sigmoid, bias=a0_s)
            k.act(SIGT, pA[:, 0:128], AF.Sigmoid)
            yield
            k.mm(pC[:, 0:128], SIGT, tric[:, 0, :])
            k.mm(pC[:, 128:256], SIGT, tric[:, 1, :])
            k.act(E13, pC[:, 0:256], AF.Exp)
            k.act(E2, pC[:, 0:128], AF.Exp, scale=-1.0)
            yield
            E1 = E13[:, 0:128]; E3 = E13[:, 128:256]
            k.ts("vector", kk, k_, kk_s, None, ALU.mult)
            k.tt("gpsimd", kk2, kk, kk, ALU.mult)
            k.mm(pT[:, 384:512], cm[:, 7, :], kk2)
            k.act(rin, pT[:, 384:512], AF.Sqrt)
            yield
            k.ts("vector", rin, rin, 1e-12, None, ALU.max)
            k.recip(rin, rin)
            k.tt("vector", kkn, kk, rin, ALU.mult)
            k.ts("vector", t1, AS, ka_s_, omka[:, hp:hp + 1], ALU.mult, ALU.add)
            yield
            k.tt("gpsimd", kp, k_, t1, ALU.mult)
            k.tt("gpsimd", bb, kkn, AS, ALU.mult)
            k.tt("vector", AR[:, 128:256], r_, E1, ALU.mult)
            k.tt("gpsimd", t1, kkn, E3, ALU.mult)
            yield
            k.ts("vector", AR[:, 0:128], t1, -1.0, None, ALU.mult)
            k.tt("vector", bT, bb, E2, ALU.mult)
            k.tt("vector", kT, kp, E2, ALU.mult)
            k.tt("gpsimd", rk, r_, kp, ALU.mult)
            yield
            k.ts("vector", rk, rk, rk_s, None, ALU.mult)
            for hd in range(2):
                for cc in range(2):
                    k.ts("vector", GM[:, hd * 2 + cc:hd * 2 + cc + 1], hself[:, hd:hd + 1],
                         E13[:, 63 + 64 * cc:64 + 64 * cc], None, ALU.mult)
            yield
            k.mm(pT[:, 0:128], v_, ident_f[:])
            k.mm(pT[:, 128:256], bT, ident_bf[:])
            k.mm(pT[:, 256:384], kT, ident_bf[:])
            k.mm(pT[:, 384:386], rk, hself)
            k.copy("scalar", v_tok, pT[:, 0:128])
            k.copy("scalar", S_["vtf"], pT[:, 0:128])
            yield
            k.copy("scalar", S_["coef"], pT[:, 384:386])
            for cc in range(2):
                k.act(btok[cc], pT[:, 128:256], AF.Copy, scale=hself[:, cc:cc + 1])
                k.act(ktok[cc], pT[:, 256:384], AF.Copy, scale=hself[:, cc:cc + 1])
            yield
            for hd in range(2):
                k.ts("vector", bTh[hd], bT, hself[:, hd:hd + 1], None, ALU.mult)
                k.ts("vector", kTh[hd], kT, hself[:, hd:hd + 1], None, ALU.mult)
                k.mm(pM[:, 0:256], bTh[hd], AR)
                k.mm(pM[:, 256:384], AR[:, 0:128], bTh[hd])
                k.mm(pM2[:, 0:256], kTh[hd], AR)
                yield
                k.tt("vector", S_["AB"][hd], pM[:, 0:256], cm[:, 1:3, :].rearrange("p a b -> p (a b)"), ALU.mult)
                k.tt("vector", A0T[hd], pM[:, 256:384], cm[:, 3, :], ALU.mult)
                k.tt("vector", S_["AK"][hd], pM2[:, 0:256], cm[:, 1:3, :].rearrange("p a b -> p (a b)"), ALU.mult)
                yield
            Zc = []
            for hd in range(2):
                Y = S_["AB"][hd][:, 0:128]; YT = A0T[hd]
                Z = Zb[hd][0]
                pI = pIs[hd]
                k.mm(pI[:, 128:256], YT, Y)
                k.mm(pI[:, 256:384], Y, YT)
                k.tt("gpsimd", Z[:, 0:128], Y, ident_bf[:], ALU.add)
                k.copy("scalar", Z[:, 128:384], pI[:, 128:384])
                Zc.append(Z)
            yield
            for lev in range(1, 5):
                for hd in range(2):
                    Z = Zc[hd]
                    Zn = Zb[hd][lev % 2]
                    pI = pIs[hd]
                    k.mm(pI[:, 0:256], Z[:, 256:384], Z[:, 0:256])
                    k.mm(pI[:, 256:384], Z[:, 128:256], Z[:, 256:384])
                    k.tt("vector", Zn[:, 0:128], pI[:, 0:128], Z[:, 0:128], ALU.add)
                    k.copy("scalar", Zn[:, 128:384], pI[:, 128:384])
                    Zc[hd] = Zn
                    yield
            for hd in range(2):
                Z = Zc[hd]
                pI = pIs[hd]
                k.mm(pI[:, 0:128], Z[:, 256:384], Z[:, 0:128])
                k.tt("vector", S_["T"][hd], pI[:, 0:128], Z[:, 0:128], ALU.add)
            yield

        def seq(i):
            S_ = sets[i % 2]
            AR = S_["AR"]; v_tok = S_["v_tok"]; btok = S_["btok"]; ktok = S_["ktok"]; GM = S_["GM"]
            y_t = S_["y"]
            t0 = i * 128
            for cc in range(2):
                rs = slice(cc * 64, cc * 64 + 64)
                for hd in range(2):
                    d = hd_b[hd]
                    vh = v_tok[:, hd * 64:(hd + 1) * 64]
                    gm = GM[:, hd * 2 + cc:hd * 2 + cc + 1]
                    pS = pSs[hd]
                    k.mm(pS[:, 0:64], AR[:, 0:128], d["S16"], start=True, stop=False)
                    k.mm(pS[:, 0:64], S_["AK"][hd][:, 0:128], vh, start=False, stop=True)
                    k.act(d["S32g"], d["S32"], AF.Copy, scale=gm)
                    k.copy("scalar", d["W"][rs, :], pS[rs, 0:64])
                    yield
                for hd in range(2):
                    d = hd_b[hd]
                    pS = pSs[hd]
                    k.mm(pS[:, 64:128], S_["T"][hd], d["W"])
                    k.copy("scalar", d["U"][rs, :], pS[rs, 64:128])
                    yield
                for hd in range(2):
                    d = hd_b[hd]
                    vh = v_tok[:, hd * 64:(hd + 1) * 64]
                    gm = GM[:, hd * 2 + cc:hd * 2 + cc + 1]
                    pS = pSs[hd]
                    k.mm(pS[:, 192:256], btok[cc], d["U"], start=True, stop=False)
                    k.mm(pS[:, 192:256], ktok[cc], vh, start=False, stop=True)
                    k.mm(pS[:, 128:192], AR[:, 128:256], d["S16"], start=True, stop=False)
                    k.mm(pS[:, 128:192], S_["AB"][hd][:, 128:256], d["U"], start=False, stop=False)
                    k.mm(pS[:, 128:192], S_["AK"][hd][:, 128:256], vh, start=False, stop=True)
                    k.stt(d["S32"], pS[:, 192:256], gm, d["S32g"], ALU.mult, ALU.add)
                    k.copy("vector", d["S16"], d["S32"])
                    k.copy("scalar", y_t[rs, hd * 64:hd * 64 + 64], pS[rs, 128:192])
                    yield
            st = B["st"]; sq = B["sq"]; yn = B["yn"]; yb = B["yb"]
            s1 = st[:, 0:2]; s2 = st[:, 2:4]; mn = st[:, 4:6]; m2 = st[:, 6:8]
            y3 = y_t.rearrange("p (a d) -> p a d", d=64)
            sq3 = sq.rearrange("p (a d) -> p a d", d=64)
            yn3 = yn.rearrange("p (a d) -> p a d", d=64)
            vt3 = S_["vtf"].rearrange("p (a d) -> p a d", d=64)
            k.reduce(s1, y3)
            k.tt("gpsimd", sq, y_t, y_t, ALU.mult)
            k.reduce(s2, sq3)
            k.ts("vector", mn, s1, 1.0 / 64, None, ALU.mult)
            yield
            k.tt("vector", m2, mn, mn, ALU.mult)
            k.ts("vector", s2, s2, 1.0 / 64, None, ALU.mult)
            k.tt("vector", s2, s2, m2, ALU.subtract)
            k.ts("vector", s2, s2, GN_EPS, None, ALU.add)
            k.act(s2, s2, AF.Sqrt)
            k.recip(s2, s2)
            yield
            k.tt("vector", yn3, y3, mn.unsqueeze(2).to_broadcast([128, 2, 64]), ALU.subtract)
            k.tt("vector", yn3, yn3, s2.unsqueeze(2).to_broadcast([128, 2, 64]), ALU.mult)
            k.tt("gpsimd", sq3, vt3, S_["coef"].unsqueeze(2).to_broadcast([128, 2, 64]), ALU.mult)
            k.tt("vector", yn, yn, gnw, ALU.mult)
            yield
            k.tt("vector", yn, yn, gnb, ALU.add)
            k.tt("vector", yn, yn, sq, ALU.add)
            k.tt("vector", yb, yn, S_["g"], ALU.mult)
            pt = X1[:].bitcast(BF16)[:, 768:896]
            k.tr(pt, yb, ident_bf[:])
            k.copy("scalar", XT[:, hp, t0:t0 + 128], pt)
            yield

        yield from prep(0)
        for i in range(NT):
            gens = [seq(i)]
            if i + 1 < NT:
                gens.append(prep(i + 1))
            yield from interleave_gen(gens)

    for hp0 in (0, 2):
        for _ in interleave_gen([pair_gen(hp0), pair_gen(hp0 + 1)]):
            pass


def build_program(nlayers=DEPTH, dbg=None, stages=("mix", "ffn")):
    nc = bass.Bass("TRN2", target_bir_lowering=False)
    dram = {}

    def din(name, shape, dt=F32):
        dram[name] = nc.dram_tensor(name, list(shape), dt, kind="ExternalInput").ap()
        return dram[name]

    def dscr(name, shape, dt=F32):
        kind = "ExternalOutput" if (dbg and name in dbg) else "Internal"
        dram[name] = nc.dram_tensor(name, list(shape), dt, kind=kind).ap()
        return dram[name]

    x = din("x", [SEQ, D])
    meta = din("meta", [NMETA, D])
    norm1_g = din("norm1_g", [DEPTH, D])
    norm2_g = din("norm2_g", [DEPTH, D])
    w_in = din("w_in", [DEPTH, D, RWC + FXC])
    w_o = din("w_o", [DEPTH, D, D])
    ffn_w_in = din("ffn_w_in", [DEPTH, D, 2 * DFF])
    ffn_w_out = din("ffn_w_out", [DEPTH, DFF, D])
    conv_wp = din("conv_wp", [DEPTH, 128, 44, 3])
    conv_bp = din("conv_bp", [DEPTH, 128, 44])
    mu_p = din("mu_p", [DEPTH, 128, 15])
    rwvec_p = din("rwvec_p", [DEPTH, 128, 4, 4])
    wup_aug = din("wup_aug", [DEPTH, 65, 512])
    a_up = din("a_up", [DEPTH, 64, 512])
    g_up = din("g_up", [DEPTH, 160, 512])
    gn_w = din("gn_w", [DEPTH, 512])
    gn_b = din("gn_b", [DEPTH, 512])
    fx_bf = din("fx_bf", [DEPTH, 8])
    fx_qg = din("fx_qg", [DEPTH, 64])
    fx_kg = din("fx_kg", [DEPTH, 64])
    cmask = din("cmask", [128, 8, 128])
    out = nc.dram_tensor("out", [SEQ, D], F32, kind="ExternalOutput").ap()

    rw_s = dscr("rw_s", [15, 128, LP])
    g_s = dscr("g_s", [LP, 512])
    qa_s = dscr("qa_s", [NB, 66, LP], BF16)
    ka_s = dscr("ka_s", [NB, 66, LP], BF16)
    sgo_s = dscr("sgo_s", [LP, 512], BF16)
    hdbg = dscr("hdbg", [LP, D]) if dbg else None
    mixdbg = dscr("mixdbg", [128, 8, LP], BF16) if dbg else None

    with contextlib.ExitStack() as st:
        def sb(name, shape, dt=F32):
            return st.enter_context(nc.sbuf_tensor(name, list(shape), dt))

        h = sb("h", [128, NT, D])
        XT = sb("XT", [128, 8, LP], BF16)
        ident_bf = sb("ident_bf", [128, 128], BF16)
        ident_f = sb("ident_f", [128, 128])
        ident_r = ident_f
        cm = sb("cm", [128, 8, 128])
        small = sb("small", [128, 256])
        ARENA_W = 24200
        arena = sb("arena", [128, ARENA_W])
        pss = [st.enter_context(nc.psum_tensor("ps%d" % i, [128, 512], F32)) for i in range(8)]

        P = Prog(nc)
        k = K(P)

        class Carver:
            def __init__(self):
                self.off = 0

            def take(self, nwords, dt=F32, shape=None):
                a = arena[:, self.off:self.off + nwords]
                self.off += nwords
                assert self.off <= ARENA_W, self.off
                if dt != F32:
                    a = a.bitcast(dt)
                if shape is not None:
                    names = " ".join("d%d" % i for i in range(len(shape)))
                    kw = {"d%d" % i: s for i, s in enumerate(shape)}
                    a = a.rearrange("p (%s) -> p %s" % (names, names), **kw)
                return a

        k.dma("sync", cm[:], cmask, "cm")
        k.copy("vector", ident_f[:], cm[:, 0, :])
        k.copy("vector", ident_bf[:], cm[:, 0, :])
        negm_bf = sb("negm_bf", [128, 128], BF16)
        k.copy("vector", negm_bf[:], cm[:, 4, :])

        ones_bf = sb("ones_bf", [8, LP], BF16)
        k.memset("vector", ones_bf[:], 1.0)
        k.dma("sync", ka_s[:, 64, :], ones_bf[:], "kones")
        k.dma("sync", ka_s[:, 65, :], ones_bf[:], "kones")
        k.memset("vector", h[:, NT - 1, :], 0.0)
        k.dma("sync", h[0:16, 0, :], meta, "hload")
        k.dma("sync", h[16:128, 0, :], x[0:112, :], "hload")
        k.dma("sync", h[:, 1:8, :], x[112:112 + 896, :].rearrange("(t p) d -> p t d", p=128), "hload")
        k.dma("gpsimd", h[:, 8:16, :], x[112 + 896:112 + 1920, :].rearrange("(t p) d -> p t d", p=128), "hload2")
        k.dma("sync", h[0:16, 16, :], x[2032:2048, :], "hload")

        ssq = small[:, 0:17]
        rstd = small[:, 32:49]
        tmp17 = small[:, 64:81]

        def norm_T(g_row):
            c = Carver()
            gbc = c.take(D)
            k.dma("sync", gbc, g_row.partition_broadcast(128), "gbc")
            junk = c.take(512, BF16)
            xn = [c.take(512, BF16), c.take(512, BF16)]
            for i in range(NT):
                k.act(junk, h[:, i, :], AF.Square, accum=ssq[:, i:i + 1])
            k.ts("vector", tmp17, ssq, 1.0 / D, EPS, ALU.mult, ALU.add)
            k.act(tmp17, tmp17, AF.Sqrt)
            k.recip(rstd, tmp17)
            for i in range(NT):
                xb = xn[i % 2]
                k.stt(xb, h[:, i, :], rstd[:, i:i + 1], gbc, ALU.mult, ALU.mult)
                pt = pss[i % 2][:].bitcast(BF16).rearrange("p (a b) -> p a b", b=128)[:, 0:8, :]
                for kc in range(8):
                    k.tr(pt[:, kc, :], xb[:, kc * 128:(kc + 1) * 128], ident_bf[:])
                k.copy("scalar" if i % 2 else "vector", XT[:, :, i * 128:(i + 1) * 128], pt)

        def add_to_h(i, n2, ps):
            k.tt("vector", h[:, i, n2 * 512:(n2 + 1) * 512], ps, h[:, i, n2 * 512:(n2 + 1) * 512], ALU.add)

        def ffn(l):
            P.phase = 'norm2'
            norm_T(norm2_g[l:l + 1, :])
            P.phase = 'ffn'
            c = Carver()
            cw = c.take(44 * 3, F32, (44, 3))
            cb = c.take(44)
            k.dma("sync", cw, conv_wp[l], "cw")
            k.dma("sync", cb, conv_bp[l], "cb")
            groups = [list(range(0, 5)), list(range(5, 10)), list(range(10, 14)), list(range(14, 18)), list(range(18, 22))]
            hid = c.take(5 * LP // 2, BF16, (5, LP))
            wout = c.take(5 * D // 2, BF16, (5, D))
            wu = [c.take(512, BF16, (8, 128)) for _ in range(2)]
            wg = [c.take(512, BF16, (8, 128)) for _ in range(2)]
            NBUF = 3
            HU = [c.take(516) for _ in range(NBUF)]
            HG = [c.take(516) for _ in range(NBUF)]
            t0s = [c.take(512) for _ in range(NBUF)]
            t1s = [c.take(512) for _ in range(NBUF)]
            txs = [c.take(512) for _ in range(2)]
            t2s = [c.take(512) for _ in range(2)]
            txBs = [c.take(512) for _ in range(2)]
            t2Bs = [c.take(512) for _ in range(2)]
            wi = ffn_w_in[l].rearrange("(kc p) c -> p kc c", p=128)
            wo = ffn_w_out[l].rearrange("(kc p) n -> p kc n", p=128)
            cnt = 0
            tcnt = 0
            pend_mul = None

            def load_w(j, cn):
                k.dma("gpsimd", wu[cn % 2], wi[:, :, j * 128:(j + 1) * 128], "wu%d" % (cn % 2))
                k.dma("gpsimd", wg[cn % 2], wi[:, :, DFF + j * 128:DFF + (j + 1) * 128], "wg%d" % (cn % 2))
            for gi, js in enumerate(groups):
                k.dma("gpsimd", wout[:, 0:len(js), :], wo[:, js[0]:js[0] + len(js), :], "wout")
                for jj, j in enumerate(js):
                    wub = wu[cnt % 2]
                    wgb = wg[cnt % 2]
                    if cnt == 0:
                        load_w(0, 0)
                    if j + 1 < 22:
                        load_w(j + 1, cnt + 1)
                    cnt += 1
                    ju = j
                    jg = 22 + j
                    for n in range(5):
                        c0 = n * 512
                        N = min(512, LP - c0)
                        pu = pss[2 + (n % 2)]
                        pg = pss[4 + (n % 2)]
                        b = tcnt % NBUF
                        bn = (tcnt + 1) % NBUF
                        hu = HU[b]; hg = HG[b]; t0 = t0s[b]; t1 = t1s[b]
                        tx = txs[tcnt % 2]; t2 = t2s[tcnt % 2]
                        tcnt += 1
                        for kc in range(8):
                            k.mm(pu[:, 0:N], wub[:, kc, :], XT[:, kc, c0:c0 + N], start=kc == 0, stop=kc == 7)
                        for kc in range(8):
                            k.mm(pg[:, 0:N], wgb[:, kc, :], XT[:, kc, c0:c0 + N], start=kc == 0, stop=kc == 7)
                        if n == 0:
                            k.memset("gpsimd", hu[:, 0:2], 0.0)
                            k.memset("gpsimd", hg[:, 0:2], 0.0)
                        k.copy("scalar", hu[:, 2:2 + N], pu[:, 0:N])
                        k.act(t0[:, 0:N], pu[:, 0:N], AF.Identity, bias=cb[:, ju:ju + 1], scale=cw[:, ju, 2:3])
                        k.copy("scalar", hg[:, 2:2 + N], pg[:, 0:N])
                        k.act(t1[:, 0:N], pg[:, 0:N], AF.Identity, bias=cb[:, jg:jg + 1], scale=cw[:, jg, 2:3])
                        if n < 4:
                            k.copy("gpsimd", HU[bn][:, 0:2], hu[:, N:N + 2])
                            k.copy("gpsimd", HG[bn][:, 0:2], hg[:, N:N + 2])
                        txb = txBs[(tcnt - 1) % 2]; t2b = t2Bs[(tcnt - 1) % 2]
                        k.ts("vector", t2[:, 0:N], hg[:, 1:1 + N], cw[:, jg, 1:2], None, ALU.mult)
                        k.ts("vector", t2b[:, 0:N], hg[:, 0:N], cw[:, jg, 0:1], None, ALU.mult)
                        k.ts("vector", tx[:, 0:N], hu[:, 1:1 + N], cw[:, ju, 1:2], None, ALU.mult)
                        k.ts("vector", txb[:, 0:N], hu[:, 0:N], cw[:, ju, 0:1], None, ALU.mult)
                        k.tt("gpsimd", t1[:, 0:N], t1[:, 0:N], t2[:, 0:N], ALU.add)
                        k.tt("gpsimd", t1[:, 0:N], t1[:, 0:N], t2b[:, 0:N], ALU.add)
                        k.act(t1[:, 0:N], t1[:, 0:N], AF.Silu)
                        k.tt("vector", t0[:, 0:N], t0[:, 0:N], tx[:, 0:N], ALU.add)
                        k.tt("vector", t0[:, 0:N], t0[:, 0:N], txb[:, 0:N], ALU.add)
                        if pend_mul is not None:
                            pend_mul()
                        pend_mul = (lambda jj=jj, c0=c0, N=N, t0=t0, t1=t1:
                                    k.tt("vector", hid[:, jj, c0:c0 + N], t1[:, 0:N], t0[:, 0:N], ALU.mult))
                pend_mul()
                pend_mul = None
                for i in range(NT):
                    for n2 in range(2):
                        ps = pss[6 + ((i * 2 + n2) % 2)]
                        for jj in range(len(js)):
                            k.mm(ps[:], hid[:, jj, i * 128:(i + 1) * 128], wout[:, jj, n2 * 512:(n2 + 1) * 512],
                                 start=jj == 0, stop=jj == len(js) - 1)
                        add_to_h(i, n2, ps[:])

        def out_proj(l):
            P.phase = 'out_proj'
            c = Carver()
            wob = c.take(8 * D // 2, BF16, (8, D))
            k.dma("gpsimd", wob, w_o[l].rearrange("(kc p) n -> p kc n", p=128), "wob")
            for i in range(NT):
                for n2 in range(2):
                    ps = pss[6 + ((i * 2 + n2) % 2)]
                    for kc in range(8):
                        k.mm(ps[:], XT[:, kc, i * 128:(i + 1) * 128], wob[:, kc, n2 * 512:(n2 + 1) * 512],
                             start=kc == 0, stop=kc == 7)
                    add_to_h(i, n2, ps[:])

        env = dict(locals())
        env["rwkv_inproj"] = rwkv_inproj
        for l in range(nlayers):
            if "mix" in stages:
                mixer(env, l)
                if RWKV_ENABLED:
                    rwkv(env, l)
                else:
                    k.memset("gpsimd", XT[:, 0:4, :], 0.0)
                if dbg and "mixdbg" in dbg and l == 0:
                    k.dma("sync", mixdbg, XT[:], "mixdbg")
                out_proj(l)
            if "ffn" in stages:
                ffn(l)

        if dbg and "hdbg" in dbg:
            k.dma("sync", hdbg.rearrange("(t p) d -> p t d", p=128), h[:], "hdbg")
        k.dma("sync", out[0:112, :], h[16:128, 0, :], "ost")
        k.dma("sync", out[112:112 + 896, :].rearrange("(t p) d -> p t d", p=128), h[:, 1:8, :], "ost")
        k.dma("gpsimd", out[112 + 896:112 + 1920, :].rearrange("(t p) d -> p t d", p=128), h[:, 8:16, :], "ost2")
        k.dma("sync", out[2032:2048, :], h[0:16, 16, :], "ost")
        fw = ["ost", "ost2"] + (["hdbg"] if dbg and "hdbg" in dbg else []) + (["mixdbg"] if dbg and "mixdbg" in dbg else [])
        P.emit(final_waits=fw)
    return nc, P


def _masks():
    m = np.zeros((128, 8, 128), np.float32)
    j = np.arange(128)[:, None]
    t = np.arange(128)[None, :]
    same = (j // 64) == (t // 64)
    m[:, 0, :] = (j == t)
    m[:, 1, :] = same & (j < t)
    m[:, 2, :] = same & (j <= t)
    m[:, 3, :] = same & (j > t)
    m[:, 4, :] = np.where(j <= t, 0.0, -30000.0)
    m[:, 5, :] = 1.0
    m[:, 6, :] = (j <= t)
    m[:, 7, :] = same
    return m


def prep_shared(inp):
    f = lambda a: np.ascontiguousarray(np.asarray(a, dtype=np.float32))
    sh = {}
    for kk_ in ("meta", "norm1_g", "norm2_g", "w_in", "w_o", "ffn_w_in", "ffn_w_out"):
        sh[kk_] = f(inp[kk_])
    cw = f(inp["ffn_conv_w"])
    sh["conv_wp"] = f(cw.reshape(DEPTH, 3, 44, 128).transpose(0, 3, 2, 1))
    sh["conv_bp"] = f(f(inp["ffn_conv_b"]).reshape(DEPTH, 44, 128).transpose(0, 2, 1))
    mu = np.zeros((DEPTH, 15 * 128), np.float32)
    mu[:, :RWC] = f(inp["rw_mu"])
    sh["mu_p"] = f(mu.reshape(DEPTH, 15, 128).transpose(0, 2, 1))
    vecs = np.stack([f(inp["rw_k_k"]), f(inp["rw_k_a"]), f(inp["rw_a0"]), f(inp["rw_r_k"]).reshape(DEPTH, 512)], -1)
    sh["rwvec_p"] = f(vecs.reshape(DEPTH, 4, 128, 4).transpose(0, 2, 1, 3))
    sh["wup_aug"] = f(np.concatenate([f(inp["rw_w_up"]), f(inp["rw_w0"])[:, None, :]], 1))
    sh["a_up"] = f(inp["rw_a_up"])
    sh["g_up"] = f(inp["rw_g_up"])
    sh["gn_w"] = f(inp["rw_gn_w"])
    sh["gn_b"] = f(inp["rw_gn_b"])
    sh["fx_bf"] = f(inp["fx_b_f"])
    sh["fx_qg"] = f(inp["fx_q_g"])
    sh["fx_kg"] = f(inp["fx_k_g"])
    sh["cmask"] = _masks()
    return sh


def kernel(**inputs):
    nc, P = build_program()
    sh = prep_shared(inputs)
    x = np.asarray(inputs["x"], dtype=np.float32)
    in_maps = []
    for b in range(NB):
        m = dict(sh)
        m["x"] = np.ascontiguousarray(x[b])
        in_maps.append(m)
    res = run_bass_kernel_spmd(nc, in_maps, core_ids=list(range(NB)))
    return np.stack([np.asarray(r["out"], dtype=np.float32) for r in res.results], 0)
```
